# Optimizing a Trainium2 kernel written in Bass

```python
import math
import jax, jax.numpy as jnp
from jax import lax
import numpy as np

D_MODEL = 1024
BATCH = 2
SEQ = 16384
DEPTH = 2
DEC_BATCH = 32
DEC_SEQ = 2048
PAST_LEN = 128

PLE_DIM = 256
GRID_W = 64
HEAD_DIM = 64
QBLK = 128
EPS = 1e-6
ROPE_THETA = 10000.0
A_HEADS = 8
A_KV_HEADS = 2
A_GROUPS = A_HEADS // A_KV_HEADS
A_WIDTH = A_HEADS * HEAD_DIM
A_KV_WIDTH = A_KV_HEADS * HEAD_DIM
B_HEADS = 4
B_QK_WIDTH = B_HEADS * 2 * HEAD_DIM
B_V_DIM = 2 * HEAD_DIM
B_WIDTH = B_HEADS * B_V_DIM
AB_SPLITS = (A_WIDTH, A_KV_WIDTH, A_KV_WIDTH, A_WIDTH, B_QK_WIDTH, B_QK_WIDTH, B_WIDTH, B_WIDTH)
AB_OFFSETS = tuple(int(v) for v in np.cumsum(AB_SPLITS)[:-1])
AB_IN = int(sum(AB_SPLITS))
AB_OUT = A_WIDTH + B_WIDTH
ALIBI_SLOPES = tuple(2.0 ** (-8.0 * (h + 1) / B_HEADS) for h in range(B_HEADS))
C_WIDTH = D_MODEL
POOL_WINDOWS = (2, 4, 8, 16)
C_GROUPS = len(POOL_WINDOWS)
C_GRP = C_WIDTH // C_GROUPS
N_EVEN = (DEPTH + 1) // 2
N_ODD = DEPTH // 2

kernel_name = "hybrid_gqa_diffattn_pool_encoder"

F32 = jnp.float32


def rms_norm(x, g):
    xf = x.astype(F32)
    y = xf * lax.rsqrt(jnp.mean(xf * xf, axis=-1, keepdims=True) + EPS) * g.astype(F32)
    return y.astype(x.dtype)


def axial_rope_tables(S):
    rows = S // GRID_W
    row = jnp.repeat(jnp.arange(rows), GRID_W).astype(F32)
    col = jnp.tile(jnp.arange(GRID_W), rows).astype(F32)
    half = HEAD_DIM // 2
    inv = ROPE_THETA ** (-jnp.arange(0, half, 2, dtype=F32) / half)
    ar = row[:, None] * inv
    ac = col[:, None] * inv
    ang = jnp.concatenate([ar, ar, ac, ac], axis=-1)
    return jnp.cos(ang), jnp.sin(ang)


def apply_rope(x, cos, sin):
    x1, x2, x3, x4 = jnp.split(x, 4, axis=-1)
    rot = jnp.concatenate([-x2, x1, -x4, x3], axis=-1)
    return (x.astype(F32) * cos + rot.astype(F32) * sin).astype(x.dtype)


def mixer_ab(xn, cos, sin, w_in, qn_a, kn_a, qn_b, kn_b, lq1, lk1, lq2, lk2, subln_b, w_out, lam_init):
    B, S, _ = xn.shape
    nb = S // QBLK
    proj = xn @ w_in
    qa, ka, va, ga, qb, kb, vb, gb = jnp.split(proj, AB_OFFSETS, axis=-1)
    qa = apply_rope(rms_norm(qa.reshape(B, S, A_KV_HEADS, A_GROUPS, HEAD_DIM), qn_a),
                    cos[:, None, None, :], sin[:, None, None, :])
    ka = apply_rope(rms_norm(ka.reshape(B, S, A_KV_HEADS, HEAD_DIM), kn_a),
                    cos[:, None, :], sin[:, None, :])
    va = va.reshape(B, S, A_KV_HEADS, HEAD_DIM)
    qb = rms_norm(qb.reshape(B, S, B_HEADS, 2, HEAD_DIM), qn_b)
    kb = rms_norm(kb.reshape(B, S, B_HEADS, 2, HEAD_DIM), kn_b)
    vb = vb.reshape(B, S, B_HEADS, B_V_DIM)
    lam = (jnp.exp(jnp.sum(lq1.astype(F32) * lk1.astype(F32)))
           - jnp.exp(jnp.sum(lq2.astype(F32) * lk2.astype(F32))) + lam_init)
    slopes = jnp.asarray(ALIBI_SLOPES, dtype=F32)
    kpos = jnp.arange(S)
    scale = HEAD_DIM ** -0.5

    def block(args):
        qa_blk, qb_blk, bi = args
        s_a = jnp.einsum('bqkgd,bskd->bkgqs', qa_blk, ka).astype(F32) * scale
        p_a = jax.nn.softmax(s_a, axis=-1).astype(va.dtype)
        o_a = jnp.einsum('bkgqs,bskd->bqkgd', p_a, va)
        qpos = bi * QBLK + jnp.arange(QBLK)
        dist = jnp.abs(qpos[:, None] - kpos[None, :]).astype(F32)
        bias = -slopes[:, None, None] * dist
        s_b = jnp.einsum('bqhmd,bshmd->bmhqs', qb_blk, kb).astype(F32) * scale + bias
        p_b = jax.nn.softmax(s_b, axis=-1)
        a_b = (p_b[:, 0] - lam * p_b[:, 1]).astype(vb.dtype)
        o_b = jnp.einsum('bhqs,bshe->bqhe', a_b, vb)
        return o_a, o_b

    def to_blocks(t):
        return jnp.moveaxis(t.reshape((B, nb, QBLK) + t.shape[2:]), 1, 0)

    o_a, o_b = lax.map(block, (to_blocks(qa), to_blocks(qb), jnp.arange(nb)))
    o_a = jnp.moveaxis(o_a, 0, 1).reshape(B, S, A_WIDTH)
    o_b = jnp.moveaxis(o_b, 0, 1).reshape(B, S, B_HEADS, B_V_DIM)
    o_b = (rms_norm(o_b, subln_b) * (1.0 - lam_init)).reshape(B, S, B_WIDTH)
    y = jnp.concatenate([o_a * jax.nn.silu(ga), o_b * jax.nn.silu(gb)], axis=-1)
    return y @ w_out


def mixer_c(xn, w_in, w_grp, c_scale, w_out):
    B, S, _ = xn.shape
    u, g = jnp.split(xn @ w_in, 2, axis=-1)
    ug = u.reshape(B, S, C_GROUPS, C_GRP).astype(F32)
    cs = jnp.concatenate([jnp.zeros_like(ug[:, :1]), jnp.cumsum(ug, axis=1)], axis=1)
    t = jnp.arange(S)
    pooled = []
    for gi, w in enumerate(POOL_WINDOWS):
        lo = jnp.maximum(t - w // 2, 0)
        hi = jnp.minimum(t + w // 2, S)
        cnt = (hi - lo).astype(F32)[None, :, None]
        pooled.append((cs[:, hi, gi] - cs[:, lo, gi]) / cnt - ug[:, :, gi])
    pooled = jnp.stack(pooled, axis=2)
    mixed = jnp.einsum('bsgc,gcd->bsgd', pooled, w_grp.astype(F32)).reshape(B, S, C_WIDTH)
    y = (mixed * c_scale.astype(F32)).astype(xn.dtype) * jax.nn.silu(g)
    return y @ w_out


def trunk(x, p, norm_mix, w_in_ab, qn_a, kn_a, qn_b, kn_b, lam_q1, lam_k1, lam_q2, lam_k2,
          subln_b, w_out_ab, w_in_c, w_grp_c, scale_c, w_out_c, norm_ple, w_ple_gate, w_ple_proj):
    S = x.shape[1]
    cos, sin = axial_rope_tables(S)
    h = x
    for i in range(DEPTH):
        j = i // 2
        xn = rms_norm(h, norm_mix[i])
        if i % 2 == 0:
            lam_init = 0.8 - 0.6 * math.exp(-0.3 * i)
            y = mixer_ab(xn, cos, sin, w_in_ab[j], qn_a[j], kn_a[j], qn_b[j], kn_b[j],
                         lam_q1[j], lam_k1[j], lam_q2[j], lam_k2[j], subln_b[j], w_out_ab[j], lam_init)
        else:
            y = mixer_c(xn, w_in_c[j], w_grp_c[j], scale_c[j], w_out_c[j])
        h = h + y
        gate = jax.nn.sigmoid(rms_norm(h, norm_ple[i]) @ w_ple_gate[i])
        h = h + gate * (p[i] @ w_ple_proj[i])
    return h


def setup_inputs(seed: int = 0) -> dict:
    key = jax.random.key(seed)
    ks = jax.random.split(key, 24)
    nrm = lambda k, shape, s: jax.random.normal(k, shape, F32) * s
    gain = lambda k, shape: 1.0 + 0.05 * jax.random.normal(k, shape, F32)
    return {
        "x_prompt": nrm(ks[0], (BATCH, SEQ, D_MODEL), 1.0),
        "x_sample": nrm(ks[1], (DEC_BATCH, DEC_SEQ, D_MODEL), 1.0),
        "p_prompt": nrm(ks[2], (DEPTH, BATCH, SEQ, PLE_DIM), 1.0),
        "p_sample": nrm(ks[3], (DEPTH, DEC_BATCH, DEC_SEQ, PLE_DIM), 1.0),
        "norm_mix": gain(ks[4], (DEPTH, D_MODEL)),
        "w_in_ab": nrm(ks[5], (N_EVEN, D_MODEL, AB_IN), D_MODEL ** -0.5),
        "qn_a": gain(ks[6], (N_EVEN, HEAD_DIM)),
        "kn_a": gain(ks[7], (N_EVEN, HEAD_DIM)),
        "qn_b": gain(ks[8], (N_EVEN, HEAD_DIM)),
        "kn_b": gain(ks[9], (N_EVEN, HEAD_DIM)),
        "lam_q1": nrm(ks[10], (N_EVEN, HEAD_DIM), 0.1),
        "lam_k1": nrm(ks[11], (N_EVEN, HEAD_DIM), 0.1),
        "lam_q2": nrm(ks[12], (N_EVEN, HEAD_DIM), 0.1),
        "lam_k2": nrm(ks[13], (N_EVEN, HEAD_DIM), 0.1),
        "subln_b": gain(ks[14], (N_EVEN, B_V_DIM)),
        "w_out_ab": nrm(ks[15], (N_EVEN, AB_OUT, D_MODEL), AB_OUT ** -0.5),
        "w_in_c": nrm(ks[16], (N_ODD, D_MODEL, 2 * C_WIDTH), D_MODEL ** -0.5),
        "w_grp_c": nrm(ks[17], (N_ODD, C_GROUPS, C_GRP, C_GRP), C_GRP ** -0.5),
        "scale_c": gain(ks[18], (N_ODD, C_WIDTH)),
        "w_out_c": nrm(ks[19], (N_ODD, C_WIDTH, D_MODEL), C_WIDTH ** -0.5),
        "norm_ple": gain(ks[20], (DEPTH, D_MODEL)),
        "w_ple_gate": nrm(ks[21], (DEPTH, D_MODEL, D_MODEL), D_MODEL ** -0.5),
        "w_ple_proj": nrm(ks[22], (DEPTH, PLE_DIM, D_MODEL), PLE_DIM ** -0.5),
    }


def reference(x_prompt, x_sample, p_prompt, p_sample, norm_mix, w_in_ab, qn_a, kn_a, qn_b, kn_b,
              lam_q1, lam_k1, lam_q2, lam_k2, subln_b, w_out_ab, w_in_c, w_grp_c, scale_c, w_out_c,
              norm_ple, w_ple_gate, w_ple_proj):
    y_prompt = trunk(x_prompt, p_prompt, norm_mix, w_in_ab, qn_a, kn_a, qn_b, kn_b, lam_q1, lam_k1,
                     lam_q2, lam_k2, subln_b, w_out_ab, w_in_c, w_grp_c, scale_c, w_out_c,
                     norm_ple, w_ple_gate, w_ple_proj)
    y_sample = trunk(x_sample, p_sample, norm_mix, w_in_ab, qn_a, kn_a, qn_b, kn_b, lam_q1, lam_k1,
                     lam_q2, lam_k2, subln_b, w_out_ab, w_in_c, w_grp_c, scale_c, w_out_c,
                     norm_ple, w_ple_gate, w_ple_proj)
    return (y_prompt, y_sample)
```

```python
import math
import contextlib
import numpy as np
import ml_dtypes
import concourse.bass as bass
import concourse.mybir as mybir
from concourse.bass_utils import run_bass_kernel_spmd

F32 = mybir.dt.float32
BF16 = mybir.dt.bfloat16
AF = mybir.ActivationFunctionType
ALU = mybir.AluOpType
AX = mybir.AxisListType
bf16 = ml_dtypes.bfloat16

D = 1024
HD = 64
PLE = 256
EPS = 1e-6
GRID_W = 64
THETA = 10000.0
SLOPES = [2.0 ** (-8.0 * (h + 1) / 4) for h in range(4)]
LAM_INIT = 0.8 - 0.6 * math.exp(-0.3 * 0)
POOL_W = (2, 4, 8, 16)
VW = 2 * 65 + 4 * 129
NCORES = 8
SAME_ENGINE_SYNC = True
PIPE = True


class Cfg:
    def __init__(self, SP=16384, NCH=4, SS=2048, NSEQ=4):
        self.SP, self.NCH, self.SS, self.NSEQ = SP, NCH, SS, NSEQ
        self.CH = SP // NCH
        self.NOWN = self.CH // 128
        self.NFAR = -(-(SP - self.CH - 8) // 128)
        self.NKBP = self.NOWN + 1 + self.NFAR
        self.NSLOT = self.NKBP * 128
        self.SQP = self.CH + 16
        self.NKBS = SS // 128
        self.NKBMAX = max(self.NKBP, self.NKBS)
        self.SQMAX = max(self.SQP, SS)


class Op:
    __slots__ = ("eng", "fn", "deps", "ddeps", "is_dma", "dsem", "dval", "inc", "cnt", "idx")


class _Rec:
    def __init__(self):
        self.calls = []

    def __getattr__(self, name):
        def f(*a, **k):
            self.calls.append((name, a, k))
        return f


class Prog:
    def __init__(self, nc):
        self.nc = nc
        self.ops = []
        self.lastw = {}
        self.readers = {}
        self.dma_keys = {}

    def _add(self, eng, fn, reads, writes, is_dma=False, dkey=None, phase=True):
        o = Op()
        rec = _Rec()
        fn(rec)
        assert len(rec.calls) == 1
        name_, a_, k_ = rec.calls[0]
        fn = lambda e, name_=name_, a_=a_, k_=k_: getattr(e, name_)(*a_, **k_)
        o.eng = eng; o.fn = fn; o.is_dma = is_dma; o.inc = False; o.cnt = 0
        o.idx = len(self.ops)
        reads = list(reads)
        if phase:
            reads.append("phase")
        deps = set()
        for k in reads:
            w = self.lastw.get(k)
            if w is not None:
                deps.add(w)
        for k in writes:
            w = self.lastw.get(k)
            if w is not None:
                deps.add(w)
            for r in self.readers.get(k, ()):
                deps.add(r)
        deps.discard(o.idx)
        o.deps = set(d for d in deps if not self.ops[d].is_dma)
        o.ddeps = {}
        for d in deps:
            p = self.ops[d]
            if p.is_dma:
                o.ddeps[p.dsem] = self.dma_keys[p.dsem]
        for k in reads:
            self.readers.setdefault(k, []).append(o.idx)
        for k in writes:
            self.lastw[k] = o.idx
            self.readers[k] = []
        if is_dma:
            o.dsem = dkey
            self.dma_keys[dkey] = self.dma_keys.get(dkey, 0) + 16
            o.dval = self.dma_keys[dkey]
        self.ops.append(o)
        return o

    def op(self, eng, fn, reads=(), writes=()):
        return self._add(eng, fn, reads, writes)

    def dma(self, out, in_, reads=(), writes=(), dkey=None, queue="sp"):
        assert dkey is not None
        fn = lambda e: e.dma_start(out=out, in_=in_)
        return self._add(queue, fn, reads, writes, is_dma=True, dkey=dkey)

    def barrier(self, tile):
        self._add("dve", lambda e: e.memset(tile, 0.0), [], ["phase"], phase=False)

    def emit(self):
        nc = self.nc
        ops = self.ops
        engs = ["pe", "act", "dve", "pool", "sp"]

        def same_skip(p, o):
            return p.eng == o.eng and (not o.is_dma) and (p.eng == "pe" or not SAME_ENGINE_SYNC)

        for o in ops:
            for d in o.deps:
                p = ops[d]
                if same_skip(p, o):
                    continue
                p.inc = True
        cnt = {e: 0 for e in engs}
        for o in ops:
            if not o.is_dma and o.inc:
                cnt[o.eng] += 1
                o.cnt = cnt[o.eng]
        es = contextlib.ExitStack()
        sem = {e: es.enter_context(nc.semaphore("s_" + e)) for e in engs}
        dsem = {}
        for i, k in enumerate(self.dma_keys):
            dsem[k] = es.enter_context(nc.semaphore("d%d" % i))
        block = es.enter_context(nc.Block())
        per = {e: [o for o in ops if o.eng == e] for e in engs}
        final = [(dsem[k], v) for k, v in self.dma_keys.items()]

        def run(e, engine):
            seen = {}
            for o in per[e]:
                waits = {}
                cands = [(dsem[k], v) for k, v in o.ddeps.items()]
                for d in o.deps:
                    p = ops[d]
                    if same_skip(p, o):
                        continue
                    cands.append((sem[p.eng], p.cnt))
                for s, v in cands:
                    key = id(s)
                    if seen.get(key, 0) >= v:
                        continue
                    if key not in waits or waits[key][1] < v:
                        waits[key] = (s, v)
                for key, (s, v) in waits.items():
                    engine.wait_ge(s, v)
                    seen[key] = v
                ins = o.fn(engine)
                if o.is_dma:
                    ins.then_inc(dsem[o.dsem], 16)
                elif o.inc:
                    ins.then_inc(sem[e], 1)
            if e == "sp":
                for s, v in final:
                    engine.wait_ge(s, v)

        @block.tensor
        def _(eng):
            run("pe", eng)

        @block.scalar
        def _(eng):
            run("act", eng)

        @block.vector
        def _(eng):
            run("dve", eng)

        @block.gpsimd
        def _(eng):
            run("pool", eng)

        @block.sync
        def _(eng):
            run("sp", eng)

        es.close()


def build_program(cfg):
    nc = bass.Bass("TRN2", target_bir_lowering=False)
    es = contextlib.ExitStack()
    CH, SS, NSEQ, NOWN = cfg.CH, cfg.SS, cfg.NSEQ, cfg.NOWN
    NSLOT, SQP, NKBP, NKBS = cfg.NSLOT, cfg.SQP, cfg.NKBP, cfg.NKBS

    def din(name, shape, dtype=F32):
        return nc.dram_tensor(name, list(shape), dtype, kind="ExternalInput").ap()

    def dscr(name, shape, dtype):
        return nc.dram_tensor(name, list(shape), dtype, kind="Internal").ap()

    xp = din("xp", [NSLOT, D]); pp0 = din("pp0", [SQP, PLE]); pp1 = din("pp1", [CH, PLE])
    xs = din("xs", [NSEQ, SS, D]); psm = din("psm", [2, NSEQ, SS, PLE])
    w_in_ab = din("w_in_ab", [D, 3328]); w_out_ab = din("w_out_ab", [D, D])
    w_gate = din("w_gate", [2 * D, D]); w_proj = din("w_proj", [2 * PLE, D])
    w_in_c = din("w_in_c", [D, 2 * D]); w_grp = din("w_grp", [D, 256]); w_out_c = din("w_out_c", [D, D])
    gmix_d = din("gmix", [128, 16]); gple_d = din("gple", [128, 16])
    hgain_d = din("hgain", [1, 4 * 64])
    lamv_d = din("lamv", [1, 4 * 64])
    subln_d = din("subln", [1, 128]); cscale_d = din("cscale", [1, D])
    cs_p = din("cs_p", [NSLOT, 128]); cs_s = din("cs_s", [SS, 128])
    kaug_p = din("kaug_p", [4, NSLOT], BF16); kaug_s = din("kaug_s", [4, SS], BF16)
    qaug_p = din("qaug_p", [4, 2, 4, SQP], BF16); qaug_s = din("qaug_s", [4, 2, 4, SS], BF16)
    dbias_d = din("dbias", [128, 4 * 128], BF16); bhalo_d = din("bhalo", [128, 2 * 4 * 16], BF16)
    vone_p_d = din("vone_p", [128, NKBP]); vone_s_d = din("vone_s", [128, NKBS])
    ident_d = din("ident", [128, 128], BF16)
    bandM_p_d = din("bandM_p", [128, 3 * 4 * 128], BF16); bandM_s_d = din("bandM_s", [128, 3 * 4 * 128], BF16)
    bandPN_d = din("bandPN", [128, 2 * 4 * 128], BF16); bandH_d = din("bandH", [16, 2 * 4 * 128], BF16)
    icnt_p_d = din("icnt_p", [128, NOWN * 4]); icnt_s_d = din("icnt_s", [128, NKBS * 4])
    yp = nc.dram_tensor("yp", [CH, D], F32, kind="ExternalOutput").ap()
    ys = nc.dram_tensor("ys", [NSEQ, SS, D], F32, kind="ExternalOutput").ap()

    wab_s = dscr("wab_s", [D, 3328], BF16); woab_s = dscr("woab_s", [D, D], BF16)
    wg_s = dscr("wg_s", [2 * D, D], BF16); wpp_s = dscr("wpp_s", [2 * PLE, D], BF16)
    wic_s = dscr("wic_s", [D, 2 * D], BF16); wgrp_s = dscr("wgrp_s", [D, 256], BF16); woc_s = dscr("woc_s", [D, D], BF16)
    SKMAX = cfg.NKBMAX * 128
    kT_s = dscr("kT_s", [10, 64, SKMAX], BF16)
    v_s = dscr("v_s", [SKMAX, VW], BF16)
    qT_s = dscr("qT_s", [16, 64, cfg.SQMAX], BF16)
    gate_s = dscr("gate_s", [cfg.SQMAX, D], F32)
    o_s = dscr("o_s", [cfg.SQMAX, D], F32)
    h1_s = dscr("h1_s", [cfg.SQMAX, D], F32)

    NB_G = 67 * 1024; NB_B = 36 * 1024; NB_W = 62 * 1024
    arenaG = es.enter_context(nc.sbuf_tensor("arenaG", [128, NB_G // 2], BF16))
    arenaB = es.enter_context(nc.sbuf_tensor("arenaB", [128, NB_B // 2], BF16))
    arenaW = es.enter_context(nc.sbuf_tensor("arenaW", [128, NB_W // 2], BF16))
    consts = es.enter_context(nc.sbuf_tensor("consts", [128, 3072], F32))
    cbf = es.enter_context(nc.sbuf_tensor("cbf", [128, 6144], BF16))
    ps = es.enter_context(nc.psum_tensor("ps", [128, 8, 512], F32))

    class Carver:
        def __init__(self, arena, nbytes):
            self.a, self.n, self.off = arena, nbytes, 0

        def reset(self):
            self.off = 0

        def get(self, free_shape, dtype):
            esz = 4 if dtype == F32 else 2
            n = int(np.prod(free_shape)) * esz
            n_al = (n + 31) // 32 * 32
            assert self.off + n_al <= self.n, ("arena overflow", self.off, n_al, self.n)
            v = self.a[:, self.off // 2:(self.off + n) // 2]
            self.off += n_al
            if dtype == F32:
                v = v.bitcast(F32)
            if len(free_shape) == 2:
                v = v.rearrange("p (a b) -> p a b", a=free_shape[0])
            elif len(free_shape) == 3:
                v = v.rearrange("p (a b c) -> p a b c", a=free_shape[0], b=free_shape[1])
            return v

    CG = Carver(arenaG, NB_G); CB = Carver(arenaB, NB_B); CW = Carver(arenaW, NB_W)

    coff = [0]

    def cget(n):
        v = consts[:, coff[0]:coff[0] + n]; coff[0] += n
        assert coff[0] <= 3072
        return v
    gmix = cget(16); gple = cget(16); hgain = cget(256); lamv = cget(256); subln = cget(128)
    cscale = cget(D); vone_p = cget(NKBP); vone_s = cget(NKBS)
    icnt_p = cget(NOWN * 4); icnt_s = cget(NKBS * 4)
    lam_t = cget(4); neglam = cget(1); sg8 = cget(128); junk1 = cget(8)
    statA = cget(64); statB = cget(64); statC = cget(64)
    identf = cget(128)
    boff = [0]

    def bget(n):
        v = cbf[:, boff[0]:boff[0] + n]; boff[0] += n
        assert boff[0] <= 6144
        return v
    ident = bget(128); dbias = bget(512); bhalo = bget(128)
    bandM_p = bget(1536); bandM_s = bget(1536); bandPN = bget(1024); bandH = bget(1024)

    P = Prog(nc)
    dk = [0]

    def DK(name):
        return name

    def cload(dst, src, key, bc=False):
        P.dma(dst, src.partition_broadcast(128) if bc else src, writes=[key], dkey="c_" + key)
    cload(gmix, gmix_d[:, :], "gmix"); cload(gple, gple_d[:, :], "gple")
    cload(hgain, hgain_d[0:1, :], "hgain", True); cload(lamv, lamv_d[0:1, :], "lamv", True)
    cload(subln, subln_d[0:1, :], "subln", True); cload(cscale, cscale_d[0:1, :], "cscale", True)
    cload(vone_p, vone_p_d[:, :], "vone_p"); cload(vone_s, vone_s_d[:, :], "vone_s")
    cload(icnt_p, icnt_p_d[:, :], "icnt_p"); cload(icnt_s, icnt_s_d[:, :], "icnt_s")
    cload(ident, ident_d[:, :], "ident"); cload(dbias, dbias_d[:, :], "dbias"); cload(bhalo, bhalo_d[:, :], "bhalo")
    cload(bandM_p, bandM_p_d[:, :], "bandM_p"); cload(bandM_s, bandM_s_d[:, :], "bandM_s")
    cload(bandPN, bandPN_d[:, :], "bandPN")
    P.dma(bandH[0:16, :], bandH_d[:, :], writes=["bandH"], dkey="c_bandH")
    P.op("dve", lambda e: e.tensor_copy(out=identf[:, :], in_=ident[:, :]), reads=["ident"], writes=["identf"])
    lv = lamv.rearrange("p (a d) -> p a d", a=4)
    P.op("dve", lambda e: e.tensor_tensor(out=statA[:, 0:64], in0=lv[:, 0, :], in1=lv[:, 1, :], op=ALU.mult), reads=["lamv"], writes=["statA"])
    P.op("dve", lambda e: e.tensor_reduce(out=lam_t[:, 0:1], in_=statA[:, 0:64], axis=AX.X, op=ALU.add), reads=["statA"], writes=["lam0"])
    P.op("dve", lambda e: e.tensor_tensor(out=statB[:, 0:64], in0=lv[:, 2, :], in1=lv[:, 3, :], op=ALU.mult), reads=["lamv"], writes=["statB"])
    P.op("dve", lambda e: e.tensor_reduce(out=lam_t[:, 1:2], in_=statB[:, 0:64], axis=AX.X, op=ALU.add), reads=["statB"], writes=["lam1"])
    P.op("act", lambda e: e.activation(out=lam_t[:, 2:4], in_=lam_t[:, 0:2], func=AF.Exp), reads=["lam0", "lam1"], writes=["lam2"])
    P.op("dve", lambda e: e.tensor_tensor(out=neglam[:, 0:1], in0=lam_t[:, 3:4], in1=lam_t[:, 2:3], op=ALU.subtract), reads=["lam2"], writes=["neglam"])
    P.op("dve", lambda e: e.tensor_scalar_add(out=neglam[:, 0:1], in0=neglam[:, 0:1], scalar1=-LAM_INIT), reads=["neglam"], writes=["neglam"])
    P.op("dve", lambda e: e.tensor_scalar_mul(out=sg8[:, :], in0=subln[:, :], scalar1=1.0 - LAM_INIT), reads=["subln"], writes=["sg8"])

    CW.reset()
    wtf = [CW.get([2048], F32) for _ in range(2)]
    wtb = [CW.get([2048], BF16) for _ in range(2)]
    wi = [0]

    def wconv(src, dst, K, N, name):
        for kc in range(K // 128):
            for n0 in range(0, N, 2048):
                nn = min(2048, N - n0)
                i = wi[0] % 2; wi[0] += 1
                P.dma(wtf[i][:, 0:nn], src[kc * 128:(kc + 1) * 128, n0:n0 + nn], writes=[("wtf", i)], dkey="wld%d" % i)
                eng = ("dve", "pool", "act")[wi[0] % 3]
                if eng == "act":
                    P.op("act", lambda e, i=i, nn=nn: e.copy(out=wtb[i][:, 0:nn], in_=wtf[i][:, 0:nn]), reads=[("wtf", i)], writes=[("wtb", i)])
                else:
                    P.op(eng, lambda e, i=i, nn=nn: e.tensor_copy(out=wtb[i][:, 0:nn], in_=wtf[i][:, 0:nn]), reads=[("wtf", i)], writes=[("wtb", i)])
                P.dma(dst[kc * 128:(kc + 1) * 128, n0:n0 + nn], wtb[i][:, 0:nn], reads=[("wtb", i)], writes=[("w", name)], dkey="wst%d" % i)
    wconv(w_in_ab, wab_s, D, 3328, "wab"); wconv(w_out_ab, woab_s, D, D, "woab")
    wconv(w_gate, wg_s, 2 * D, D, "wg"); wconv(w_proj, wpp_s, 2 * PLE, D, "wpp")
    wconv(w_in_c, wic_s, D, 2 * D, "wic"); wconv(w_grp, wgrp_s, D, 256, "wgrp"); wconv(w_out_c, woc_s, D, D, "woc")
    P.barrier(junk1[:, 0:1])

    jobs = []
    jobs.append(dict(name="p", x=xp, nkb=NKBP, nown=NOWN, halo=True, cs=cs_p, kaug=kaug_p, qaug=qaug_p, vone=vone_p,
                     p0=pp0, p1=pp1, icnt=icnt_p, bandM=bandM_p, out=yp, vkey="vone_p", ikey="icnt_p", bkey="bandM_p"))
    for s in range(NSEQ):
        jobs.append(dict(name="s%d" % s, x=xs[s], nkb=NKBS, nown=NKBS, halo=False, cs=cs_s, kaug=kaug_s, qaug=qaug_s, vone=vone_s,
                         p0=psm[0, s], p1=psm[1, s], icnt=icnt_s, bandM=bandM_s, out=ys[s], vkey="vone_s", ikey="icnt_s", bkey="bandM_s"))

    psb = ps.bitcast(BF16) if False else None

    def ps_bf(bank):
        return ps[:, bank, :].bitcast(BF16)

    evac_i = [0]

    def evac_copy(out, in_, reads, writes, scale=None):
        evac_i[0] += 1
        if evac_i[0] % 2 == 0:
            if scale is None:
                P.op("dve", lambda e: e.tensor_copy(out=out, in_=in_), reads=reads, writes=writes)
            else:
                P.op("dve", lambda e: e.tensor_scalar_mul(out=out, in0=in_, scalar1=scale), reads=reads, writes=writes)
        else:
            if scale is None:
                P.op("act", lambda e: e.copy(out=out, in_=in_), reads=reads, writes=writes)
            else:
                P.op("act", lambda e: e.activation(out=out, in_=in_, func=AF.Copy, scale=scale), reads=reads, writes=writes)

    def rms_rstd(ss_ap, n, r, reads, key):
        P.op("act", lambda e: e.activation(out=ss_ap, in_=ss_ap, func=AF.Ln, scale=1.0 / n, bias=EPS), reads=reads, writes=[key])
        P.op("act", lambda e: e.activation(out=ss_ap, in_=ss_ap, func=AF.Exp, scale=-0.5), reads=[key], writes=[key])

    def sigmoid_act(dst, src, reads, key):
        P.op("act", lambda e: e.activation(out=dst, in_=src, func=AF.Exp, scale=-1.0), reads=reads, writes=[key])
        P.op("act", lambda e: e.activation(out=dst, in_=dst, func=AF.Ln, scale=1.0, bias=1.0), reads=[key], writes=[key])
        P.op("act", lambda e: e.activation(out=dst, in_=dst, func=AF.Exp, scale=-1.0), reads=[key], writes=[key])

    def transpose_rows(src_bf, r, n_kc, si, bank0, reads, tagw):
        for kc in range(n_kc):
            pb = ps_bf(bank0 + kc // 2)
            c0 = (kc % 2) * 512 + si * 128
            P.op("pe", lambda e, pb=pb, c0=c0, kc=kc: e.transpose(out=pb[:, c0:c0 + r], in_=src_bf[0:r, kc * 128:(kc + 1) * 128], identity=ident[0:r, 0:r]),
                 reads=reads + ["ident"], writes=[("ps", bank0 + kc // 2)])

    def evac_T(dstT, n_kc, W, bank0, key, gain=None, gl=0):
        for kc in range(n_kc):
            pb = ps_bf(bank0 + kc // 2)
            c0 = (kc % 2) * 512
            sc = None if gain is None else gain[:, gl * 8 + kc:gl * 8 + kc + 1]
            evac_copy(dstT[:, kc, 0:W], pb[:, c0:c0 + W], reads=[("ps", bank0 + kc // 2)] + ([] if gain is None else ["gmix", "gple"]), writes=[key], scale=sc)

    def linear(actT, r, si, wt, n_kc, col0, ncols, bank, reads):
        for kc in range(n_kc):
            P.op("pe", lambda e, kc=kc: e.matmul(ps[0:r, bank, 0:ncols], lhsT=actT[:, kc, si * 128:si * 128 + r], rhs=wt[:, kc, col0:col0 + ncols],
                                                 start=(kc == 0), stop=(kc == n_kc - 1)),
                 reads=reads, writes=[("ps", bank)])

    def load_w(carver, src, K, N, key, wkey):
        t = carver.get([K // 128, N], BF16)
        P.dma(t, src.rearrange("(kc p) n -> p kc n", p=128), reads=[("w", wkey)], writes=[key], dkey="wl_" + key)
        return t

    def macros_of(job):
        out = []
        n = job["nown"]
        for m in range(0, n, 4):
            out.append((list(range(m, min(m + 4, n))), True, 128))
        k = n
        if job["halo"]:
            out.append(([n], True, 16))
            k = n + 1
        while k < job["nkb"]:
            out.append((list(range(k, min(k + 4, job["nkb"]))), False, 128))
            k += 4
        return out

    def qrow0(job, sub):
        return sub * 128 if sub < job["nown"] else job["nown"] * 128

    for job in jobs:
        nkb, nown, halo = job["nkb"], job["nown"], job["halo"]
        x = job["x"]
        SK = nkb * 128
        jn = job["name"]
        CG.reset(); CB.reset(); CW.reset()
        wab = load_w(CG, wab_s, D, 3328, "wab_t", "wab")
        xt = [CW.get([D], F32) for _ in range(2)]
        xnb = [CW.get([D], BF16) for _ in range(2)]
        xnT = CW.get([8, 512], BF16)
        sqt = CW.get([D], F32)
        t1 = CW.get([512], F32)
        t2 = CW.get([512], F32)
        t3 = CW.get([512], F32)
        knb = CW.get([640], BF16)
        qnb = CW.get([1024], BF16)
        kst = [CW.get([10, 128], BF16) for _ in range(2)]
        qst = [CW.get([16, 128], BF16) for _ in range(2)]
        vst = [CW.get([VW], BF16) for _ in range(2)]
        gst = [CW.get([D], F32) for _ in range(2)]
        cst = [CW.get([128], F32) for _ in range(2)]
        hg = hgain.rearrange("p (a d) -> p a d", a=4)

        def norm_heads(pv, r, nh, gi, rope, cs_t, out_bf, pkeys, okey):
            n = nh * 64
            P.op("act", lambda e: e.activation(out=sqt[0:r, 0:n], in_=pv, func=AF.Square), reads=pkeys, writes=["sqt"])
            P.op("dve", lambda e: e.tensor_reduce(out=statA[0:r, 0:nh], in_=sqt[0:r, 0:n].rearrange("p (h d) -> p h d", d=64), axis=AX.X, op=ALU.add),
                 reads=["sqt"], writes=["statA"])
            rms_rstd(statA[0:r, 0:nh], 64.0, r, ["statA"], "statA")
            P.op("dve", lambda e: e.tensor_tensor(out=t1[0:r, 0:n].rearrange("p (h d) -> p h d", d=64), in0=pv.rearrange("p (h d) -> p h d", d=64),
                                                  in1=statA[0:r, 0:nh].unsqueeze(2).to_broadcast([r, nh, 64]), op=ALU.mult),
                 reads=pkeys + ["statA"], writes=["t1"])
            gb = hg[0:r, gi, :].unsqueeze(1).to_broadcast([r, nh, 64])
            if not rope:
                P.op("pool", lambda e: e.tensor_tensor(out=out_bf.rearrange("p (h d) -> p h d", d=64), in0=t1[0:r, 0:n].rearrange("p (h d) -> p h d", d=64), in1=gb, op=ALU.mult),
                     reads=["t1", "hgain"], writes=[okey])
                return
            P.op("pool", lambda e: e.tensor_tensor(out=t2[0:r, 0:n].rearrange("p (h d) -> p h d", d=64), in0=t1[0:r, 0:n].rearrange("p (h d) -> p h d", d=64), in1=gb, op=ALU.mult),
                 reads=["t1", "hgain"], writes=["t2"])
            v2 = t2[0:r, 0:n].rearrange("p (h a b c) -> p h a b c", a=2, b=2, c=16)
            v3 = t3[0:r, 0:n].rearrange("p (h a b c) -> p h a b c", a=2, b=2, c=16)
            cosv = cs_t[0:r, 0:64].unsqueeze(1).to_broadcast([r, nh, 64])
            sinv = cs_t[0:r, 64:128].rearrange("p (a b c) -> p a b c", a=2, b=2)
            for half in range(2):
                P.op("pool", lambda e, half=half: e.tensor_tensor(out=v3[:, :, :, half, :], in0=v2[:, :, :, 1 - half, :],
                                                                  in1=sinv[:, :, half, :].unsqueeze(1).to_broadcast([r, nh, 2, 16]), op=ALU.mult),
                     reads=["t2", "cst"], writes=["t3"])
            P.op("dve", lambda e: e.tensor_tensor(out=t1[0:r, 0:n].rearrange("p (h d) -> p h d", d=64), in0=t2[0:r, 0:n].rearrange("p (h d) -> p h d", d=64), in1=cosv, op=ALU.mult),
                 reads=["t2", "cst"], writes=["t1"])
            P.op("dve", lambda e: e.tensor_tensor(out=out_bf, in0=t1[0:r, 0:n], in1=t3[0:r, 0:n], op=ALU.add), reads=["t1", "t3"], writes=[okey])

        gsub = 0
        r = 128
        for (subs, own, rq) in macros_of(job):
            ns = len(subs)
            W = ns * 128
            for si, sb in enumerate(subs):
                b = gsub % 2; gsub += 1
                P.dma(xt[b][:, :], x[sb * 128:(sb + 1) * 128, :], writes=[("xt", b)], dkey="xt%d" % b)
                P.op("act", lambda e, b=b, si=si: e.activation(out=sqt[:, :], in_=xt[b][:, :], func=AF.Square, accum_out=statB[:, si:si + 1]),
                     reads=[("xt", b)], writes=["sqt", ("statB", si)])
                rms_rstd(statB[:, si:si + 1], float(D), r, [("statB", si)], ("statB", si))
                P.op("dve", lambda e, b=b, si=si: e.tensor_scalar_mul(out=xnb[b][:, :], in0=xt[b][:, :], scalar1=statB[:, si:si + 1]),
                     reads=[("xt", b), ("statB", si)], writes=[("xnb", b)])
                transpose_rows(xnb[b], r, 8, si, 0, [("xnb", b)], None)
            evac_T(xnT, 8, W, 0, "xnT", gain=gmix, gl=0)
            for si, sb in enumerate(subs):
                b = sb % 2
                P.dma(cst[b][:, :], job["cs"][sb * 128:(sb + 1) * 128, :], writes=[("cst", b), "cst"], dkey="cst%d" % b)
                linear(xnT, r, si, wab, 8, 512, 256, 4, ["xnT", "wab_t"])
                norm_heads(ps[0:r, 4, 0:128], r, 2, 1, True, cst[b], knb[0:r, 0:128], [("ps", 4)], "knb")
                P.op("dve", lambda e, b=b: e.tensor_copy(out=vst[b][0:r, 0:130].rearrange("p (h c) -> p h c", c=65)[:, :, 0:64],
                                                         in_=ps[0:r, 4, 128:256].rearrange("p (h c) -> p h c", c=64)),
                     reads=[("ps", 4)], writes=[("vst", b)])
                linear(xnT, r, si, wab, 8, 1792, 512, 5, ["xnT", "wab_t"])
                norm_heads(ps[0:r, 5, 0:512], r, 8, 3, False, None, knb[0:r, 128:640], [("ps", 5)], "knb")
                linear(xnT, r, si, wab, 8, 2304, 512, 4, ["xnT", "wab_t"])
                P.op("act", lambda e, b=b: e.copy(out=vst[b][0:r, 130:VW].rearrange("p (h c) -> p h c", c=129)[:, :, 0:128],
                                                  in_=ps[0:r, 4, 0:512].rearrange("p (h c) -> p h c", c=128)),
                     reads=[("ps", 4)], writes=[("vst", b)])
                vo = job["vone"][0:r, sb:sb + 1]
                P.op("pool", lambda e, b=b, vo=vo: e.tensor_copy(out=vst[b][0:r, 0:130].rearrange("p (h c) -> p h c", c=65)[:, :, 64:65],
                                                                 in_=vo.unsqueeze(1).to_broadcast([r, 2, 1])),
                     reads=[job["vkey"]], writes=[("vst", b)])
                P.op("pool", lambda e, b=b, vo=vo: e.tensor_copy(out=vst[b][0:r, 130:VW].rearrange("p (h c) -> p h c", c=129)[:, :, 128:129],
                                                                 in_=vo.unsqueeze(1).to_broadcast([r, 4, 1])),
                     reads=[job["vkey"]], writes=[("vst", b)])
                P.dma(v_s[sb * 128:(sb + 1) * 128, :], vst[b][:, :], reads=[("vst", b)], writes=[("v", sb)], dkey="vst%d" % b)
                for hh in range(10):
                    bank = 6 if hh < 2 else 7
                    c0 = (hh if hh < 2 else hh - 2) * 128
                    P.op("pe", lambda e, hh=hh, bank=bank, c0=c0: e.transpose(out=ps_bf(bank)[0:64, c0:c0 + r], in_=knb[0:r, hh * 64:(hh + 1) * 64], identity=ident[0:r, 0:r]),
                         reads=["knb", "ident"], writes=[("ps", bank)])
                evac_copy(kst[b][0:64, 0:2, :], ps_bf(6)[0:64, 0:256].rearrange("p (h c) -> p h c", c=128), [("ps", 6)], [("kst", b)])
                evac_copy(kst[b][0:64, 2:10, :], ps_bf(7)[0:64, 0:1024].rearrange("p (h c) -> p h c", c=128), [("ps", 7)], [("kst", b)])
                P.dma(kT_s[:, :, sb * 128:(sb + 1) * 128].rearrange("u d t -> d u t"), kst[b][0:64, :, :], reads=[("kst", b)], writes=[("kT", sb)], dkey="kst%d" % b)
                if own:
                    linear(xnT, rq, si, wab, 8, 0, 512, 5, ["xnT", "wab_t"])
                    norm_heads(ps[0:rq, 5, 0:512], rq, 8, 0, True, cst[b], qnb[0:rq, 0:512], [("ps", 5)], "qnb")
                    linear(xnT, rq, si, wab, 8, 1280, 512, 4, ["xnT", "wab_t"])
                    norm_heads(ps[0:rq, 4, 0:512], rq, 8, 2, False, None, qnb[0:rq, 512:1024], [("ps", 4)], "qnb")
                    linear(xnT, rq, si, wab, 8, 768, 512, 5, ["xnT", "wab_t"])
                    sigmoid_act(gst[b][0:rq, 0:512], ps[0:rq, 5, 0:512], [("ps", 5)], ("gst", b))
                    P.op("dve", lambda e, b=b: e.tensor_tensor(out=gst[b][0:rq, 0:512], in0=gst[b][0:rq, 0:512], in1=ps[0:rq, 5, 0:512], op=ALU.mult), reads=[("ps", 5), ("gst", b)], writes=[("gst", b)])
                    linear(xnT, rq, si, wab, 8, 2816, 512, 4, ["xnT", "wab_t"])
                    sigmoid_act(gst[b][0:rq, 512:1024], ps[0:rq, 4, 0:512], [("ps", 4)], ("gst", b))
                    P.op("dve", lambda e, b=b: e.tensor_tensor(out=gst[b][0:rq, 512:1024], in0=gst[b][0:rq, 512:1024], in1=ps[0:rq, 4, 0:512], op=ALU.mult), reads=[("ps", 4), ("gst", b)], writes=[("gst", b)])
                    q0 = qrow0(job, sb)
                    P.dma(gate_s[q0:q0 + rq, :], gst[b][0:rq, :], reads=[("gst", b)], writes=[("gate", sb)], dkey="gst%d" % b)
                    for hh in range(16):
                        bank = 6 if hh < 8 else 7
                        c0 = (hh % 8) * 128
                        P.op("pe", lambda e, hh=hh, bank=bank, c0=c0: e.transpose(out=ps_bf(bank)[0:64, c0:c0 + rq], in_=qnb[0:rq, hh * 64:(hh + 1) * 64], identity=ident[0:rq, 0:rq]),
                             reads=["qnb", "ident"], writes=[("ps", bank)])
                    for half in range(2):
                        evac_copy(qst[b][0:64, half * 8:half * 8 + 8, 0:rq], ps_bf(6 + half)[0:64, 0:1024].rearrange("p (h c) -> p h c", c=128)[:, :, 0:rq],
                                  [("ps", 6 + half)], [("qst", b)])
                    P.dma(qT_s[:, :, q0:q0 + rq].rearrange("u d t -> d u t"), qst[b][0:64, :, 0:rq], reads=[("qst", b)], writes=[("qT", sb)], dkey="qst%d" % b)
        P.barrier(junk1[:, 0:1])

        CG.reset(); CB.reset(); CW.reset()
        KT = CG.get([2, SK], BF16)
        VtA = CB.get([nkb, 65], BF16)
        CB.reset()
        VtB = CB.get([nkb, 129], BF16)
        QT = [CW.get([2, 2, 512], BF16) for _ in range(2)]
        PT = [CW.get([2, 512], BF16) for _ in range(3)]
        ost = [CW.get([4, 128], F32) for _ in range(2)]
        oT = [CW.get([512], F32) for _ in range(2)]
        o1 = CW.get([4, 128], F32); o2 = CW.get([4, 128], F32); ob = CW.get([4, 128], F32); sq2 = CW.get([4, 128], F32)
        kT_keys = [("kT", sb) for sb in range(nkb)]
        v_keys = [("v", sb) for sb in range(nkb)]
        qtiles = []
        for m in range(0, nown, 4):
            qtiles.append(dict(i=m // 4, q0=m * 128, ns=min(4, nown - m), r=128, halo=False, keys=[("qT", m + t_) for t_ in range(min(4, nown - m))], subs0=m))
        if halo:
            qtiles.append(dict(i=-1, q0=nown * 128, ns=1, r=16, halo=True, keys=[("qT", nown)], subs0=nown))
        rnd = [0]; gi_ = [0]; qi_ = [0]

        def key_loop(qt, QTt, mslot, kmap, kind, h, obanks, vt, vw, round_started):
            W = (qt["ns"] - 1) * 128 + qt["r"]
            ns, r = qt["ns"], qt["r"]
            KR = 64 if kind == "a" else 68
            groups = []
            kb = 0
            while kb < nkb:
                grp = [kb] if kb + 1 >= nkb else [kb, kb + 1]
                g = gi_[0]; gi_[0] += 1
                groups.append((grp, g % 2, g % 3))
                kb += len(grp)

            def emit_qk(grp, sslot):
                for j, k in enumerate(grp):
                    bank = sslot * 2 + j
                    kcols = slice(k * 128, (k + 1) * 128)
                    mode = "plain"; var = 0
                    if kind == "b":
                        if qt["halo"]:
                            if k < nown:
                                var = 1
                            elif k == nown:
                                mode = "halo"
                            else:
                                var = 0
                        else:
                            i4 = qt["i"] * 4
                            if k < i4:
                                var = 1
                            elif k < i4 + 4 and k < nown:
                                mode = "diag"
                            else:
                                var = 0
                    rd = ["KT", ("QT", qi_[0] % 2)]
                    if mode == "plain":
                        P.op("pe", lambda e: e.matmul(ps[:, bank, 0:W], lhsT=KT[0:KR, kmap, kcols], rhs=QTt[0:KR, mslot, var, 0:W], start=True, stop=True),
                             reads=rd, writes=[("ps", bank)])
                    elif mode == "diag":
                        dd = k - qt["i"] * 4
                        first = True
                        for s in range(ns):
                            cs_ = slice(s * 128, (s + 1) * 128)
                            if s == dd:
                                P.op("pe", lambda e: e.matmul(ps[:, bank, cs_], lhsT=KT[0:64, kmap, kcols], rhs=QTt[0:64, mslot, 0, cs_],
                                                              start=first, stop=False, skip_group_check=True),
                                     reads=rd, writes=[("ps", bank)])
                                P.op("pe", lambda e: e.matmul(ps[:, bank, cs_], lhsT=ident[:, :], rhs=dbias[:, h * 128:(h + 1) * 128], start=False, stop=True, skip_group_check=True),
                                     reads=["ident", "dbias"], writes=[("ps", bank)])
                            else:
                                v_ = 0 if s < dd else 1
                                P.op("pe", lambda e: e.matmul(ps[:, bank, cs_], lhsT=KT[0:68, kmap, kcols], rhs=QTt[0:68, mslot, v_, cs_],
                                                              start=first, stop=True, skip_group_check=True),
                                     reads=rd, writes=[("ps", bank)])
                            first = False
                    else:
                        P.op("pe", lambda e: e.matmul(ps[:, bank, 0:W], lhsT=KT[0:64, kmap, kcols], rhs=QTt[0:64, mslot, 0, 0:W], start=True, stop=False, skip_group_check=True),
                             reads=rd, writes=[("ps", bank)])
                        for hl in range(2):
                            c0 = (hl * 4 + h) * 16
                            P.op("pe", lambda e: e.matmul(ps[:, bank, 0:W], lhsT=ident[:, :], rhs=bhalo[:, c0:c0 + 16], start=False, stop=(hl == 1), skip_group_check=True),
                                 reads=["ident", "bhalo"], writes=[("ps", bank)])

            if PIPE:
                emit_qk(groups[0][0], groups[0][1])
            for gidx, (grp, sslot, pslot) in enumerate(groups):
                if not PIPE:
                    emit_qk(grp, sslot)
                elif gidx + 1 < len(groups):
                    emit_qk(groups[gidx + 1][0], groups[gidx + 1][1])
                ng = len(grp)
                P.op("act", lambda e: e.activation(out=PT[pslot][:, 0:ng, 0:W], in_=ps[:, sslot * 2:sslot * 2 + ng, 0:W], func=AF.Exp, scale=0.125),
                     reads=[("ps", sslot * 2 + j) for j in range(ng)], writes=[("PT", pslot)])
                for j, k in enumerate(grp):
                    last = (k == nkb - 1)
                    if kind == "a":
                        obank = obanks[0]
                        st = obank not in round_started
                        round_started.add(obank)
                        P.op("pe", lambda e: e.matmul(ps[0:65, obank, 0:W], lhsT=vt[:, k, 0:65], rhs=PT[pslot][:, j, 0:W], start=st, stop=last, skip_group_check=True),
                             reads=[("PT", pslot), "Vt"], writes=[("ps", obank)])
                        continue
                    for s in range(ns):
                        obank = obanks[s // 2]; oc = (s % 2) * 129
                        st = obank not in round_started
                        round_started.add(obank)
                        P.op("pe", lambda e: e.matmul(ps[0:r, obank, oc:oc + vw], lhsT=PT[pslot][:, j, s * 128:s * 128 + r], rhs=vt[:, k, 0:vw], start=st, stop=last, skip_group_check=True),
                             reads=[("PT", pslot), "Vt"], writes=[("ps", obank)])

        for kvh in range(2):
            P.dma(KT[0:64, 0, :], kT_s[kvh, :, 0:SK], reads=kT_keys, writes=["KT"], dkey="KTa")
            for c0 in range(0, nkb, 16):
                c1 = min(nkb, c0 + 16)
                P.dma(VtA[:, c0:c1, :], v_s[c0 * 128:c1 * 128, kvh * 65:(kvh + 1) * 65].rearrange("(kb p) c -> p kb c", p=128), reads=v_keys[c0:c1], writes=["Vt"], dkey="Vta")
            for g4 in range(4):
                hq = kvh * 4 + g4
                for qt in qtiles:
                    qi_[0] += 1
                    qs = qi_[0] % 2
                    W = (qt["ns"] - 1) * 128 + qt["r"]
                    P.dma(QT[qs][0:64, 0, 0, 0:W], qT_s[hq, :, qt["q0"]:qt["q0"] + W], reads=qt["keys"], writes=[("QT", qs)], dkey="QT%d" % qs)
                    rnd[0] += 1
                    obank = 4 + rnd[0] % 2
                    key_loop(qt, QT[qs], 0, 0, "a", 0, [obank], VtA, 65, set())
                    ns, r = qt["ns"], qt["r"]
                    os_ = ost[rnd[0] % 2]
                    oTt = oT[rnd[0] % 2]
                    evac_copy(oTt[0:65, 0:W], ps[0:65, obank, 0:W], [("ps", obank)], [("oT", rnd[0] % 2)])
                    obank = 6 + rnd[0] % 2
                    for s_ in range(ns):
                        P.op("pe", lambda e: e.transpose(out=ps[0:r, obank, s_ * 65:(s_ + 1) * 65], in_=oTt[0:65, s_ * 128:s_ * 128 + r], identity=identf[0:65, 0:65]),
                             reads=[("oT", rnd[0] % 2), "identf"], writes=[("ps", obank)])
                    pso = ps[0:r, obank, 0:ns * 65].rearrange("p (s c) -> p s c", c=65)
                    P.op("dve", lambda e, pso=pso, r=r, ns=ns: e.reciprocal(out=statC[0:r, 0:ns].unsqueeze(2), in_=pso[:, :, 64:65]), reads=[("ps", obank)], writes=["statC"])
                    P.op("dve", lambda e, pso=pso, r=r, ns=ns, os_=os_: e.tensor_tensor(out=os_[0:r, 0:ns, 0:64], in0=pso[:, :, 0:64], in1=statC[0:r, 0:ns].unsqueeze(2).to_broadcast([r, ns, 64]), op=ALU.mult),
                         reads=[("ps", obank), "statC"], writes=[("ost", rnd[0] % 2)])
                    if qt["halo"]:
                        dst = o_s[qt["q0"]:qt["q0"] + r, hq * 64:(hq + 1) * 64]
                        srcv = os_[0:r, 0, 0:64]
                    else:
                        dst = o_s[qt["q0"]:qt["q0"] + ns * 128, hq * 64:(hq + 1) * 64].rearrange("(s p) c -> p s c", p=128)
                        srcv = os_[:, 0:ns, 0:64]
                    P.dma(dst, srcv, reads=[("ost", rnd[0] % 2)], writes=[("o", qt["subs0"])], dkey="ost%d" % (rnd[0] % 2))
        for h in range(4):
            for m in range(2):
                P.dma(KT[0:64, m, :], kT_s[2 + h * 2 + m, :, 0:SK], reads=kT_keys, writes=["KT"], dkey="KTb%d" % m)
                P.dma(KT[64:68, m, :], job["kaug"][:, 0:SK], reads=[], writes=["KT"], dkey="KTx%d" % m)
            for c0 in range(0, nkb, 16):
                c1 = min(nkb, c0 + 16)
                P.dma(VtB[:, c0:c1, :], v_s[c0 * 128:c1 * 128, 130 + h * 129:130 + (h + 1) * 129].rearrange("(kb p) c -> p kb c", p=128), reads=v_keys[c0:c1], writes=["Vt"], dkey="Vtb")
            for qt in qtiles:
                qi_[0] += 1
                qs = qi_[0] % 2
                ns, r = qt["ns"], qt["r"]
                W = (ns - 1) * 128 + r
                for m in range(2):
                    for var in range(2):
                        P.dma(QT[qs][0:64, m, var, 0:W], qT_s[8 + h * 2 + m, :, qt["q0"]:qt["q0"] + W], reads=qt["keys"], writes=[("QT", qs)], dkey="QTb%d%d%d" % (qs, m, var))
                        P.dma(QT[qs][64:68, m, var, 0:W], job["qaug"][h, var, :, qt["q0"]:qt["q0"] + W], reads=[], writes=[("QT", qs)], dkey="QTx%d%d%d" % (qs, m, var))
                for m in range(2):
                    key_loop(qt, QT[qs], m, m, "b", h, [4 + 2 * m, 5 + 2 * m], VtB, 129, set())
                rnd[0] += 1

                def pso(m, half):
                    return ps[0:r, 4 + 2 * m + half, 0:258].rearrange("p (s c) -> p s c", c=129)
                nhalf = (ns + 1) // 2
                for m in range(2):
                    dstt = o1 if m == 0 else o2
                    for half in range(nhalf):
                        nn = min(2, ns - half * 2)
                        pv = pso(m, half)
                        P.op("dve", lambda e, pv=pv, nn=nn, m=m, half=half: e.reciprocal(out=statC[0:r, m * 4 + half * 2:m * 4 + half * 2 + nn].unsqueeze(2), in_=pv[:, 0:nn, 128:129]),
                             reads=[("ps", 4 + 2 * m + half)], writes=[("statC", m, half)])
                        P.op("dve", lambda e, pv=pv, nn=nn, m=m, half=half, dstt=dstt: e.tensor_tensor(
                            out=dstt[0:r, half * 2:half * 2 + nn, :], in0=pv[:, 0:nn, 0:128],
                            in1=statC[0:r, m * 4 + half * 2:m * 4 + half * 2 + nn].unsqueeze(2).to_broadcast([r, nn, 128]), op=ALU.mult),
                            reads=[("ps", 4 + 2 * m + half), ("statC", m, half)], writes=[("o12", m)])
                P.op("dve", lambda e, r=r, ns=ns: e.scalar_tensor_tensor(out=ob[0:r, 0:ns, :], in0=o2[0:r, 0:ns, :], scalar=neglam[0:r, 0:1], in1=o1[0:r, 0:ns, :], op0=ALU.mult, op1=ALU.add),
                     reads=[("o12", 0), ("o12", 1), "neglam"], writes=["ob"])
                P.op("act", lambda e, r=r, ns=ns: e.activation(out=sq2[0:r, 0:ns, :], in_=ob[0:r, 0:ns, :], func=AF.Square), reads=["ob"], writes=["sq2"])
                P.op("dve", lambda e, r=r, ns=ns: e.tensor_reduce(out=statA[0:r, 0:ns], in_=sq2[0:r, 0:ns, :], axis=AX.X, op=ALU.add), reads=["sq2"], writes=["statA"])
                rms_rstd(statA[0:r, 0:ns], 128.0, r, ["statA"], "statA")
                P.op("dve", lambda e, r=r, ns=ns: e.tensor_tensor(out=sq2[0:r, 0:ns, :], in0=ob[0:r, 0:ns, :], in1=statA[0:r, 0:ns].unsqueeze(2).to_broadcast([r, ns, 128]), op=ALU.mult),
                     reads=["ob", "statA"], writes=["sq2"])
                os_ = ost[rnd[0] % 2]
                P.op("pool", lambda e, r=r, ns=ns, os_=os_: e.tensor_tensor(out=os_[0:r, 0:ns, :], in0=sq2[0:r, 0:ns, :], in1=sg8[0:r, :].unsqueeze(1).to_broadcast([r, ns, 128]), op=ALU.mult),
                     reads=["sq2", "sg8"], writes=[("ost", rnd[0] % 2)])
                if qt["halo"]:
                    dst = o_s[qt["q0"]:qt["q0"] + r, 512 + h * 128:512 + (h + 1) * 128]
                    srcv = os_[0:r, 0, :]
                else:
                    dst = o_s[qt["q0"]:qt["q0"] + ns * 128, 512 + h * 128:512 + (h + 1) * 128].rearrange("(s p) c -> p s c", p=128)
                    srcv = os_[:, 0:ns, :]
                P.dma(dst, srcv, reads=[("ost", rnd[0] % 2)], writes=[("o", qt["subs0"])], dkey="ost%d" % (rnd[0] % 2))
        P.barrier(junk1[:, 0:1])

        def ple_block(hts, r, ns, W, gT, pT, wgt, wpt, layer, p_ap, prow0, ptile, pbf, sgt, tmp, hnb, gkey):
            for si in range(ns):
                ht = hts[si]
                P.op("act", lambda e, ht=ht, si=si: e.activation(out=tmp[0:r, :], in_=ht[0:r, :], func=AF.Square, accum_out=statB[0:r, si:si + 1]),
                     reads=[("ht", si)], writes=["tmp", ("statB", si)])
                rms_rstd(statB[0:r, si:si + 1], float(D), r, [("statB", si)], ("statB", si))
                hb = hnb[si % 2]
                P.op("dve", lambda e, ht=ht, si=si, hb=hb: e.tensor_scalar_mul(out=hb[0:r, :], in0=ht[0:r, :], scalar1=statB[0:r, si:si + 1]),
                     reads=[("ht", si), ("statB", si)], writes=[("hnb", si % 2)])
                transpose_rows(hb, r, 8, si, 0, [("hnb", si % 2)], None)
                P.dma(ptile[si % 2][0:r, :], p_ap[prow0 + si * 128:prow0 + si * 128 + r, :], writes=[("ptile", si % 2)], dkey="pt%d" % (si % 2))
                P.op("pool", lambda e, si=si: e.tensor_copy(out=pbf[si % 2][0:r, :], in_=ptile[si % 2][0:r, :]), reads=[("ptile", si % 2)], writes=[("pbf", si % 2)])
                for kc in range(2):
                    P.op("pe", lambda e, si=si, kc=kc: e.transpose(out=ps_bf(6)[:, kc * 512 + si * 128:kc * 512 + si * 128 + r], in_=pbf[si % 2][0:r, kc * 128:(kc + 1) * 128], identity=ident[0:r, 0:r]),
                         reads=[("pbf", si % 2), "ident"], writes=[("ps", 6)])
            evac_T(gT, 8, W, 0, gkey, gain=gple, gl=layer)
            for kc in range(2):
                evac_copy(pT[:, kc, 0:W], ps_bf(6)[:, kc * 512:kc * 512 + W], [("ps", 6)], ["pT"])
            for si in range(ns):
                ht = hts[si]
                for c in range(2):
                    linear(gT, r, si, wgt, 8, c * 512, 512, 4 + c, [gkey, "wgt"])
                    sigmoid_act(sgt[0:r, c * 512:(c + 1) * 512], ps[0:r, 4 + c, :], [("ps", 4 + c)], ("sgt", c))
                for c in range(2):
                    linear(pT, r, si, wpt, 2, c * 512, 512, 6 + c, ["pT", "wpt"])
                    P.op("dve", lambda e, c=c: e.tensor_tensor(out=tmp[0:r, c * 512:(c + 1) * 512], in0=ps[0:r, 6 + c, :], in1=sgt[0:r, c * 512:(c + 1) * 512], op=ALU.mult),
                         reads=[("ps", 6 + c), ("sgt", c)], writes=["tmp"])
                    P.op("pool", lambda e, c=c, ht=ht: e.tensor_tensor(out=ht[0:r, c * 512:(c + 1) * 512], in0=ht[0:r, c * 512:(c + 1) * 512], in1=tmp[0:r, c * 512:(c + 1) * 512], op=ALU.add),
                         reads=["tmp", ("ht", si)], writes=[("ht", si)])

        CG.reset(); CB.reset(); CW.reset()
        woab = load_w(CB, woab_s, D, D, "woab_t", "woab")
        wgt0 = load_w(CB, wg_s[0:D, :], D, D, "wgt", "wg")
        wpt0 = load_w(CB, wpp_s[0:PLE, :], PLE, D, "wpt", "wpp")
        hts = [CW.get([D], F32) for _ in range(4)]
        ot = [CW.get([D], F32) for _ in range(2)]
        gt_ = [CW.get([D], F32) for _ in range(2)]
        ybf = [CW.get([D], BF16) for _ in range(2)]
        yT = CW.get([8, 512], BF16)
        gT = yT
        pT = CW.get([2, 512], BF16)
        ptile = [CW.get([PLE], F32) for _ in range(2)]
        pbf = [CW.get([PLE], BF16) for _ in range(2)]
        sgt = CW.get([D], F32)
        tmp = CW.get([D], F32)
        hnb = [CW.get([D], BF16) for _ in range(2)]
        for qt in qtiles:
            ns, r = qt["ns"], qt["r"]
            W = (ns - 1) * 128 + r
            for si in range(ns):
                b = si % 2
                row = qt["q0"] + si * 128
                P.dma(ot[b][0:r, :], o_s[row:row + r, :], reads=[("o", qt["subs0"])], writes=[("ot", b)], dkey="ot%d" % b)
                P.dma(gt_[b][0:r, :], gate_s[row:row + r, :], reads=[("gate", qt["subs0"] + si)], writes=[("gt", b)], dkey="gt%d" % b)
                P.op("dve", lambda e, b=b: e.tensor_tensor(out=ybf[b][0:r, :], in0=ot[b][0:r, :], in1=gt_[b][0:r, :], op=ALU.mult), reads=[("ot", b), ("gt", b)], writes=[("ybf", b)])
                transpose_rows(ybf[b], r, 8, si, 0, [("ybf", b)], None)
            evac_T(yT, 8, W, 0, "yT")
            for si in range(ns):
                b = si % 2
                xrow = (qt["subs0"] + si) * 128
                P.dma(hts[si][0:r, :], x[xrow:xrow + r, :], writes=[("ht", si)], dkey="xq%d" % si)
                for c in range(2):
                    linear(yT, r, si, woab, 8, c * 512, 512, 4 + c, ["yT", "woab_t"])
                    P.op("dve", lambda e, c=c, si=si: e.tensor_tensor(out=hts[si][0:r, c * 512:(c + 1) * 512], in0=ps[0:r, 4 + c, :], in1=hts[si][0:r, c * 512:(c + 1) * 512], op=ALU.add),
                         reads=[("ps", 4 + c), ("ht", si)], writes=[("ht", si)])
            ple_block(hts, r, ns, W, gT, pT, wgt0, wpt0, 0, job["p0"], qt["q0"], ptile, pbf, sgt, tmp, hnb, "yT")
            for si in range(ns):
                row = qt["q0"] + si * 128
                P.dma(h1_s[row:row + r, :], hts[si][0:r, :], reads=[("ht", si)], writes=[("h1", qt["subs0"] + si)], dkey="h1st%d" % si)
        P.barrier(junk1[:, 0:1])

        CG.reset(); CB.reset(); CW.reset()
        wic = load_w(CG, wic_s, D, 2 * D, "wic_t", "wic")
        wgr = load_w(CG, wgrp_s, D, 256, "wgr_t", "wgrp")
        woc = load_w(CG, woc_s, D, D, "woc_t", "woc")
        wgt1 = load_w(CB, wg_s[D:2 * D, :], D, D, "wgt", "wg")
        wpt1 = load_w(CB, wpp_s[PLE:2 * PLE, :], PLE, D, "wpt", "wpp")
        NU = 5
        ubf = [CG.get([D], BF16) for _ in range(NU)]
        uh = CG.get([D], BF16)
        hT = CW.get([8, 256], BF16)
        sgb = [CB.get([D], BF16) for _ in range(4)]
        h1t = [CB.get([D], F32) for _ in range(2)]
        hnb1 = [CW.get([D], BF16) for _ in range(2)]
        pooled = [CW.get([D], BF16) for _ in range(2)]
        pooledT = CW.get([8, 256], BF16)
        y2 = [CW.get([D], BF16) for _ in range(2)]
        y2T = CW.get([8, 256], BF16)
        h2 = [CW.get([D], F32) for _ in range(2)]
        tmpc = CW.get([D], F32)
        sgt1 = CW.get([D], F32)
        tmp1 = CW.get([D], F32)
        gT1 = CW.get([8, 256], BF16)
        pT1 = CW.get([2, 256], BF16)
        ptile1 = [CW.get([PLE], F32) for _ in range(2)]
        pbf1 = [CW.get([PLE], BF16) for _ in range(2)]
        hnb2 = [CW.get([D], BF16) for _ in range(2)]
        sgf = CW.get([D], F32)

        def evac_T2(dstT, n_kc, W, key, gain=None, gl=0):
            evac_T(dstT, n_kc, W, 0, key, gain, gl)

        def stageA(subs_, r, want_g, dst_u):
            ns = len(subs_)
            W = (ns - 1) * 128 + r
            for si, sb in enumerate(subs_):
                b = si % 2
                row = qrow0(job, sb)
                P.dma(h1t[b][0:r, :], h1_s[row:row + r, :], reads=[("h1", sb)], writes=[("h1t", b)], dkey="h1t%d" % b)
                P.op("act", lambda e, b=b, si=si: e.activation(out=tmp1[0:r, :], in_=h1t[b][0:r, :], func=AF.Square, accum_out=statB[0:r, 8 + si:9 + si]),
                     reads=[("h1t", b)], writes=["tmp1", ("statB", 8 + si)])
                rms_rstd(statB[0:r, 8 + si:9 + si], float(D), r, [("statB", 8 + si)], ("statB", 8 + si))
                P.op("dve", lambda e, b=b, si=si: e.tensor_scalar_mul(out=hnb1[b][0:r, :], in0=h1t[b][0:r, :], scalar1=statB[0:r, 8 + si:9 + si]),
                     reads=[("h1t", b), ("statB", 8 + si)], writes=[("hnb1", b)])
                transpose_rows(hnb1[b], r, 8, si, 0, [("hnb1", b)], None)
            evac_T2(hT, 8, W, "hT", gain=gmix, gl=1)
            for si, sb in enumerate(subs_):
                ut, ukey = dst_u(sb)
                for c in range(2):
                    linear(hT, r, si, wic, 8, c * 512, 512, 4 + c, ["hT", "wic_t"])
                    evac_copy(ut[0:r, c * 512:(c + 1) * 512], ps[0:r, 4 + c, :], [("ps", 4 + c)], [ukey])
                if want_g:
                    for c in range(2):
                        linear(hT, r, si, wic, 8, D + c * 512, 512, 6 + c, ["hT", "wic_t"])
                        sigmoid_act(sgf[0:r, c * 512:(c + 1) * 512], ps[0:r, 6 + c, :], [("ps", 6 + c)], ("sgf", c))
                        P.op("dve", lambda e, c=c, sb=sb: e.tensor_tensor(out=sgb[sb % 4][0:r, c * 512:(c + 1) * 512], in0=sgf[0:r, c * 512:(c + 1) * 512], in1=ps[0:r, 6 + c, :], op=ALU.mult),
                             reads=[("ps", 6 + c), ("sgf", c)], writes=[("sgb", sb % 4)])

        l1macros = [list(range(m, min(m + 2, nown))) for m in range(0, nown, 2)]
        if halo:
            stageA([nown], 16, False, lambda sb: (uh, "uh"))
        stageA(l1macros[0], 128, True, lambda sb: (ubf[sb % NU], ("ubf", sb % NU)))
        bM = job["bandM"].rearrange("p (v g t) -> p v g t", v=3, g=4)
        bPN = bandPN.rearrange("p (v g t) -> p v g t", v=2, g=4)
        bH = bandH.rearrange("p (v g t) -> p v g t", v=2, g=4)
        ic = job["icnt"].rearrange("p (s g) -> p s g", g=4)
        for mi, subs_ in enumerate(l1macros):
            if mi + 1 < len(l1macros):
                stageA(l1macros[mi + 1], 128, True, lambda sb: (ubf[sb % NU], ("ubf", sb % NU)))
            ns = len(subs_)
            W = ns * 128
            r = 128
            for si, sb in enumerate(subs_):
                b = si % 2
                var = 0 if sb == 0 else (2 if sb == nown - 1 else 1)
                for half in range(2):
                    bank = 4 + half
                    for gg in range(2):
                        g = half * 2 + gg
                        cols = slice(g * 256, (g + 1) * 256)
                        mm = []
                        mm.append((bM[:, var, g, :], ubf[sb % NU][:, cols], [job["bkey"], ("ubf", sb % NU)]))
                        if sb > 0:
                            mm.append((bPN[:, 0, g, :], ubf[(sb - 1) % NU][:, cols], ["bandPN", ("ubf", (sb - 1) % NU)]))
                        elif halo:
                            mm.append((bH[0:16, 0, g, :], uh[0:16, cols], ["bandH", "uh"]))
                        if sb < nown - 1:
                            mm.append((bPN[:, 1, g, :], ubf[(sb + 1) % NU][:, cols], ["bandPN", ("ubf", (sb + 1) % NU)]))
                        elif halo:
                            mm.append((bH[0:16, 1, g, :], uh[0:16, cols], ["bandH", "uh"]))
                        for k_, (lt, rh, rd) in enumerate(mm):
                            P.op("pe", lambda e, lt=lt, rh=rh, bank=bank, gg=gg, k_=k_, nmm=len(mm): e.matmul(ps[:, bank, gg * 256:(gg + 1) * 256], lhsT=lt, rhs=rh,
                                                                                                             start=(k_ == 0 and gg == 0), stop=(k_ == nmm - 1), skip_group_check=True),
                                 reads=rd, writes=[("ps", bank)])
                    P.op("dve", lambda e, bank=bank, half=half, b=b, sb=sb: e.tensor_tensor(
                        out=pooled[b][:, half * 512:(half + 1) * 512].rearrange("p (g c) -> p g c", g=2), in0=ps[:, bank, :].rearrange("p (g c) -> p g c", g=2),
                        in1=ic[:, sb, half * 2:half * 2 + 2].unsqueeze(2).to_broadcast([128, 2, 256]), op=ALU.mult),
                        reads=[("ps", bank), job["ikey"]], writes=[("pooled", b)])
                transpose_rows(pooled[b], r, 8, si, 0, [("pooled", b)], None)
            evac_T2(pooledT, 8, W, "pooledT")
            for si, sb in enumerate(subs_):
                b = si % 2
                for half in range(2):
                    bank = 4 + half
                    for gg in range(2):
                        g = half * 2 + gg
                        for kc in range(2):
                            P.op("pe", lambda e, bank=bank, gg=gg, g=g, kc=kc, si=si: e.matmul(ps[:, bank, gg * 256:(gg + 1) * 256], lhsT=pooledT[:, g * 2 + kc, si * 128:(si + 1) * 128],
                                                                                               rhs=wgr[:, g * 2 + kc, :], start=(kc == 0 and gg == 0), stop=(kc == 1), skip_group_check=True),
                                 reads=["pooledT", "wgr_t"], writes=[("ps", bank)])
                    P.op("dve", lambda e, bank=bank, half=half: e.tensor_tensor(out=tmpc[:, half * 512:(half + 1) * 512], in0=ps[:, bank, :], in1=cscale[:, half * 512:(half + 1) * 512], op=ALU.mult),
                         reads=[("ps", bank), "cscale"], writes=[("tmpc", half)])
                    P.op("pool", lambda e, half=half, b=b, sb=sb: e.tensor_tensor(out=y2[b][:, half * 512:(half + 1) * 512], in0=tmpc[:, half * 512:(half + 1) * 512],
                                                                                  in1=sgb[sb % 4][:, half * 512:(half + 1) * 512], op=ALU.mult),
                         reads=[("tmpc", half), ("sgb", sb % 4)], writes=[("y2", b)])
                transpose_rows(y2[b], r, 8, si, 0, [("y2", b)], None)
            evac_T2(y2T, 8, W, "y2T")
            for si, sb in enumerate(subs_):
                b = si % 2
                row = sb * 128
                P.dma(h2[b][:, :], h1_s[row:row + 128, :], reads=[("h1", sb)], writes=[("ht", si)], dkey="h2l%d" % b)
                for c in range(2):
                    linear(y2T, r, si, woc, 8, c * 512, 512, 4 + c, ["y2T", "woc_t"])
                    P.op("dve", lambda e, c=c, b=b: e.tensor_tensor(out=h2[b][:, c * 512:(c + 1) * 512], in0=ps[:, 4 + c, :], in1=h2[b][:, c * 512:(c + 1) * 512], op=ALU.add),
                         reads=[("ps", 4 + c), ("ht", si)], writes=[("ht", si)])
            ple_block([h2[si % 2] for si in range(ns)], r, ns, W, gT1, pT1, wgt1, wpt1, 1, job["p1"], subs_[0] * 128, ptile1, pbf1, sgt1, tmp1, hnb2, "gT1")
            for si, sb in enumerate(subs_):
                P.dma(job["out"][sb * 128:(sb + 1) * 128, :], h2[si % 2][:, :], reads=[("ht", si)], writes=[("out", jn, sb)], dkey="out%d" % (si % 2))
        P.barrier(junk1[:, 0:1])

    P.emit()
    es.close()
    return nc, len(P.ops)


def rope_cs(pos):
    pos = np.asarray(pos, dtype=np.int64)
    row = (pos // GRID_W).astype(np.float32)
    col = (pos % GRID_W).astype(np.float32)
    half = HD // 2
    inv = (np.float32(THETA) ** (-np.arange(0, half, 2, dtype=np.float32) / np.float32(half))).astype(np.float32)
    ar = row[:, None] * inv
    ac = col[:, None] * inv
    ang = np.concatenate([ar, ar, ac, ac], axis=-1).astype(np.float32)
    cos = np.cos(ang).astype(np.float32)
    sin = np.sin(ang).astype(np.float32)
    sign = np.concatenate([-np.ones(16), np.ones(16), -np.ones(16), np.ones(16)]).astype(np.float32)
    return np.concatenate([cos, sin * sign[None, :]], axis=-1).astype(np.float32)


def kaug_rows(pos, sigma):
    pos = np.asarray(pos, dtype=np.int64)
    hi = (pos // 128) * 128
    lo = pos % 128
    s = np.asarray(sigma, dtype=np.float32)
    return np.stack([s, s, s * hi, s * lo], 0).astype(bf16)


def qaug_rows(pos, sigma):
    pos = np.asarray(pos, dtype=np.int64)
    hi = ((pos // 128) * 128).astype(np.float32)
    lo = (pos % 128).astype(np.float32)
    s = np.asarray(sigma, dtype=np.float32)
    out = []
    for h in range(4):
        s8 = np.float32(SLOPES[h] * 8.0)
        out.append(np.stack([s * s8 * hi, s * s8 * lo, -s * s8 * np.ones_like(hi), -s * s8 * np.ones_like(hi)], 0))
    return np.stack(out, 0).astype(bf16)


def band_tables(S, first, last):
    M = np.zeros((4, 128, 128), np.float32)
    for g, w in enumerate(POOL_W):
        for t in range(128):
            lo = t - w // 2
            hi = t + w // 2
            if first:
                lo = max(lo, 0)
            if last:
                hi = min(hi, 128)
            cnt = hi - lo
            for tp in range(max(lo, 0), min(hi, 128)):
                M[g, tp, t] = 1.0
            M[g, t, t] += -cnt
    return M


def band_pn():
    Pm = np.zeros((4, 128, 128), np.float32)
    Nm = np.zeros((4, 128, 128), np.float32)
    for g, w in enumerate(POOL_W):
        for t in range(128):
            for tp in range(128):
                if tp - 128 >= t - w // 2:
                    Pm[g, tp, t] = 1.0
                if tp + 128 < t + w // 2:
                    Nm[g, tp, t] = 1.0
    return Pm, Nm


def icnt_table(nsub, first_edge, last_edge):
    S = nsub * 128
    out = np.zeros((128, nsub, 4), np.float32)
    for g, w in enumerate(POOL_W):
        t = np.arange(S)
        lo = t - w // 2
        hi = t + w // 2
        if first_edge:
            lo = np.maximum(lo, 0)
        if last_edge:
            hi = np.minimum(hi, S)
        cnt = (hi - lo).astype(np.float32)
        out[:, :, g] = (1.0 / cnt).reshape(nsub, 128).T
    return out


def pack_pmajor(a):
    n = a.shape[0]
    return np.ascontiguousarray(np.transpose(a, (1, 0, 2)).reshape(128, -1))


def host_prepare(cfg, inp):
    CH, SP, NCH, SS, NSEQ, NOWN = cfg.CH, cfg.SP, cfg.NCH, cfg.SS, cfg.NSEQ, cfg.NOWN
    f32 = lambda a: np.ascontiguousarray(np.asarray(a, dtype=np.float32))
    x_prompt = f32(inp["x_prompt"]); x_sample = f32(inp["x_sample"])
    p_prompt = f32(inp["p_prompt"]); p_sample = f32(inp["p_sample"])
    shared = {}
    shared["w_in_ab"] = f32(inp["w_in_ab"][0]); shared["w_out_ab"] = f32(inp["w_out_ab"][0])
    shared["w_gate"] = f32(inp["w_ple_gate"]).reshape(2 * D, D); shared["w_proj"] = f32(inp["w_ple_proj"]).reshape(2 * PLE, D)
    shared["w_in_c"] = f32(inp["w_in_c"][0]); shared["w_grp"] = f32(inp["w_grp_c"][0]).reshape(D, 256); shared["w_out_c"] = f32(inp["w_out_c"][0])
    shared["gmix"] = np.ascontiguousarray(f32(inp["norm_mix"]).reshape(2, 8, 128).transpose(2, 0, 1).reshape(128, 16))
    shared["gple"] = np.ascontiguousarray(f32(inp["norm_ple"]).reshape(2, 8, 128).transpose(2, 0, 1).reshape(128, 16))
    shared["hgain"] = np.concatenate([f32(inp[k][0]) for k in ("qn_a", "kn_a", "qn_b", "kn_b")])[None, :]
    shared["lamv"] = np.concatenate([f32(inp[k][0]) for k in ("lam_q1", "lam_k1", "lam_q2", "lam_k2")])[None, :]
    shared["subln"] = f32(inp["subln_b"][0])[None, :]; shared["cscale"] = f32(inp["scale_c"][0])[None, :]
    spos = np.arange(SS)
    shared["cs_s"] = rope_cs(spos)
    shared["kaug_s"] = kaug_rows(spos, np.ones(SS))
    shared["qaug_s"] = np.ascontiguousarray(np.stack([qaug_rows(spos, np.ones(SS)), qaug_rows(spos, -np.ones(SS))], 1))
    kk = np.arange(128)
    db = np.stack([-(SLOPES[h] * 8.0) * np.abs(kk[None, :] - kk[:, None]) for h in range(4)], 1)
    shared["dbias"] = np.ascontiguousarray(db.reshape(128, 512)).astype(bf16)
    shared["vone_s"] = np.ones((128, cfg.NKBS), np.float32)
    shared["ident"] = np.eye(128).astype(bf16)
    Pm, Nm = band_pn()
    shared["bandPN"] = pack_pmajor(np.concatenate([Pm, Nm], 0)).astype(bf16)
    bs = np.concatenate([band_tables(SS, True, False), band_tables(SS, False, False), band_tables(SS, False, True)], 0)
    shared["bandM_s"] = pack_pmajor(bs).astype(bf16)
    shared["icnt_s"] = np.ascontiguousarray(icnt_table(cfg.NKBS, True, True).reshape(128, -1))
    in_maps = []
    for c in range(NCORES):
        b, j = c // NCH, c % NCH
        c0 = j * CH
        m = dict(shared)
        kpos = -np.ones(cfg.NSLOT, np.int64)
        kpos[0:CH] = c0 + np.arange(CH)
        hb = NOWN * 128
        hpos = -np.ones(16, np.int64)
        if j > 0:
            hpos[0:8] = c0 - 8 + np.arange(8)
        if j < NCH - 1:
            hpos[8:16] = c0 + CH + np.arange(8)
        kpos[hb:hb + 16] = hpos
        used = np.zeros(SP, bool)
        used[kpos[kpos >= 0]] = True
        far = np.nonzero(~used)[0]
        fb = (NOWN + 1) * 128
        assert len(far) <= cfg.NFAR * 128
        kpos[fb:fb + len(far)] = far
        real = kpos >= 0
        kp0 = np.where(real, kpos, 0)
        xp = np.zeros((cfg.NSLOT, D), np.float32)
        xp[real] = x_prompt[b, kpos[real]]
        m["xp"] = xp
        pp0 = np.zeros((cfg.SQP, PLE), np.float32)
        pp0[0:CH] = p_prompt[0, b, c0:c0 + CH]
        hr = hpos >= 0
        pp0[CH:CH + 16][hr] = p_prompt[0, b, hpos[hr]]
        m["pp0"] = pp0
        m["pp1"] = np.ascontiguousarray(p_prompt[1, b, c0:c0 + CH])
        m["xs"] = np.ascontiguousarray(x_sample[c * NSEQ:(c + 1) * NSEQ])
        m["psm"] = np.ascontiguousarray(p_sample[:, c * NSEQ:(c + 1) * NSEQ])
        m["cs_p"] = rope_cs(kp0)
        sigma = np.where(kp0 < c0, -1.0, 1.0)
        sigma[0:CH] = 1.0
        sigma[~real] = 1.0
        kpa = np.where(real, kpos, 2 * SP)
        m["kaug_p"] = kaug_rows(kpa, sigma)
        hq = hpos.copy()
        hq[0:8] = np.where(hpos[0:8] >= 0, hpos[0:8], 0)
        hq[8:16] = np.where(hpos[8:16] >= 0, hpos[8:16], SP)
        qpos = np.concatenate([c0 + np.arange(CH), hq])
        s0 = np.ones(cfg.SQP); s1 = -np.ones(cfg.SQP)
        s1[CH:CH + 8] = 1.0
        s1[CH + 8:CH + 16] = -1.0
        m["qaug_p"] = np.ascontiguousarray(np.stack([qaug_rows(qpos, s0), qaug_rows(qpos, s1)], 1))
        bh = np.zeros((128, 2, 4, 16), np.float32)
        for s in range(16):
            if hpos[s] < 0:
                continue
            for q in range(16):
                if hpos[q] < 0:
                    continue
                dist = abs(int(hpos[q]) - int(hpos[s]))
                for h in range(4):
                    s8 = SLOPES[h] * 8.0
                    bh[s, 0, h, q] = -s8 * ((dist // 128) * 128)
                    bh[s, 1, h, q] = -s8 * (dist % 128)
        m["bhalo"] = np.ascontiguousarray(bh.reshape(128, 128)).astype(bf16)
        m["vone_p"] = np.ascontiguousarray(real.astype(np.float32).reshape(cfg.NKBP, 128).T)
        first_edge = (j == 0); last_edge = (j == NCH - 1)
        bp = np.concatenate([band_tables(CH, first_edge, False), band_tables(CH, False, False), band_tables(CH, False, last_edge)], 0)
        m["bandM_p"] = pack_pmajor(bp).astype(bf16)
        HPm = np.zeros((4, 16, 128), np.float32); HNm = np.zeros((4, 16, 128), np.float32)
        if j > 0:
            HPm[:, 0:8, :] = Pm[:, 120:128, :]
        if j < NCH - 1:
            HNm[:, 8:16, :] = Nm[:, 0:8, :]
        m["bandH"] = np.ascontiguousarray(np.transpose(np.concatenate([HPm, HNm], 0), (1, 0, 2)).reshape(16, -1)).astype(bf16)
        m["icnt_p"] = np.ascontiguousarray(icnt_table(NOWN, first_edge, last_edge).reshape(128, -1))
        in_maps.append(m)
    return in_maps


_CACHE = {}


def run(cfg, inp, runner=None):
    key = (cfg.SP, cfg.NCH, cfg.SS, cfg.NSEQ)
    if key not in _CACHE:
        _CACHE[key] = build_program(cfg)
    nc, nops = _CACHE[key]
    in_maps = host_prepare(cfg, inp)
    if runner is None:
        res = run_bass_kernel_spmd(nc, in_maps, core_ids=list(range(NCORES))).results
    else:
        res = runner(nc, in_maps)
    B = NCORES // cfg.NCH
    yp = np.zeros((B, cfg.SP, D), np.float32)
    ys = np.zeros((NCORES * cfg.NSEQ, cfg.SS, D), np.float32)
    for c in range(len(res)):
        b, j = c // cfg.NCH, c % cfg.NCH
        yp[b, j * cfg.CH:(j + 1) * cfg.CH] = np.asarray(res[c]["yp"], dtype=np.float32)
        ys[c * cfg.NSEQ:(c + 1) * cfg.NSEQ] = np.asarray(res[c]["ys"], dtype=np.float32)
    return yp, ys


def kernel(**inputs):
    cfg = Cfg()
    return run(cfg, inputs)
```

```python
import math
import contextlib
import numpy as np
import ml_dtypes
import concourse.bass as bass
import concourse.mybir as mybir
from concourse.bass_utils import run_bass_kernel_spmd

F32 = mybir.dt.float32
BF16 = mybir.dt.bfloat16
AF = mybir.ActivationFunctionType
ALU = mybir.AluOpType
AX = mybir.AxisListType
bf16 = ml_dtypes.bfloat16

D = 1024
HD = 64
PLE = 256
EPS = 1e-6
GRID_W = 64
THETA = 10000.0
SLOPES = [2.0 ** (-8.0 * (h + 1) / 4) for h in range(4)]
LAM_INIT = 0.8 - 0.6 * math.exp(-0.3 * 0)
POOL_W = (2, 4, 8, 16)
VW = 2 * 65 + 4 * 129
NCORES = 8
SAME_ENGINE_SYNC = True
PIPE = True


class Cfg:
    def __init__(self, SP=16384, NCH=4, SS=2048, NSEQ=4):
        self.SP, self.NCH, self.SS, self.NSEQ = SP, NCH, SS, NSEQ
        self.CH = SP // NCH
        self.NOWN = self.CH // 128
        self.NFAR = -(-(SP - self.CH - 8) // 128)
        self.NKBP = self.NOWN + 1 + self.NFAR
        self.NSLOT = self.NKBP * 128
        self.SQP = self.CH + 16
        self.NKBS = SS // 128
        self.NKBMAX = max(self.NKBP, self.NKBS)
        self.SQMAX = max(self.SQP, SS)


class Op:
    __slots__ = ("eng", "fn", "deps", "ddeps", "is_dma", "dsem", "dval", "inc", "cnt", "idx")


class _Rec:
    def __init__(self):
        self.calls = []

    def __getattr__(self, name):
        def f(*a, **k):
            self.calls.append((name, a, k))
        return f


class Prog:
    def __init__(self, nc):
        self.nc = nc
        self.ops = []
        self.lastw = {}
        self.readers = {}
        self.dma_keys = {}

    def _add(self, eng, fn, reads, writes, is_dma=False, dkey=None, phase=True):
        o = Op()
        rec = _Rec()
        fn(rec)
        assert len(rec.calls) == 1
        name_, a_, k_ = rec.calls[0]
        fn = lambda e, name_=name_, a_=a_, k_=k_: getattr(e, name_)(*a_, **k_)
        o.eng = eng; o.fn = fn; o.is_dma = is_dma; o.inc = False; o.cnt = 0
        o.idx = len(self.ops)
        reads = list(reads)
        if phase:
            reads.append("phase")
        deps = set()
        for k in reads:
            w = self.lastw.get(k)
            if w is not None:
                deps.add(w)
        for k in writes:
            w = self.lastw.get(k)
            if w is not None:
                deps.add(w)
            for r in self.readers.get(k, ()):
                deps.add(r)
        deps.discard(o.idx)
        o.deps = set(d for d in deps if not self.ops[d].is_dma)
        o.ddeps = {}
        for d in deps:
            p = self.ops[d]
            if p.is_dma:
                o.ddeps[p.dsem] = self.dma_keys[p.dsem]
        for k in reads:
            self.readers.setdefault(k, []).append(o.idx)
        for k in writes:
            self.lastw[k] = o.idx
            self.readers[k] = []
        if is_dma:
            o.dsem = dkey
            self.dma_keys[dkey] = self.dma_keys.get(dkey, 0) + 16
            o.dval = self.dma_keys[dkey]
        self.ops.append(o)
        return o

    def op(self, eng, fn, reads=(), writes=()):
        return self._add(eng, fn, reads, writes)

    def dma(self, out, in_, reads=(), writes=(), dkey=None, queue="sp"):
        assert dkey is not None
        fn = lambda e: e.dma_start(out=out, in_=in_)
        return self._add(queue, fn, reads, writes, is_dma=True, dkey=dkey)

    def barrier(self, tile):
        self._add("dve", lambda e: e.memset(tile, 0.0), [], ["phase"], phase=False)

    def emit(self):
        nc = self.nc
        ops = self.ops
        engs = ["pe", "act", "dve", "pool", "sp"]

        def same_skip(p, o):
            return p.eng == o.eng and (not o.is_dma) and (p.eng == "pe" or not SAME_ENGINE_SYNC)

        for o in ops:
            for d in o.deps:
                p = ops[d]
                if same_skip(p, o):
                    continue
                p.inc = True
        cnt = {e: 0 for e in engs}
        for o in ops:
            if not o.is_dma and o.inc:
                cnt[o.eng] += 1
                o.cnt = cnt[o.eng]
        es = contextlib.ExitStack()
        sem = {e: es.enter_context(nc.semaphore("s_" + e)) for e in engs}
        dsem = {}
        for i, k in enumerate(self.dma_keys):
            dsem[k] = es.enter_context(nc.semaphore("d%d" % i))
        block = es.enter_context(nc.Block())
        per = {e: [o for o in ops if o.eng == e] for e in engs}
        final = [(dsem[k], v) for k, v in self.dma_keys.items()]

        def run(e, engine):
            seen = {}
            for o in per[e]:
                waits = {}
                cands = [(dsem[k], v) for k, v in o.ddeps.items()]
                for d in o.deps:
                    p = ops[d]
                    if same_skip(p, o):
                        continue
                    cands.append((sem[p.eng], p.cnt))
                for s, v in cands:
                    key = id(s)
                    if seen.get(key, 0) >= v:
                        continue
                    if key not in waits or waits[key][1] < v:
                        waits[key] = (s, v)
                for key, (s, v) in waits.items():
                    engine.wait_ge(s, v)
                    seen[key] = v
                ins = o.fn(engine)
                if o.is_dma:
                    ins.then_inc(dsem[o.dsem], 16)
                elif o.inc:
                    ins.then_inc(sem[e], 1)
            if e == "sp":
                for s, v in final:
                    engine.wait_ge(s, v)

        @block.tensor
        def _(eng):
            run("pe", eng)

        @block.scalar
        def _(eng):
            run("act", eng)

        @block.vector
        def _(eng):
            run("dve", eng)

        @block.gpsimd
        def _(eng):
            run("pool", eng)

        @block.sync
        def _(eng):
            run("sp", eng)

        es.close()


def build_program(cfg):
    nc = bass.Bass("TRN2", target_bir_lowering=False)
    es = contextlib.ExitStack()
    CH, SS, NSEQ, NOWN = cfg.CH, cfg.SS, cfg.NSEQ, cfg.NOWN
    NSLOT, SQP, NKBP, NKBS = cfg.NSLOT, cfg.SQP, cfg.NKBP, cfg.NKBS

    def din(name, shape, dtype=F32):
        return nc.dram_tensor(name, list(shape), dtype, kind="ExternalInput").ap()

    def dscr(name, shape, dtype):
        return nc.dram_tensor(name, list(shape), dtype, kind="Internal").ap()

    xp = din("xp", [NSLOT, D]); pp0 = din("pp0", [SQP, PLE]); pp1 = din("pp1", [CH, PLE])
    xs = din("xs", [NSEQ, SS, D]); psm = din("psm", [2, NSEQ, SS, PLE])
    w_in_ab = din("w_in_ab", [D, 3328]); w_out_ab = din("w_out_ab", [D, D])
    w_gate = din("w_gate", [2 * D, D]); w_proj = din("w_proj", [2 * PLE, D])
    w_in_c = din("w_in_c", [D, 2 * D]); w_grp = din("w_grp", [D, 256]); w_out_c = din("w_out_c", [D, D])
    gmix_d = din("gmix", [128, 16]); gple_d = din("gple", [128, 16])
    hgain_d = din("hgain", [1, 4 * 64])
    lamv_d = din("lamv", [1, 4 * 64])
    subln_d = din("subln", [1, 128]); cscale_d = din("cscale", [1, D])
    cs_p = din("cs_p", [NSLOT, 128]); cs_s = din("cs_s", [SS, 128])
    kaug_p = din("kaug_p", [4, NSLOT], BF16); kaug_s = din("kaug_s", [4, SS], BF16)
    qaug_p = din("qaug_p", [4, 2, 4, SQP], BF16); qaug_s = din("qaug_s", [4, 2, 4, SS], BF16)
    dbias_d = din("dbias", [128, 4 * 128], BF16); bhalo_d = din("bhalo", [128, 2 * 4 * 16], BF16)
    vone_p_d = din("vone_p", [128, NKBP]); vone_s_d = din("vone_s", [128, NKBS])
    ident_d = din("ident", [128, 128], BF16)
    bandM_p_d = din("bandM_p", [128, 3 * 4 * 128], BF16); bandM_s_d = din("bandM_s", [128, 3 * 4 * 128], BF16)
    bandPN_d = din("bandPN", [128, 2 * 4 * 128], BF16); bandH_d = din("bandH", [16, 2 * 4 * 128], BF16)
    icnt_p_d = din("icnt_p", [128, NOWN * 4]); icnt_s_d = din("icnt_s", [128, NKBS * 4])
    yp = nc.dram_tensor("yp", [CH, D], F32, kind="ExternalOutput").ap()
    ys = nc.dram_tensor("ys", [NSEQ, SS, D], F32, kind="ExternalOutput").ap()

    wab_s = dscr("wab_s", [D, 3328], BF16); woab_s = dscr("woab_s", [D, D], BF16)
    wg_s = dscr("wg_s", [2 * D, D], BF16); wpp_s = dscr("wpp_s", [2 * PLE, D], BF16)
    wic_s = dscr("wic_s", [D, 2 * D], BF16); wgrp_s = dscr("wgrp_s", [D, 256], BF16); woc_s = dscr("woc_s", [D, D], BF16)
    SKMAX = cfg.NKBMAX * 128
    kT_s = dscr("kT_s", [10, 64, SKMAX], BF16)
    v_s = dscr("v_s", [SKMAX, VW], BF16)
    qT_s = dscr("qT_s", [16, 64, cfg.SQMAX], BF16)
    gate_s = dscr("gate_s", [cfg.SQMAX, D], F32)
    o_s = dscr("o_s", [cfg.SQMAX, D], F32)
    h1_s = dscr("h1_s", [cfg.SQMAX, D], F32)

    NB_G = 67 * 1024; NB_B = 36 * 1024; NB_W = 62 * 1024
    arenaG = es.enter_context(nc.sbuf_tensor("arenaG", [128, NB_G // 2], BF16))
    arenaB = es.enter_context(nc.sbuf_tensor("arenaB", [128, NB_B // 2], BF16))
    arenaW = es.enter_context(nc.sbuf_tensor("arenaW", [128, NB_W // 2], BF16))
    consts = es.enter_context(nc.sbuf_tensor("consts", [128, 3072], F32))
    cbf = es.enter_context(nc.sbuf_tensor("cbf", [128, 6144], BF16))
    ps = es.enter_context(nc.psum_tensor("ps", [128, 8, 512], F32))

    class Carver:
        def __init__(self, arena, nbytes):
            self.a, self.n, self.off = arena, nbytes, 0

        def reset(self):
            self.off = 0

        def get(self, free_shape, dtype):
            esz = 4 if dtype == F32 else 2
            n = int(np.prod(free_shape)) * esz
            n_al = (n + 31) // 32 * 32
            assert self.off + n_al <= self.n, ("arena overflow", self.off, n_al, self.n)
            v = self.a[:, self.off // 2:(self.off + n) // 2]
            self.off += n_al
            if dtype == F32:
                v = v.bitcast(F32)
            if len(free_shape) == 2:
                v = v.rearrange("p (a b) -> p a b", a=free_shape[0])
            elif len(free_shape) == 3:
                v = v.rearrange("p (a b c) -> p a b c", a=free_shape[0], b=free_shape[1])
            return v

    CG = Carver(arenaG, NB_G); CB = Carver(arenaB, NB_B); CW = Carver(arenaW, NB_W)

    coff = [0]

    def cget(n):
        v = consts[:, coff[0]:coff[0] + n]; coff[0] += n
        assert coff[0] <= 3072
        return v
    gmix = cget(16); gple = cget(16); hgain = cget(256); lamv = cget(256); subln = cget(128)
    cscale = cget(D); vone_p = cget(NKBP); vone_s = cget(NKBS)
    icnt_p = cget(NOWN * 4); icnt_s = cget(NKBS * 4)
    lam_t = cget(4); neglam = cget(1); sg8 = cget(128); junk1 = cget(8)
    statA = cget(64); statB = cget(64); statC = cget(64)
    identf = cget(128)
    boff = [0]

    def bget(n):
        v = cbf[:, boff[0]:boff[0] + n]; boff[0] += n
        assert boff[0] <= 6144
        return v
    ident = bget(128); dbias = bget(512); bhalo = bget(128)
    bandM_p = bget(1536); bandM_s = bget(1536); bandPN = bget(1024); bandH = bget(1024)

    P = Prog(nc)
    dk = [0]

    def DK(name):
        return name

    def cload(dst, src, key, bc=False):
        P.dma(dst, src.partition_broadcast(128) if bc else src, writes=[key], dkey="c_" + key)
    cload(gmix, gmix_d[:, :], "gmix"); cload(gple, gple_d[:, :], "gple")
    cload(hgain, hgain_d[0:1, :], "hgain", True); cload(lamv, lamv_d[0:1, :], "lamv", True)
    cload(subln, subln_d[0:1, :], "subln", True); cload(cscale, cscale_d[0:1, :], "cscale", True)
    cload(vone_p, vone_p_d[:, :], "vone_p"); cload(vone_s, vone_s_d[:, :], "vone_s")
    cload(icnt_p, icnt_p_d[:, :], "icnt_p"); cload(icnt_s, icnt_s_d[:, :], "icnt_s")
    cload(ident, ident_d[:, :], "ident"); cload(dbias, dbias_d[:, :], "dbias"); cload(bhalo, bhalo_d[:, :], "bhalo")
    cload(bandM_p, bandM_p_d[:, :], "bandM_p"); cload(bandM_s, bandM_s_d[:, :], "bandM_s")
    cload(bandPN, bandPN_d[:, :], "bandPN")
    P.dma(bandH[0:16, :], bandH_d[:, :], writes=["bandH"], dkey="c_bandH")
    P.op("dve", lambda e: e.tensor_copy(out=identf[:, :], in_=ident[:, :]), reads=["ident"], writes=["identf"])
    lv = lamv.rearrange("p (a d) -> p a d", a=4)
    P.op("dve", lambda e: e.tensor_tensor(out=statA[:, 0:64], in0=lv[:, 0, :], in1=lv[:, 1, :], op=ALU.mult), reads=["lamv"], writes=["statA"])
    P.op("dve", lambda e: e.tensor_reduce(out=lam_t[:, 0:1], in_=statA[:, 0:64], axis=AX.X, op=ALU.add), reads=["statA"], writes=["lam0"])
    P.op("dve", lambda e: e.tensor_tensor(out=statB[:, 0:64], in0=lv[:, 2, :], in1=lv[:, 3, :], op=ALU.mult), reads=["lamv"], writes=["statB"])
    P.op("dve", lambda e: e.tensor_reduce(out=lam_t[:, 1:2], in_=statB[:, 0:64], axis=AX.X, op=ALU.add), reads=["statB"], writes=["lam1"])
    P.op("act", lambda e: e.activation(out=lam_t[:, 2:4], in_=lam_t[:, 0:2], func=AF.Exp), reads=["lam0", "lam1"], writes=["lam2"])
    P.op("dve", lambda e: e.tensor_tensor(out=neglam[:, 0:1], in0=lam_t[:, 3:4], in1=lam_t[:, 2:3], op=ALU.subtract), reads=["lam2"], writes=["neglam"])
    P.op("dve", lambda e: e.tensor_scalar_add(out=neglam[:, 0:1], in0=neglam[:, 0:1], scalar1=-LAM_INIT), reads=["neglam"], writes=["neglam"])
    P.op("dve", lambda e: e.tensor_scalar_mul(out=sg8[:, :], in0=subln[:, :], scalar1=1.0 - LAM_INIT), reads=["subln"], writes=["sg8"])

    CW.reset()
    wtf = [CW.get([2048], F32) for _ in range(2)]
    wtb = [CW.get([2048], BF16) for _ in range(2)]
    wi = [0]

    def wconv(src, dst, K, N, name):
        for kc in range(K // 128):
            for n0 in range(0, N, 2048):
                nn = min(2048, N - n0)
                i = wi[0] % 2; wi[0] += 1
                P.dma(wtf[i][:, 0:nn], src[kc * 128:(kc + 1) * 128, n0:n0 + nn], writes=[("wtf", i)], dkey="wld%d" % i)
                eng = ("dve", "pool", "act")[wi[0] % 3]
                if eng == "act":
                    P.op("act", lambda e, i=i, nn=nn: e.copy(out=wtb[i][:, 0:nn], in_=wtf[i][:, 0:nn]), reads=[("wtf", i)], writes=[("wtb", i)])
                else:
                    P.op(eng, lambda e, i=i, nn=nn: e.tensor_copy(out=wtb[i][:, 0:nn], in_=wtf[i][:, 0:nn]), reads=[("wtf", i)], writes=[("wtb", i)])
                P.dma(dst[kc * 128:(kc + 1) * 128, n0:n0 + nn], wtb[i][:, 0:nn], reads=[("wtb", i)], writes=[("w", name)], dkey="wst%d" % i)
    wconv(w_in_ab, wab_s, D, 3328, "wab"); wconv(w_out_ab, woab_s, D, D, "woab")
    wconv(w_gate, wg_s, 2 * D, D, "wg"); wconv(w_proj, wpp_s, 2 * PLE, D, "wpp")
    wconv(w_in_c, wic_s, D, 2 * D, "wic"); wconv(w_grp, wgrp_s, D, 256, "wgrp"); wconv(w_out_c, woc_s, D, D, "woc")
    P.barrier(junk1[:, 0:1])

    jobs = []
    jobs.append(dict(name="p", x=xp, nkb=NKBP, nown=NOWN, halo=True, cs=cs_p, kaug=kaug_p, qaug=qaug_p, vone=vone_p,
                     p0=pp0, p1=pp1, icnt=icnt_p, bandM=bandM_p, out=yp, vkey="vone_p", ikey="icnt_p", bkey="bandM_p"))
    for s in range(NSEQ):
        jobs.append(dict(name="s%d" % s, x=xs[s], nkb=NKBS, nown=NKBS, halo=False, cs=cs_s, kaug=kaug_s, qaug=qaug_s, vone=vone_s,
                         p0=psm[0, s], p1=psm[1, s], icnt=icnt_s, bandM=bandM_s, out=ys[s], vkey="vone_s", ikey="icnt_s", bkey="bandM_s"))

    psb = ps.bitcast(BF16) if False else None

    def ps_bf(bank):
        return ps[:, bank, :].bitcast(BF16)

    evac_i = [0]

    def evac_copy(out, in_, reads, writes, scale=None):
        evac_i[0] += 1
        if evac_i[0] % 2 == 0:
            if scale is None:
                P.op("dve", lambda e: e.tensor_copy(out=out, in_=in_), reads=reads, writes=writes)
            else:
                P.op("dve", lambda e: e.tensor_scalar_mul(out=out, in0=in_, scalar1=scale), reads=reads, writes=writes)
        else:
            if scale is None:
                P.op("act", lambda e: e.copy(out=out, in_=in_), reads=reads, writes=writes)
            else:
                P.op("act", lambda e: e.activation(out=out, in_=in_, func=AF.Copy, scale=scale), reads=reads, writes=writes)

    def rms_rstd(ss_ap, n, r, reads, key):
        P.op("act", lambda e: e.activation(out=ss_ap, in_=ss_ap, func=AF.Ln, scale=1.0 / n, bias=EPS), reads=reads, writes=[key])
        P.op("act", lambda e: e.activation(out=ss_ap, in_=ss_ap, func=AF.Exp, scale=-0.5), reads=[key], writes=[key])

    def sigmoid_act(dst, src, reads, key):
        P.op("act", lambda e: e.activation(out=dst, in_=src, func=AF.Exp, scale=-1.0), reads=reads, writes=[key])
        P.op("act", lambda e: e.activation(out=dst, in_=dst, func=AF.Ln, scale=1.0, bias=1.0), reads=[key], writes=[key])
        P.op("act", lambda e: e.activation(out=dst, in_=dst, func=AF.Exp, scale=-1.0), reads=[key], writes=[key])

    def transpose_rows(src_bf, r, n_kc, si, bank0, reads, tagw):
        for kc in range(n_kc):
            pb = ps_bf(bank0 + kc // 2)
            c0 = (kc % 2) * 512 + si * 128
            P.op("pe", lambda e, pb=pb, c0=c0, kc=kc: e.transpose(out=pb[:, c0:c0 + r], in_=src_bf[0:r, kc * 128:(kc + 1) * 128], identity=ident[0:r, 0:r]),
                 reads=reads + ["ident"], writes=[("ps", bank0 + kc // 2)])

    def evac_T(dstT, n_kc, W, bank0, key, gain=None, gl=0):
        for kc in range(n_kc):
            pb = ps_bf(bank0 + kc // 2)
            c0 = (kc % 2) * 512
            sc = None if gain is None else gain[:, gl * 8 + kc:gl * 8 + kc + 1]
            evac_copy(dstT[:, kc, 0:W], pb[:, c0:c0 + W], reads=[("ps", bank0 + kc // 2)] + ([] if gain is None else ["gmix", "gple"]), writes=[key], scale=sc)

    def linear(actT, r, si, wt, n_kc, col0, ncols, bank, reads):
        for kc in range(n_kc):
            P.op("pe", lambda e, kc=kc: e.matmul(ps[0:r, bank, 0:ncols], lhsT=actT[:, kc, si * 128:si * 128 + r], rhs=wt[:, kc, col0:col0 + ncols],
                                                 start=(kc == 0), stop=(kc == n_kc - 1)),
                 reads=reads, writes=[("ps", bank)])

    def load_w(carver, src, K, N, key, wkey):
        t = carver.get([K // 128, N], BF16)
        P.dma(t, src.rearrange("(kc p) n -> p kc n", p=128), reads=[("w", wkey)], writes=[key], dkey="wl_" + key)
        return t

    def macros_of(job):
        out = []
        n = job["nown"]
        for m in range(0, n, 4):
            out.append((list(range(m, min(m + 4, n))), True, 128))
        k = n
        if job["halo"]:
            out.append(([n], True, 16))
            k = n + 1
        while k < job["nkb"]:
            out.append((list(range(k, min(k + 4, job["nkb"]))), False, 128))
            k += 4
        return out

    def qrow0(job, sub):
        return sub * 128 if sub < job["nown"] else job["nown"] * 128

    for job in jobs:
        nkb, nown, halo = job["nkb"], job["nown"], job["halo"]
        x = job["x"]
        SK = nkb * 128
        jn = job["name"]
        CG.reset(); CB.reset(); CW.reset()
        wab = load_w(CG, wab_s, D, 3328, "wab_t", "wab")
        xt = [CW.get([D], F32) for _ in range(2)]
        xnb = [CW.get([D], BF16) for _ in range(2)]
        xnT = CW.get([8, 512], BF16)
        sqt = CW.get([D], F32)
        t1 = CW.get([512], F32)
        t2 = CW.get([512], F32)
        t3 = CW.get([512], F32)
        knb = CW.get([640], BF16)
        qnb = CW.get([1024], BF16)
        kst = [CW.get([10, 128], BF16) for _ in range(2)]
        qst = [CW.get([16, 128], BF16) for _ in range(2)]
        vst = [CW.get([VW], BF16) for _ in range(2)]
        gst = [CW.get([D], F32) for _ in range(2)]
        cst = [CW.get([128], F32) for _ in range(2)]
        hg = hgain.rearrange("p (a d) -> p a d", a=4)

        def norm_heads(pv, r, nh, gi, rope, cs_t, out_bf, pkeys, okey):
            n = nh * 64
            P.op("act", lambda e: e.activation(out=sqt[0:r, 0:n], in_=pv, func=AF.Square), reads=pkeys, writes=["sqt"])
            P.op("dve", lambda e: e.tensor_reduce(out=statA[0:r, 0:nh], in_=sqt[0:r, 0:n].rearrange("p (h d) -> p h d", d=64), axis=AX.X, op=ALU.add),
                 reads=["sqt"], writes=["statA"])
            rms_rstd(statA[0:r, 0:nh], 64.0, r, ["statA"], "statA")
            P.op("dve", lambda e: e.tensor_tensor(out=t1[0:r, 0:n].rearrange("p (h d) -> p h d", d=64), in0=pv.rearrange("p (h d) -> p h d", d=64),
                                                  in1=statA[0:r, 0:nh].unsqueeze(2).to_broadcast([r, nh, 64]), op=ALU.mult),
                 reads=pkeys + ["statA"], writes=["t1"])
            gb = hg[0:r, gi, :].unsqueeze(1).to_broadcast([r, nh, 64])
            if not rope:
                P.op("pool", lambda e: e.tensor_tensor(out=out_bf.rearrange("p (h d) -> p h d", d=64), in0=t1[0:r, 0:n].rearrange("p (h d) -> p h d", d=64), in1=gb, op=ALU.mult),
                     reads=["t1", "hgain"], writes=[okey])
                return
            P.op("pool", lambda e: e.tensor_tensor(out=t2[0:r, 0:n].rearrange("p (h d) -> p h d", d=64), in0=t1[0:r, 0:n].rearrange("p (h d) -> p h d", d=64), in1=gb, op=ALU.mult),
                 reads=["t1", "hgain"], writes=["t2"])
            v2 = t2[0:r, 0:n].rearrange("p (h a b c) -> p h a b c", a=2, b=2, c=16)
            v3 = t3[0:r, 0:n].rearrange("p (h a b c) -> p h a b c", a=2, b=2, c=16)
            cosv = cs_t[0:r, 0:64].unsqueeze(1).to_broadcast([r, nh, 64])
            sinv = cs_t[0:r, 64:128].rearrange("p (a b c) -> p a b c", a=2, b=2)
            for half in range(2):
                P.op("pool", lambda e, half=half: e.tensor_tensor(out=v3[:, :, :, half, :], in0=v2[:, :, :, 1 - half, :],
                                                                  in1=sinv[:, :, half, :].unsqueeze(1).to_broadcast([r, nh, 2, 16]), op=ALU.mult),
                     reads=["t2", "cst"], writes=["t3"])
            P.op("dve", lambda e: e.tensor_tensor(out=t1[0:r, 0:n].rearrange("p (h d) -> p h d", d=64), in0=t2[0:r, 0:n].rearrange("p (h d) -> p h d", d=64), in1=cosv, op=ALU.mult),
                 reads=["t2", "cst"], writes=["t1"])
            P.op("dve", lambda e: e.tensor_tensor(out=out_bf, in0=t1[0:r, 0:n], in1=t3[0:r, 0:n], op=ALU.add), reads=["t1", "t3"], writes=[okey])

        gsub = 0
        r = 128
        for (subs, own, rq) in macros_of(job):
            ns = len(subs)
            W = ns * 128
            for si, sb in enumerate(subs):
                b = gsub % 2; gsub += 1
                P.dma(xt[b][:, :], x[sb * 128:(sb + 1) * 128, :], writes=[("xt", b)], dkey="xt%d" % b)
                P.op("act", lambda e, b=b, si=si: e.activation(out=sqt[:, :], in_=xt[b][:, :], func=AF.Square, accum_out=statB[:, si:si + 1]),
                     reads=[("xt", b)], writes=["sqt", ("statB", si)])
                rms_rstd(statB[:, si:si + 1], float(D), r, [("statB", si)], ("statB", si))
                P.op("dve", lambda e, b=b, si=si: e.tensor_scalar_mul(out=xnb[b][:, :], in0=xt[b][:, :], scalar1=statB[:, si:si + 1]),
                     reads=[("xt", b), ("statB", si)], writes=[("xnb", b)])
                transpose_rows(xnb[b], r, 8, si, 0, [("xnb", b)], None)
            evac_T(xnT, 8, W, 0, "xnT", gain=gmix, gl=0)
            for si, sb in enumerate(subs):
                b = sb % 2
                P.dma(cst[b][:, :], job["cs"][sb * 128:(sb + 1) * 128, :], writes=[("cst", b), "cst"], dkey="cst%d" % b)
                linear(xnT, r, si, wab, 8, 512, 256, 4, ["xnT", "wab_t"])
                norm_heads(ps[0:r, 4, 0:128], r, 2, 1, True, cst[b], knb[0:r, 0:128], [("ps", 4)], "knb")
                P.op("dve", lambda e, b=b: e.tensor_copy(out=vst[b][0:r, 0:130].rearrange("p (h c) -> p h c", c=65)[:, :, 0:64],
                                                         in_=ps[0:r, 4, 128:256].rearrange("p (h c) -> p h c", c=64)),
                     reads=[("ps", 4)], writes=[("vst", b)])
                linear(xnT, r, si, wab, 8, 1792, 512, 5, ["xnT", "wab_t"])
                norm_heads(ps[0:r, 5, 0:512], r, 8, 3, False, None, knb[0:r, 128:640], [("ps", 5)], "knb")
                linear(xnT, r, si, wab, 8, 2304, 512, 4, ["xnT", "wab_t"])
                P.op("act", lambda e, b=b: e.copy(out=vst[b][0:r, 130:VW].rearrange("p (h c) -> p h c", c=129)[:, :, 0:128],
                                                  in_=ps[0:r, 4, 0:512].rearrange("p (h c) -> p h c", c=128)),
                     reads=[("ps", 4)], writes=[("vst", b)])
                vo = job["vone"][0:r, sb:sb + 1]
                P.op("pool", lambda e, b=b, vo=vo: e.tensor_copy(out=vst[b][0:r, 0:130].rearrange("p (h c) -> p h c", c=65)[:, :, 64:65],
                                                                 in_=vo.unsqueeze(1).to_broadcast([r, 2, 1])),
                     reads=[job["vkey"]], writes=[("vst", b)])
                P.op("pool", lambda e, b=b, vo=vo: e.tensor_copy(out=vst[b][0:r, 130:VW].rearrange("p (h c) -> p h c", c=129)[:, :, 128:129],
                                                                 in_=vo.unsqueeze(1).to_broadcast([r, 4, 1])),
                     reads=[job["vkey"]], writes=[("vst", b)])
                P.dma(v_s[sb * 128:(sb + 1) * 128, :], vst[b][:, :], reads=[("vst", b)], writes=[("v", sb)], dkey="vst%d" % b)
                for hh in range(10):
                    bank = 6 if hh < 2 else 7
                    c0 = (hh if hh < 2 else hh - 2) * 128
                    P.op("pe", lambda e, hh=hh, bank=bank, c0=c0: e.transpose(out=ps_bf(bank)[0:64, c0:c0 + r], in_=knb[0:r, hh * 64:(hh + 1) * 64], identity=ident[0:r, 0:r]),
                         reads=["knb", "ident"], writes=[("ps", bank)])
                evac_copy(kst[b][0:64, 0:2, :], ps_bf(6)[0:64, 0:256].rearrange("p (h c) -> p h c", c=128), [("ps", 6)], [("kst", b)])
                evac_copy(kst[b][0:64, 2:10, :], ps_bf(7)[0:64, 0:1024].rearrange("p (h c) -> p h c", c=128), [("ps", 7)], [("kst", b)])
                P.dma(kT_s[:, :, sb * 128:(sb + 1) * 128].rearrange("u d t -> d u t"), kst[b][0:64, :, :], reads=[("kst", b)], writes=[("kT", sb)], dkey="kst%d" % b)
                if own:
                    linear(xnT, rq, si, wab, 8, 0, 512, 5, ["xnT", "wab_t"])
                    norm_heads(ps[0:rq, 5, 0:512], rq, 8, 0, True, cst[b], qnb[0:rq, 0:512], [("ps", 5)], "qnb")
                    linear(xnT, rq, si, wab, 8, 1280, 512, 4, ["xnT", "wab_t"])
                    norm_heads(ps[0:rq, 4, 0:512], rq, 8, 2, False, None, qnb[0:rq, 512:1024], [("ps", 4)], "qnb")
                    linear(xnT, rq, si, wab, 8, 768, 512, 5, ["xnT", "wab_t"])
                    sigmoid_act(gst[b][0:rq, 0:512], ps[0:rq, 5, 0:512], [("ps", 5)], ("gst", b))
                    P.op("dve", lambda e, b=b: e.tensor_tensor(out=gst[b][0:rq, 0:512], in0=gst[b][0:rq, 0:512], in1=ps[0:rq, 5, 0:512], op=ALU.mult), reads=[("ps", 5), ("gst", b)], writes=[("gst", b)])
                    linear(xnT, rq, si, wab, 8, 2816, 512, 4, ["xnT", "wab_t"])
                    sigmoid_act(gst[b][0:rq, 512:1024], ps[0:rq, 4, 0:512], [("ps", 4)], ("gst", b))
                    P.op("dve", lambda e, b=b: e.tensor_tensor(out=gst[b][0:rq, 512:1024], in0=gst[b][0:rq, 512:1024], in1=ps[0:rq, 4, 0:512], op=ALU.mult), reads=[("ps", 4), ("gst", b)], writes=[("gst", b)])
                    q0 = qrow0(job, sb)
                    P.dma(gate_s[q0:q0 + rq, :], gst[b][0:rq, :], reads=[("gst", b)], writes=[("gate", sb)], dkey="gst%d" % b)
                    for hh in range(16):
                        bank = 6 if hh < 8 else 7
                        c0 = (hh % 8) * 128
                        P.op("pe", lambda e, hh=hh, bank=bank, c0=c0: e.transpose(out=ps_bf(bank)[0:64, c0:c0 + rq], in_=qnb[0:rq, hh * 64:(hh + 1) * 64], identity=ident[0:rq, 0:rq]),
                             reads=["qnb", "ident"], writes=[("ps", bank)])
                    for half in range(2):
                        evac_copy(qst[b][0:64, half * 8:half * 8 + 8, 0:rq], ps_bf(6 + half)[0:64, 0:1024].rearrange("p (h c) -> p h c", c=128)[:, :, 0:rq],
                                  [("ps", 6 + half)], [("qst", b)])
                    P.dma(qT_s[:, :, q0:q0 + rq].rearrange("u d t -> d u t"), qst[b][0:64, :, 0:rq], reads=[("qst", b)], writes=[("qT", sb)], dkey="qst%d" % b)
        P.barrier(junk1[:, 0:1])

        CG.reset(); CB.reset(); CW.reset()
        KT = CG.get([2, SK], BF16)
        VtA = CB.get([nkb, 65], BF16)
        CB.reset()
        VtB = CB.get([nkb, 129], BF16)
        QT = [CW.get([2, 2, 512], BF16) for _ in range(2)]
        PT = [CW.get([2, 512], BF16) for _ in range(3)]
        ost = [CW.get([4, 128], F32) for _ in range(2)]
        oT = [CW.get([512], F32) for _ in range(2)]
        o1 = CW.get([4, 128], F32); o2 = CW.get([4, 128], F32); ob = CW.get([4, 128], F32); sq2 = CW.get([4, 128], F32)
        P.op("pool", lambda e: e.memset(KT[64:128, :, :], 0.0), writes=["KT"])
        for qs_ in range(2):
            P.op("pool", lambda e: e.memset(QT[qs_][64:128, :, :, :], 0.0), writes=[("QT", qs_)])
        kT_keys = [("kT", sb) for sb in range(nkb)]
        v_keys = [("v", sb) for sb in range(nkb)]
        qtiles = []
        for m in range(0, nown, 4):
            qtiles.append(dict(i=m // 4, q0=m * 128, ns=min(4, nown - m), r=128, halo=False, keys=[("qT", m + t_) for t_ in range(min(4, nown - m))], subs0=m))
        if halo:
            qtiles.append(dict(i=-1, q0=nown * 128, ns=1, r=16, halo=True, keys=[("qT", nown)], subs0=nown))
        rnd = [0]; gi_ = [0]; qi_ = [0]

        def key_loop(qt, QTt, mslot, kmap, kind, h, obanks, vt, vw, round_started):
            W = (qt["ns"] - 1) * 128 + qt["r"]
            ns, r = qt["ns"], qt["r"]
            KR = 128
            groups = []
            kb = 0
            while kb < nkb:
                grp = [kb] if kb + 1 >= nkb else [kb, kb + 1]
                g = gi_[0]; gi_[0] += 1
                groups.append((grp, g % 2, g % 3))
                kb += len(grp)

            def emit_qk(grp, sslot):
                for j, k in enumerate(grp):
                    bank = sslot * 2 + j
                    kcols = slice(k * 128, (k + 1) * 128)
                    mode = "plain"; var = 0
                    if kind == "b":
                        if qt["halo"]:
                            if k < nown:
                                var = 1
                            elif k == nown:
                                mode = "halo"
                            else:
                                var = 0
                        else:
                            i4 = qt["i"] * 4
                            if k < i4:
                                var = 1
                            elif k < i4 + 4 and k < nown:
                                mode = "diag"
                            else:
                                var = 0
                    rd = ["KT", ("QT", qi_[0] % 2)]
                    if mode == "plain":
                        P.op("pe", lambda e: e.matmul(ps[:, bank, 0:W], lhsT=KT[0:KR, kmap, kcols], rhs=QTt[0:KR, mslot, var, 0:W], start=True, stop=True),
                             reads=rd, writes=[("ps", bank)])
                    elif mode == "diag":
                        dd = k - qt["i"] * 4
                        first = True
                        for s in range(ns):
                            cs_ = slice(s * 128, (s + 1) * 128)
                            if s == dd:
                                P.op("pe", lambda e: e.matmul(ps[:, bank, cs_], lhsT=KT[0:64, kmap, kcols], rhs=QTt[0:64, mslot, 0, cs_],
                                                              start=first, stop=False, skip_group_check=True),
                                     reads=rd, writes=[("ps", bank)])
                                P.op("pe", lambda e: e.matmul(ps[:, bank, cs_], lhsT=ident[:, :], rhs=dbias[:, h * 128:(h + 1) * 128], start=False, stop=True, skip_group_check=True),
                                     reads=["ident", "dbias"], writes=[("ps", bank)])
                            else:
                                v_ = 0 if s < dd else 1
                                P.op("pe", lambda e: e.matmul(ps[:, bank, cs_], lhsT=KT[0:128, kmap, kcols], rhs=QTt[0:128, mslot, v_, cs_],
                                                              start=first, stop=True, skip_group_check=True),
                                     reads=rd, writes=[("ps", bank)])
                            first = False
                    else:
                        P.op("pe", lambda e: e.matmul(ps[:, bank, 0:W], lhsT=KT[0:64, kmap, kcols], rhs=QTt[0:64, mslot, 0, 0:W], start=True, stop=False, skip_group_check=True),
                             reads=rd, writes=[("ps", bank)])
                        for hl in range(2):
                            c0 = (hl * 4 + h) * 16
                            P.op("pe", lambda e: e.matmul(ps[:, bank, 0:W], lhsT=ident[:, :], rhs=bhalo[:, c0:c0 + 16], start=False, stop=(hl == 1), skip_group_check=True),
                                 reads=["ident", "bhalo"], writes=[("ps", bank)])

            if PIPE:
                emit_qk(groups[0][0], groups[0][1])
            for gidx, (grp, sslot, pslot) in enumerate(groups):
                if not PIPE:
                    emit_qk(grp, sslot)
                elif gidx + 1 < len(groups):
                    emit_qk(groups[gidx + 1][0], groups[gidx + 1][1])
                ng = len(grp)
                P.op("act", lambda e: e.activation(out=PT[pslot][:, 0:ng, 0:W], in_=ps[:, sslot * 2:sslot * 2 + ng, 0:W], func=AF.Exp, scale=0.125),
                     reads=[("ps", sslot * 2 + j) for j in range(ng)], writes=[("PT", pslot)])
                for j, k in enumerate(grp):
                    last = (k == nkb - 1)
                    if kind == "a":
                        obank = obanks[0]
                        st = obank not in round_started
                        round_started.add(obank)
                        P.op("pe", lambda e: e.matmul(ps[0:65, obank, 0:W], lhsT=vt[:, k, 0:65], rhs=PT[pslot][:, j, 0:W], start=st, stop=last, skip_group_check=True),
                             reads=[("PT", pslot), "Vt"], writes=[("ps", obank)])
                        continue
                    for s in range(ns):
                        obank = obanks[s // 2]; oc = (s % 2) * 129
                        st = obank not in round_started
                        round_started.add(obank)
                        P.op("pe", lambda e: e.matmul(ps[0:r, obank, oc:oc + vw], lhsT=PT[pslot][:, j, s * 128:s * 128 + r], rhs=vt[:, k, 0:vw], start=st, stop=last, skip_group_check=True),
                             reads=[("PT", pslot), "Vt"], writes=[("ps", obank)])

        for kvh in range(2):
            P.dma(KT[0:64, 0, :], kT_s[kvh, :, 0:SK], reads=kT_keys, writes=["KT"], dkey="KTa")
            for c0 in range(0, nkb, 16):
                c1 = min(nkb, c0 + 16)
                P.dma(VtA[:, c0:c1, :], v_s[c0 * 128:c1 * 128, kvh * 65:(kvh + 1) * 65].rearrange("(kb p) c -> p kb c", p=128), reads=v_keys[c0:c1], writes=["Vt"], dkey="Vta")
            for g4 in range(4):
                hq = kvh * 4 + g4
                for qt in qtiles:
                    qi_[0] += 1
                    qs = qi_[0] % 2
                    W = (qt["ns"] - 1) * 128 + qt["r"]
                    P.dma(QT[qs][0:64, 0, 0, 0:W], qT_s[hq, :, qt["q0"]:qt["q0"] + W], reads=qt["keys"], writes=[("QT", qs)], dkey="QT%d" % qs)
                    rnd[0] += 1
                    obank = 4 + rnd[0] % 2
                    key_loop(qt, QT[qs], 0, 0, "a", 0, [obank], VtA, 65, set())
                    ns, r = qt["ns"], qt["r"]
                    os_ = ost[rnd[0] % 2]
                    oTt = oT[rnd[0] % 2]
                    evac_copy(oTt[0:65, 0:W], ps[0:65, obank, 0:W], [("ps", obank)], [("oT", rnd[0] % 2)])
                    obank = 6 + rnd[0] % 2
                    for s_ in range(ns):
                        P.op("pe", lambda e: e.transpose(out=ps[0:r, obank, s_ * 65:(s_ + 1) * 65], in_=oTt[0:65, s_ * 128:s_ * 128 + r], identity=identf[0:65, 0:65]),
                             reads=[("oT", rnd[0] % 2), "identf"], writes=[("ps", obank)])
                    pso = ps[0:r, obank, 0:ns * 65].rearrange("p (s c) -> p s c", c=65)
                    P.op("dve", lambda e, pso=pso, r=r, ns=ns: e.reciprocal(out=statC[0:r, 0:ns].unsqueeze(2), in_=pso[:, :, 64:65]), reads=[("ps", obank)], writes=["statC"])
                    P.op("dve", lambda e, pso=pso, r=r, ns=ns, os_=os_: e.tensor_tensor(out=os_[0:r, 0:ns, 0:64], in0=pso[:, :, 0:64], in1=statC[0:r, 0:ns].unsqueeze(2).to_broadcast([r, ns, 64]), op=ALU.mult),
                         reads=[("ps", obank), "statC"], writes=[("ost", rnd[0] % 2)])
                    if qt["halo"]:
                        dst = o_s[qt["q0"]:qt["q0"] + r, hq * 64:(hq + 1) * 64]
                        srcv = os_[0:r, 0, 0:64]
                    else:
                        dst = o_s[qt["q0"]:qt["q0"] + ns * 128, hq * 64:(hq + 1) * 64].rearrange("(s p) c -> p s c", p=128)
                        srcv = os_[:, 0:ns, 0:64]
                    P.dma(dst, srcv, reads=[("ost", rnd[0] % 2)], writes=[("o", qt["subs0"])], dkey="ost%d" % (rnd[0] % 2))
        for h in range(4):
            for m in range(2):
                P.dma(KT[0:64, m, :], kT_s[2 + h * 2 + m, :, 0:SK], reads=kT_keys, writes=["KT"], dkey="KTb%d" % m)
                P.dma(KT[64:68, m, :], job["kaug"][:, 0:SK], reads=[], writes=["KT"], dkey="KTx%d" % m)
            for c0 in range(0, nkb, 16):
                c1 = min(nkb, c0 + 16)
                P.dma(VtB[:, c0:c1, :], v_s[c0 * 128:c1 * 128, 130 + h * 129:130 + (h + 1) * 129].rearrange("(kb p) c -> p kb c", p=128), reads=v_keys[c0:c1], writes=["Vt"], dkey="Vtb")
            for qt in qtiles:
                qi_[0] += 1
                qs = qi_[0] % 2
                ns, r = qt["ns"], qt["r"]
                W = (ns - 1) * 128 + r
                for m in range(2):
                    for var in range(2):
                        P.dma(QT[qs][0:64, m, var, 0:W], qT_s[8 + h * 2 + m, :, qt["q0"]:qt["q0"] + W], reads=qt["keys"], writes=[("QT", qs)], dkey="QTb%d%d%d" % (qs, m, var))
                        P.dma(QT[qs][64:68, m, var, 0:W], job["qaug"][h, var, :, qt["q0"]:qt["q0"] + W], reads=[], writes=[("QT", qs)], dkey="QTx%d%d%d" % (qs, m, var))
                for m in range(2):
                    key_loop(qt, QT[qs], m, m, "b", h, [4 + 2 * m, 5 + 2 * m], VtB, 129, set())
                rnd[0] += 1

                def pso(m, half):
                    return ps[0:r, 4 + 2 * m + half, 0:258].rearrange("p (s c) -> p s c", c=129)
                nhalf = (ns + 1) // 2
                for m in range(2):
                    dstt = o1 if m == 0 else o2
                    for half in range(nhalf):
                        nn = min(2, ns - half * 2)
                        pv = pso(m, half)
                        P.op("dve", lambda e, pv=pv, nn=nn, m=m, half=half: e.reciprocal(out=statC[0:r, m * 4 + half * 2:m * 4 + half * 2 + nn].unsqueeze(2), in_=pv[:, 0:nn, 128:129]),
                             reads=[("ps", 4 + 2 * m + half)], writes=[("statC", m, half)])
                        P.op("dve", lambda e, pv=pv, nn=nn, m=m, half=half, dstt=dstt: e.tensor_tensor(
                            out=dstt[0:r, half * 2:half * 2 + nn, :], in0=pv[:, 0:nn, 0:128],
                            in1=statC[0:r, m * 4 + half * 2:m * 4 + half * 2 + nn].unsqueeze(2).to_broadcast([r, nn, 128]), op=ALU.mult),
                            reads=[("ps", 4 + 2 * m + half), ("statC", m, half)], writes=[("o12", m)])
                P.op("dve", lambda e, r=r, ns=ns: e.scalar_tensor_tensor(out=ob[0:r, 0:ns, :], in0=o2[0:r, 0:ns, :], scalar=neglam[0:r, 0:1], in1=o1[0:r, 0:ns, :], op0=ALU.mult, op1=ALU.add),
                     reads=[("o12", 0), ("o12", 1), "neglam"], writes=["ob"])
                P.op("act", lambda e, r=r, ns=ns: e.activation(out=sq2[0:r, 0:ns, :], in_=ob[0:r, 0:ns, :], func=AF.Square), reads=["ob"], writes=["sq2"])
                P.op("dve", lambda e, r=r, ns=ns: e.tensor_reduce(out=statA[0:r, 0:ns], in_=sq2[0:r, 0:ns, :], axis=AX.X, op=ALU.add), reads=["sq2"], writes=["statA"])
                rms_rstd(statA[0:r, 0:ns], 128.0, r, ["statA"], "statA")
                P.op("dve", lambda e, r=r, ns=ns: e.tensor_tensor(out=sq2[0:r, 0:ns, :], in0=ob[0:r, 0:ns, :], in1=statA[0:r, 0:ns].unsqueeze(2).to_broadcast([r, ns, 128]), op=ALU.mult),
                     reads=["ob", "statA"], writes=["sq2"])
                os_ = ost[rnd[0] % 2]
                P.op("pool", lambda e, r=r, ns=ns, os_=os_: e.tensor_tensor(out=os_[0:r, 0:ns, :], in0=sq2[0:r, 0:ns, :], in1=sg8[0:r, :].unsqueeze(1).to_broadcast([r, ns, 128]), op=ALU.mult),
                     reads=["sq2", "sg8"], writes=[("ost", rnd[0] % 2)])
                if qt["halo"]:
                    dst = o_s[qt["q0"]:qt["q0"] + r, 512 + h * 128:512 + (h + 1) * 128]
                    srcv = os_[0:r, 0, :]
                else:
                    dst = o_s[qt["q0"]:qt["q0"] + ns * 128, 512 + h * 128:512 + (h + 1) * 128].rearrange("(s p) c -> p s c", p=128)
                    srcv = os_[:, 0:ns, :]
                P.dma(dst, srcv, reads=[("ost", rnd[0] % 2)], writes=[("o", qt["subs0"])], dkey="ost%d" % (rnd[0] % 2))
        P.barrier(junk1[:, 0:1])

        def ple_block(hts, r, ns, W, gT, pT, wgt, wpt, layer, p_ap, prow0, ptile, pbf, sgt, tmp, hnb, gkey):
            for si in range(ns):
                ht = hts[si]
                P.op("act", lambda e, ht=ht, si=si: e.activation(out=tmp[0:r, :], in_=ht[0:r, :], func=AF.Square, accum_out=statB[0:r, si:si + 1]),
                     reads=[("ht", si)], writes=["tmp", ("statB", si)])
                rms_rstd(statB[0:r, si:si + 1], float(D), r, [("statB", si)], ("statB", si))
                hb = hnb[si % 2]
                P.op("dve", lambda e, ht=ht, si=si, hb=hb: e.tensor_scalar_mul(out=hb[0:r, :], in0=ht[0:r, :], scalar1=statB[0:r, si:si + 1]),
                     reads=[("ht", si), ("statB", si)], writes=[("hnb", si % 2)])
                transpose_rows(hb, r, 8, si, 0, [("hnb", si % 2)], None)
                P.dma(ptile[si % 2][0:r, :], p_ap[prow0 + si * 128:prow0 + si * 128 + r, :], writes=[("ptile", si % 2)], dkey="pt%d" % (si % 2))
                P.op("pool", lambda e, si=si: e.tensor_copy(out=pbf[si % 2][0:r, :], in_=ptile[si % 2][0:r, :]), reads=[("ptile", si % 2)], writes=[("pbf", si % 2)])
                for kc in range(2):
                    P.op("pe", lambda e, si=si, kc=kc: e.transpose(out=ps_bf(6)[:, kc * 512 + si * 128:kc * 512 + si * 128 + r], in_=pbf[si % 2][0:r, kc * 128:(kc + 1) * 128], identity=ident[0:r, 0:r]),
                         reads=[("pbf", si % 2), "ident"], writes=[("ps", 6)])
            evac_T(gT, 8, W, 0, gkey, gain=gple, gl=layer)
            for kc in range(2):
                evac_copy(pT[:, kc, 0:W], ps_bf(6)[:, kc * 512:kc * 512 + W], [("ps", 6)], ["pT"])
            for si in range(ns):
                ht = hts[si]
                for c in range(2):
                    linear(gT, r, si, wgt, 8, c * 512, 512, 4 + c, [gkey, "wgt"])
                    sigmoid_act(sgt[0:r, c * 512:(c + 1) * 512], ps[0:r, 4 + c, :], [("ps", 4 + c)], ("sgt", c))
                for c in range(2):
                    linear(pT, r, si, wpt, 2, c * 512, 512, 6 + c, ["pT", "wpt"])
                    P.op("dve", lambda e, c=c: e.tensor_tensor(out=tmp[0:r, c * 512:(c + 1) * 512], in0=ps[0:r, 6 + c, :], in1=sgt[0:r, c * 512:(c + 1) * 512], op=ALU.mult),
                         reads=[("ps", 6 + c), ("sgt", c)], writes=["tmp"])
                    P.op("pool", lambda e, c=c, ht=ht: e.tensor_tensor(out=ht[0:r, c * 512:(c + 1) * 512], in0=ht[0:r, c * 512:(c + 1) * 512], in1=tmp[0:r, c * 512:(c + 1) * 512], op=ALU.add),
                         reads=["tmp", ("ht", si)], writes=[("ht", si)])

        CG.reset(); CB.reset(); CW.reset()
        woab = load_w(CB, woab_s, D, D, "woab_t", "woab")
        wgt0 = load_w(CB, wg_s[0:D, :], D, D, "wgt", "wg")
        wpt0 = load_w(CB, wpp_s[0:PLE, :], PLE, D, "wpt", "wpp")
        hts = [CW.get([D], F32) for _ in range(4)]
        ot = [CW.get([D], F32) for _ in range(2)]
        gt_ = [CW.get([D], F32) for _ in range(2)]
        ybf = [CW.get([D], BF16) for _ in range(2)]
        yT = CW.get([8, 512], BF16)
        gT = yT
        pT = CW.get([2, 512], BF16)
        ptile = [CW.get([PLE], F32) for _ in range(2)]
        pbf = [CW.get([PLE], BF16) for _ in range(2)]
        sgt = CW.get([D], F32)
        tmp = CW.get([D], F32)
        hnb = [CW.get([D], BF16) for _ in range(2)]
        for qt in qtiles:
            ns, r = qt["ns"], qt["r"]
            W = (ns - 1) * 128 + r
            for si in range(ns):
                b = si % 2
                row = qt["q0"] + si * 128
                P.dma(ot[b][0:r, :], o_s[row:row + r, :], reads=[("o", qt["subs0"])], writes=[("ot", b)], dkey="ot%d" % b)
                P.dma(gt_[b][0:r, :], gate_s[row:row + r, :], reads=[("gate", qt["subs0"] + si)], writes=[("gt", b)], dkey="gt%d" % b)
                P.op("dve", lambda e, b=b: e.tensor_tensor(out=ybf[b][0:r, :], in0=ot[b][0:r, :], in1=gt_[b][0:r, :], op=ALU.mult), reads=[("ot", b), ("gt", b)], writes=[("ybf", b)])
                transpose_rows(ybf[b], r, 8, si, 0, [("ybf", b)], None)
            evac_T(yT, 8, W, 0, "yT")
            for si in range(ns):
                b = si % 2
                xrow = (qt["subs0"] + si) * 128
                P.dma(hts[si][0:r, :], x[xrow:xrow + r, :], writes=[("ht", si)], dkey="xq%d" % si)
                for c in range(2):
                    linear(yT, r, si, woab, 8, c * 512, 512, 4 + c, ["yT", "woab_t"])
                    P.op("dve", lambda e, c=c, si=si: e.tensor_tensor(out=hts[si][0:r, c * 512:(c + 1) * 512], in0=ps[0:r, 4 + c, :], in1=hts[si][0:r, c * 512:(c + 1) * 512], op=ALU.add),
                         reads=[("ps", 4 + c), ("ht", si)], writes=[("ht", si)])
            ple_block(hts, r, ns, W, gT, pT, wgt0, wpt0, 0, job["p0"], qt["q0"], ptile, pbf, sgt, tmp, hnb, "yT")
            for si in range(ns):
                row = qt["q0"] + si * 128
                P.dma(h1_s[row:row + r, :], hts[si][0:r, :], reads=[("ht", si)], writes=[("h1", qt["subs0"] + si)], dkey="h1st%d" % si)
        P.barrier(junk1[:, 0:1])

        CG.reset(); CB.reset(); CW.reset()
        wic = load_w(CG, wic_s, D, 2 * D, "wic_t", "wic")
        wgr = load_w(CG, wgrp_s, D, 256, "wgr_t", "wgrp")
        woc = load_w(CG, woc_s, D, D, "woc_t", "woc")
        wgt1 = load_w(CB, wg_s[D:2 * D, :], D, D, "wgt", "wg")
        wpt1 = load_w(CB, wpp_s[PLE:2 * PLE, :], PLE, D, "wpt", "wpp")
        NU = 5
        ubf = [CG.get([D], BF16) for _ in range(NU)]
        uh = CG.get([D], BF16)
        hT = CW.get([8, 256], BF16)
        sgb = [CB.get([D], BF16) for _ in range(4)]
        h1t = [CB.get([D], F32) for _ in range(2)]
        hnb1 = [CW.get([D], BF16) for _ in range(2)]
        pooled = [CW.get([D], BF16) for _ in range(2)]
        pooledT = CW.get([8, 256], BF16)
        y2 = [CW.get([D], BF16) for _ in range(2)]
        y2T = CW.get([8, 256], BF16)
        h2 = [CW.get([D], F32) for _ in range(2)]
        tmpc = CW.get([D], F32)
        sgt1 = CW.get([D], F32)
        tmp1 = CW.get([D], F32)
        gT1 = CW.get([8, 256], BF16)
        pT1 = CW.get([2, 256], BF16)
        ptile1 = [CW.get([PLE], F32) for _ in range(2)]
        pbf1 = [CW.get([PLE], BF16) for _ in range(2)]
        hnb2 = [CW.get([D], BF16) for _ in range(2)]
        sgf = CW.get([D], F32)

        def evac_T2(dstT, n_kc, W, key, gain=None, gl=0):
            evac_T(dstT, n_kc, W, 0, key, gain, gl)

        def stageA(subs_, r, want_g, dst_u):
            ns = len(subs_)
            W = (ns - 1) * 128 + r
            for si, sb in enumerate(subs_):
                b = si % 2
                row = qrow0(job, sb)
                P.dma(h1t[b][0:r, :], h1_s[row:row + r, :], reads=[("h1", sb)], writes=[("h1t", b)], dkey="h1t%d" % b)
                P.op("act", lambda e, b=b, si=si: e.activation(out=tmp1[0:r, :], in_=h1t[b][0:r, :], func=AF.Square, accum_out=statB[0:r, 8 + si:9 + si]),
                     reads=[("h1t", b)], writes=["tmp1", ("statB", 8 + si)])
                rms_rstd(statB[0:r, 8 + si:9 + si], float(D), r, [("statB", 8 + si)], ("statB", 8 + si))
                P.op("dve", lambda e, b=b, si=si: e.tensor_scalar_mul(out=hnb1[b][0:r, :], in0=h1t[b][0:r, :], scalar1=statB[0:r, 8 + si:9 + si]),
                     reads=[("h1t", b), ("statB", 8 + si)], writes=[("hnb1", b)])
                transpose_rows(hnb1[b], r, 8, si, 0, [("hnb1", b)], None)
            evac_T2(hT, 8, W, "hT", gain=gmix, gl=1)
            for si, sb in enumerate(subs_):
                ut, ukey = dst_u(sb)
                for c in range(2):
                    linear(hT, r, si, wic, 8, c * 512, 512, 4 + c, ["hT", "wic_t"])
                    evac_copy(ut[0:r, c * 512:(c + 1) * 512], ps[0:r, 4 + c, :], [("ps", 4 + c)], [ukey])
                if want_g:
                    for c in range(2):
                        linear(hT, r, si, wic, 8, D + c * 512, 512, 6 + c, ["hT", "wic_t"])
                        sigmoid_act(sgf[0:r, c * 512:(c + 1) * 512], ps[0:r, 6 + c, :], [("ps", 6 + c)], ("sgf", c))
                        P.op("dve", lambda e, c=c, sb=sb: e.tensor_tensor(out=sgb[sb % 4][0:r, c * 512:(c + 1) * 512], in0=sgf[0:r, c * 512:(c + 1) * 512], in1=ps[0:r, 6 + c, :], op=ALU.mult),
                             reads=[("ps", 6 + c), ("sgf", c)], writes=[("sgb", sb % 4)])

        l1macros = [list(range(m, min(m + 2, nown))) for m in range(0, nown, 2)]
        if halo:
            stageA([nown], 16, False, lambda sb: (uh, "uh"))
        stageA(l1macros[0], 128, True, lambda sb: (ubf[sb % NU], ("ubf", sb % NU)))
        bM = job["bandM"].rearrange("p (v g t) -> p v g t", v=3, g=4)
        bPN = bandPN.rearrange("p (v g t) -> p v g t", v=2, g=4)
        bH = bandH.rearrange("p (v g t) -> p v g t", v=2, g=4)
        ic = job["icnt"].rearrange("p (s g) -> p s g", g=4)
        for mi, subs_ in enumerate(l1macros):
            if mi + 1 < len(l1macros):
                stageA(l1macros[mi + 1], 128, True, lambda sb: (ubf[sb % NU], ("ubf", sb % NU)))
            ns = len(subs_)
            W = ns * 128
            r = 128
            for si, sb in enumerate(subs_):
                b = si % 2
                var = 0 if sb == 0 else (2 if sb == nown - 1 else 1)
                for half in range(2):
                    bank = 4 + half
                    for gg in range(2):
                        g = half * 2 + gg
                        cols = slice(g * 256, (g + 1) * 256)
                        mm = []
                        mm.append((bM[:, var, g, :], ubf[sb % NU][:, cols], [job["bkey"], ("ubf", sb % NU)]))
                        if sb > 0:
                            mm.append((bPN[:, 0, g, :], ubf[(sb - 1) % NU][:, cols], ["bandPN", ("ubf", (sb - 1) % NU)]))
                        elif halo:
                            mm.append((bH[0:16, 0, g, :], uh[0:16, cols], ["bandH", "uh"]))
                        if sb < nown - 1:
                            mm.append((bPN[:, 1, g, :], ubf[(sb + 1) % NU][:, cols], ["bandPN", ("ubf", (sb + 1) % NU)]))
                        elif halo:
                            mm.append((bH[0:16, 1, g, :], uh[0:16, cols], ["bandH", "uh"]))
                        for k_, (lt, rh, rd) in enumerate(mm):
                            P.op("pe", lambda e, lt=lt, rh=rh, bank=bank, gg=gg, k_=k_, nmm=len(mm): e.matmul(ps[:, bank, gg * 256:(gg + 1) * 256], lhsT=lt, rhs=rh,
                                                                                                             start=(k_ == 0 and gg == 0), stop=(k_ == nmm - 1), skip_group_check=True),
                                 reads=rd, writes=[("ps", bank)])
                    P.op("dve", lambda e, bank=bank, half=half, b=b, sb=sb: e.tensor_tensor(
                        out=pooled[b][:, half * 512:(half + 1) * 512].rearrange("p (g c) -> p g c", g=2), in0=ps[:, bank, :].rearrange("p (g c) -> p g c", g=2),
                        in1=ic[:, sb, half * 2:half * 2 + 2].unsqueeze(2).to_broadcast([128, 2, 256]), op=ALU.mult),
                        reads=[("ps", bank), job["ikey"]], writes=[("pooled", b)])
                transpose_rows(pooled[b], r, 8, si, 0, [("pooled", b)], None)
            evac_T2(pooledT, 8, W, "pooledT")
            for si, sb in enumerate(subs_):
                b = si % 2
                for half in range(2):
                    bank = 4 + half
                    for gg in range(2):
                        g = half * 2 + gg
                        for kc in range(2):
                            P.op("pe", lambda e, bank=bank, gg=gg, g=g, kc=kc, si=si: e.matmul(ps[:, bank, gg * 256:(gg + 1) * 256], lhsT=pooledT[:, g * 2 + kc, si * 128:(si + 1) * 128],
                                                                                               rhs=wgr[:, g * 2 + kc, :], start=(kc == 0 and gg == 0), stop=(kc == 1), skip_group_check=True),
                                 reads=["pooledT", "wgr_t"], writes=[("ps", bank)])
                    P.op("dve", lambda e, bank=bank, half=half: e.tensor_tensor(out=tmpc[:, half * 512:(half + 1) * 512], in0=ps[:, bank, :], in1=cscale[:, half * 512:(half + 1) * 512], op=ALU.mult),
                         reads=[("ps", bank), "cscale"], writes=[("tmpc", half)])
                    P.op("pool", lambda e, half=half, b=b, sb=sb: e.tensor_tensor(out=y2[b][:, half * 512:(half + 1) * 512], in0=tmpc[:, half * 512:(half + 1) * 512],
                                                                                  in1=sgb[sb % 4][:, half * 512:(half + 1) * 512], op=ALU.mult),
                         reads=[("tmpc", half), ("sgb", sb % 4)], writes=[("y2", b)])
                transpose_rows(y2[b], r, 8, si, 0, [("y2", b)], None)
            evac_T2(y2T, 8, W, "y2T")
            for si, sb in enumerate(subs_):
                b = si % 2
                row = sb * 128
                P.dma(h2[b][:, :], h1_s[row:row + 128, :], reads=[("h1", sb)], writes=[("ht", si)], dkey="h2l%d" % b)
                for c in range(2):
                    linear(y2T, r, si, woc, 8, c * 512, 512, 4 + c, ["y2T", "woc_t"])
                    P.op("dve", lambda e, c=c, b=b: e.tensor_tensor(out=h2[b][:, c * 512:(c + 1) * 512], in0=ps[:, 4 + c, :], in1=h2[b][:, c * 512:(c + 1) * 512], op=ALU.add),
                         reads=[("ps", 4 + c), ("ht", si)], writes=[("ht", si)])
            ple_block([h2[si % 2] for si in range(ns)], r, ns, W, gT1, pT1, wgt1, wpt1, 1, job["p1"], subs_[0] * 128, ptile1, pbf1, sgt1, tmp1, hnb2, "gT1")
            for si, sb in enumerate(subs_):
                P.dma(job["out"][sb * 128:(sb + 1) * 128, :], h2[si % 2][:, :], reads=[("ht", si)], writes=[("out", jn, sb)], dkey="out%d" % (si % 2))
        P.barrier(junk1[:, 0:1])

    P.emit()
    es.close()
    return nc, len(P.ops)


def rope_cs(pos):
    pos = np.asarray(pos, dtype=np.int64)
    row = (pos // GRID_W).astype(np.float32)
    col = (pos % GRID_W).astype(np.float32)
    half = HD // 2
    inv = (np.float32(THETA) ** (-np.arange(0, half, 2, dtype=np.float32) / np.float32(half))).astype(np.float32)
    ar = row[:, None] * inv
    ac = col[:, None] * inv
    ang = np.concatenate([ar, ar, ac, ac], axis=-1).astype(np.float32)
    cos = np.cos(ang).astype(np.float32)
    sin = np.sin(ang).astype(np.float32)
    sign = np.concatenate([-np.ones(16), np.ones(16), -np.ones(16), np.ones(16)]).astype(np.float32)
    return np.concatenate([cos, sin * sign[None, :]], axis=-1).astype(np.float32)


def kaug_rows(pos, sigma):
    pos = np.asarray(pos, dtype=np.int64)
    hi = (pos // 128) * 128
    lo = pos % 128
    s = np.asarray(sigma, dtype=np.float32)
    return np.stack([s, s, s * hi, s * lo], 0).astype(bf16)


def qaug_rows(pos, sigma):
    pos = np.asarray(pos, dtype=np.int64)
    hi = ((pos // 128) * 128).astype(np.float32)
    lo = (pos % 128).astype(np.float32)
    s = np.asarray(sigma, dtype=np.float32)
    out = []
    for h in range(4):
        s8 = np.float32(SLOPES[h] * 8.0)
        out.append(np.stack([s * s8 * hi, s * s8 * lo, -s * s8 * np.ones_like(hi), -s * s8 * np.ones_like(hi)], 0))
    return np.stack(out, 0).astype(bf16)


def band_tables(S, first, last):
    M = np.zeros((4, 128, 128), np.float32)
    for g, w in enumerate(POOL_W):
        for t in range(128):
            lo = t - w // 2
            hi = t + w // 2
            if first:
                lo = max(lo, 0)
            if last:
                hi = min(hi, 128)
            cnt = hi - lo
            for tp in range(max(lo, 0), min(hi, 128)):
                M[g, tp, t] = 1.0
            M[g, t, t] += -cnt
    return M


def band_pn():
    Pm = np.zeros((4, 128, 128), np.float32)
    Nm = np.zeros((4, 128, 128), np.float32)
    for g, w in enumerate(POOL_W):
        for t in range(128):
            for tp in range(128):
                if tp - 128 >= t - w // 2:
                    Pm[g, tp, t] = 1.0
                if tp + 128 < t + w // 2:
                    Nm[g, tp, t] = 1.0
    return Pm, Nm


def icnt_table(nsub, first_edge, last_edge):
    S = nsub * 128
    out = np.zeros((128, nsub, 4), np.float32)
    for g, w in enumerate(POOL_W):
        t = np.arange(S)
        lo = t - w // 2
        hi = t + w // 2
        if first_edge:
            lo = np.maximum(lo, 0)
        if last_edge:
            hi = np.minimum(hi, S)
        cnt = (hi - lo).astype(np.float32)
        out[:, :, g] = (1.0 / cnt).reshape(nsub, 128).T
    return out


def pack_pmajor(a):
    n = a.shape[0]
    return np.ascontiguousarray(np.transpose(a, (1, 0, 2)).reshape(128, -1))


def host_prepare(cfg, inp):
    CH, SP, NCH, SS, NSEQ, NOWN = cfg.CH, cfg.SP, cfg.NCH, cfg.SS, cfg.NSEQ, cfg.NOWN
    f32 = lambda a: np.ascontiguousarray(np.asarray(a, dtype=np.float32))
    x_prompt = f32(inp["x_prompt"]); x_sample = f32(inp["x_sample"])
    p_prompt = f32(inp["p_prompt"]); p_sample = f32(inp["p_sample"])
    shared = {}
    shared["w_in_ab"] = f32(inp["w_in_ab"][0]); shared["w_out_ab"] = f32(inp["w_out_ab"][0])
    shared["w_gate"] = f32(inp["w_ple_gate"]).reshape(2 * D, D); shared["w_proj"] = f32(inp["w_ple_proj"]).reshape(2 * PLE, D)
    shared["w_in_c"] = f32(inp["w_in_c"][0]); shared["w_grp"] = f32(inp["w_grp_c"][0]).reshape(D, 256); shared["w_out_c"] = f32(inp["w_out_c"][0])
    shared["gmix"] = np.ascontiguousarray(f32(inp["norm_mix"]).reshape(2, 8, 128).transpose(2, 0, 1).reshape(128, 16))
    shared["gple"] = np.ascontiguousarray(f32(inp["norm_ple"]).reshape(2, 8, 128).transpose(2, 0, 1).reshape(128, 16))
    shared["hgain"] = np.concatenate([f32(inp[k][0]) for k in ("qn_a", "kn_a", "qn_b", "kn_b")])[None, :]
    shared["lamv"] = np.concatenate([f32(inp[k][0]) for k in ("lam_q1", "lam_k1", "lam_q2", "lam_k2")])[None, :]
    shared["subln"] = f32(inp["subln_b"][0])[None, :]; shared["cscale"] = f32(inp["scale_c"][0])[None, :]
    spos = np.arange(SS)
    shared["cs_s"] = rope_cs(spos)
    shared["kaug_s"] = kaug_rows(spos, np.ones(SS))
    shared["qaug_s"] = np.ascontiguousarray(np.stack([qaug_rows(spos, np.ones(SS)), qaug_rows(spos, -np.ones(SS))], 1))
    kk = np.arange(128)
    db = np.stack([-(SLOPES[h] * 8.0) * np.abs(kk[None, :] - kk[:, None]) for h in range(4)], 1)
    shared["dbias"] = np.ascontiguousarray(db.reshape(128, 512)).astype(bf16)
    shared["vone_s"] = np.ones((128, cfg.NKBS), np.float32)
    shared["ident"] = np.eye(128).astype(bf16)
    Pm, Nm = band_pn()
    shared["bandPN"] = pack_pmajor(np.concatenate([Pm, Nm], 0)).astype(bf16)
    bs = np.concatenate([band_tables(SS, True, False), band_tables(SS, False, False), band_tables(SS, False, True)], 0)
    shared["bandM_s"] = pack_pmajor(bs).astype(bf16)
    shared["icnt_s"] = np.ascontiguousarray(icnt_table(cfg.NKBS, True, True).reshape(128, -1))
    in_maps = []
    for c in range(NCORES):
        b, j = c // NCH, c % NCH
        c0 = j * CH
        m = dict(shared)
        kpos = -np.ones(cfg.NSLOT, np.int64)
        kpos[0:CH] = c0 + np.arange(CH)
        hb = NOWN * 128
        hpos = -np.ones(16, np.int64)
        if j > 0:
            hpos[0:8] = c0 - 8 + np.arange(8)
        if j < NCH - 1:
            hpos[8:16] = c0 + CH + np.arange(8)
        kpos[hb:hb + 16] = hpos
        used = np.zeros(SP, bool)
        used[kpos[kpos >= 0]] = True
        far = np.nonzero(~used)[0]
        fb = (NOWN + 1) * 128
        assert len(far) <= cfg.NFAR * 128
        kpos[fb:fb + len(far)] = far
        real = kpos >= 0
        kp0 = np.where(real, kpos, 0)
        xp = np.zeros((cfg.NSLOT, D), np.float32)
        xp[real] = x_prompt[b, kpos[real]]
        m["xp"] = xp
        pp0 = np.zeros((cfg.SQP, PLE), np.float32)
        pp0[0:CH] = p_prompt[0, b, c0:c0 + CH]
        hr = hpos >= 0
        pp0[CH:CH + 16][hr] = p_prompt[0, b, hpos[hr]]
        m["pp0"] = pp0
        m["pp1"] = np.ascontiguousarray(p_prompt[1, b, c0:c0 + CH])
        m["xs"] = np.ascontiguousarray(x_sample[c * NSEQ:(c + 1) * NSEQ])
        m["psm"] = np.ascontiguousarray(p_sample[:, c * NSEQ:(c + 1) * NSEQ])
        m["cs_p"] = rope_cs(kp0)
        sigma = np.where(kp0 < c0, -1.0, 1.0)
        sigma[0:CH] = 1.0
        sigma[~real] = 1.0
        kpa = np.where(real, kpos, 2 * SP)
        m["kaug_p"] = kaug_rows(kpa, sigma)
        hq = hpos.copy()
        hq[0:8] = np.where(hpos[0:8] >= 0, hpos[0:8], 0)
        hq[8:16] = np.where(hpos[8:16] >= 0, hpos[8:16], SP)
        qpos = np.concatenate([c0 + np.arange(CH), hq])
        s0 = np.ones(cfg.SQP); s1 = -np.ones(cfg.SQP)
        s1[CH:CH + 8] = 1.0
        s1[CH + 8:CH + 16] = -1.0
        m["qaug_p"] = np.ascontiguousarray(np.stack([qaug_rows(qpos, s0), qaug_rows(qpos, s1)], 1))
        bh = np.zeros((128, 2, 4, 16), np.float32)
        for s in range(16):
            if hpos[s] < 0:
                continue
            for q in range(16):
                if hpos[q] < 0:
                    continue
                dist = abs(int(hpos[q]) - int(hpos[s]))
                for h in range(4):
                    s8 = SLOPES[h] * 8.0
                    bh[s, 0, h, q] = -s8 * ((dist // 128) * 128)
                    bh[s, 1, h, q] = -s8 * (dist % 128)
        m["bhalo"] = np.ascontiguousarray(bh.reshape(128, 128)).astype(bf16)
        m["vone_p"] = np.ascontiguousarray(real.astype(np.float32).reshape(cfg.NKBP, 128).T)
        first_edge = (j == 0); last_edge = (j == NCH - 1)
        bp = np.concatenate([band_tables(CH, first_edge, False), band_tables(CH, False, False), band_tables(CH, False, last_edge)], 0)
        m["bandM_p"] = pack_pmajor(bp).astype(bf16)
        HPm = np.zeros((4, 16, 128), np.float32); HNm = np.zeros((4, 16, 128), np.float32)
        if j > 0:
            HPm[:, 0:8, :] = Pm[:, 120:128, :]
        if j < NCH - 1:
            HNm[:, 8:16, :] = Nm[:, 0:8, :]
        m["bandH"] = np.ascontiguousarray(np.transpose(np.concatenate([HPm, HNm], 0), (1, 0, 2)).reshape(16, -1)).astype(bf16)
        m["icnt_p"] = np.ascontiguousarray(icnt_table(NOWN, first_edge, last_edge).reshape(128, -1))
        in_maps.append(m)
    return in_maps


_CACHE = {}


def run(cfg, inp, runner=None):
    key = (cfg.SP, cfg.NCH, cfg.SS, cfg.NSEQ)
    if key not in _CACHE:
        _CACHE[key] = build_program(cfg)
    nc, nops = _CACHE[key]
    in_maps = host_prepare(cfg, inp)
    if runner is None:
        res = run_bass_kernel_spmd(nc, in_maps, core_ids=list(range(NCORES))).results
    else:
        res = runner(nc, in_maps)
    B = NCORES // cfg.NCH
    yp = np.zeros((B, cfg.SP, D), np.float32)
    ys = np.zeros((NCORES * cfg.NSEQ, cfg.SS, D), np.float32)
    for c in range(len(res)):
        b, j = c // cfg.NCH, c % cfg.NCH
        yp[b, j * cfg.CH:(j + 1) * cfg.CH] = np.asarray(res[c]["yp"], dtype=np.float32)
        ys[c * cfg.NSEQ:(c + 1) * cfg.NSEQ] = np.asarray(res[c]["ys"], dtype=np.float32)
    return yp, ys


def kernel(**inputs):
    cfg = Cfg()
    return run(cfg, inputs)
```

```python
import math
import contextlib
import numpy as np
import ml_dtypes
import concourse.bass as bass
import concourse.mybir as mybir
from concourse.bass_utils import run_bass_kernel_spmd

F32 = mybir.dt.float32
BF16 = mybir.dt.bfloat16
AF = mybir.ActivationFunctionType
ALU = mybir.AluOpType
AX = mybir.AxisListType
bf16 = ml_dtypes.bfloat16

D = 1024
HD = 64
PLE = 256
EPS = 1e-6
GRID_W = 64
THETA = 10000.0
SLOPES = [2.0 ** (-8.0 * (h + 1) / 4) for h in range(4)]
LAM_INIT = 0.8 - 0.6 * math.exp(-0.3 * 0)
POOL_W = (2, 4, 8, 16)
VW = 2 * 65 + 4 * 129
NCORES = 8
SAME_ENGINE_SYNC = True
PIPE = True


class Cfg:
    def __init__(self, SP=16384, NCH=4, SS=2048, NSEQ=4):
        self.SP, self.NCH, self.SS, self.NSEQ = SP, NCH, SS, NSEQ
        self.CH = SP // NCH
        self.NOWN = self.CH // 128
        self.NFAR = -(-(SP - self.CH - 8) // 128)
        self.NKBP = self.NOWN + 1 + self.NFAR
        self.NSLOT = self.NKBP * 128
        self.SQP = self.CH + 16
        self.NKBS = SS // 128
        self.NKBMAX = max(self.NKBP, self.NKBS)
        self.SQMAX = max(self.SQP, SS)


class Op:
    __slots__ = ("eng", "fn", "deps", "ddeps", "is_dma", "dsem", "dval", "inc", "cnt", "idx")


class _Rec:
    def __init__(self):
        self.calls = []

    def __getattr__(self, name):
        def f(*a, **k):
            self.calls.append((name, a, k))
        return f


class Prog:
    def __init__(self, nc):
        self.nc = nc
        self.ops = []
        self.lastw = {}
        self.readers = {}
        self.dma_keys = {}

    def _add(self, eng, fn, reads, writes, is_dma=False, dkey=None, phase=True):
        o = Op()
        rec = _Rec()
        fn(rec)
        assert len(rec.calls) == 1
        name_, a_, k_ = rec.calls[0]
        fn = lambda e, name_=name_, a_=a_, k_=k_: getattr(e, name_)(*a_, **k_)
        o.eng = eng; o.fn = fn; o.is_dma = is_dma; o.inc = False; o.cnt = 0
        o.idx = len(self.ops)
        reads = list(reads)
        if phase:
            reads.append("phase")
        deps = set()
        for k in reads:
            w = self.lastw.get(k)
            if w is not None:
                deps.add(w)
        for k in writes:
            w = self.lastw.get(k)
            if w is not None:
                deps.add(w)
            for r in self.readers.get(k, ()):
                deps.add(r)
        deps.discard(o.idx)
        o.deps = set(d for d in deps if not self.ops[d].is_dma)
        o.ddeps = {}
        for d in deps:
            p = self.ops[d]
            if p.is_dma:
                o.ddeps[p.dsem] = self.dma_keys[p.dsem]
        for k in reads:
            self.readers.setdefault(k, []).append(o.idx)
        for k in writes:
            self.lastw[k] = o.idx
            self.readers[k] = []
        if is_dma:
            o.dsem = dkey
            self.dma_keys[dkey] = self.dma_keys.get(dkey, 0) + 16
            o.dval = self.dma_keys[dkey]
        self.ops.append(o)
        return o

    def op(self, eng, fn, reads=(), writes=()):
        return self._add(eng, fn, reads, writes)

    def dma(self, out, in_, reads=(), writes=(), dkey=None, queue="sp"):
        assert dkey is not None
        fn = lambda e: e.dma_start(out=out, in_=in_)
        return self._add(queue, fn, reads, writes, is_dma=True, dkey=dkey)

    def barrier(self, tile):
        self._add("dve", lambda e: e.memset(tile, 0.0), [], ["phase"], phase=False)

    def emit(self):
        nc = self.nc
        ops = self.ops
        engs = ["pe", "act", "dve", "pool", "sp"]

        def same_skip(p, o):
            return p.eng == o.eng and (not o.is_dma) and (p.eng == "pe" or not SAME_ENGINE_SYNC)

        for o in ops:
            for d in o.deps:
                p = ops[d]
                if same_skip(p, o):
                    continue
                p.inc = True
        cnt = {e: 0 for e in engs}
        for o in ops:
            if not o.is_dma and o.inc:
                cnt[o.eng] += 1
                o.cnt = cnt[o.eng]
        es = contextlib.ExitStack()
        sem = {e: es.enter_context(nc.semaphore("s_" + e)) for e in engs}
        dsem = {}
        for i, k in enumerate(self.dma_keys):
            dsem[k] = es.enter_context(nc.semaphore("d%d" % i))
        block = es.enter_context(nc.Block())
        per = {e: [o for o in ops if o.eng == e] for e in engs}
        final = [(dsem[k], v) for k, v in self.dma_keys.items()]

        def run(e, engine):
            seen = {}
            for o in per[e]:
                waits = {}
                cands = [(dsem[k], v) for k, v in o.ddeps.items()]
                for d in o.deps:
                    p = ops[d]
                    if same_skip(p, o):
                        continue
                    cands.append((sem[p.eng], p.cnt))
                for s, v in cands:
                    key = id(s)
                    if seen.get(key, 0) >= v:
                        continue
                    if key not in waits or waits[key][1] < v:
                        waits[key] = (s, v)
                for key, (s, v) in waits.items():
                    engine.wait_ge(s, v)
                    seen[key] = v
                ins = o.fn(engine)
                if o.is_dma:
                    ins.then_inc(dsem[o.dsem], 16)
                elif o.inc:
                    ins.then_inc(sem[e], 1)
            if e == "sp":
                for s, v in final:
                    engine.wait_ge(s, v)

        @block.tensor
        def _(eng):
            run("pe", eng)

        @block.scalar
        def _(eng):
            run("act", eng)

        @block.vector
        def _(eng):
            run("dve", eng)

        @block.gpsimd
        def _(eng):
            run("pool", eng)

        @block.sync
        def _(eng):
            run("sp", eng)

        es.close()


def build_program(cfg):
    nc = bass.Bass("TRN2", target_bir_lowering=False)
    es = contextlib.ExitStack()
    CH, SS, NSEQ, NOWN = cfg.CH, cfg.SS, cfg.NSEQ, cfg.NOWN
    NSLOT, SQP, NKBP, NKBS = cfg.NSLOT, cfg.SQP, cfg.NKBP, cfg.NKBS

    def din(name, shape, dtype=F32):
        return nc.dram_tensor(name, list(shape), dtype, kind="ExternalInput").ap()

    def dscr(name, shape, dtype):
        return nc.dram_tensor(name, list(shape), dtype, kind="Internal").ap()

    xp = din("xp", [NSLOT, D]); pp0 = din("pp0", [SQP, PLE]); pp1 = din("pp1", [CH, PLE])
    xs = din("xs", [NSEQ, SS, D]); psm = din("psm", [2, NSEQ, SS, PLE])
    w_in_ab = din("w_in_ab", [D, 3328]); w_out_ab = din("w_out_ab", [D, D])
    w_gate = din("w_gate", [2 * D, D]); w_proj = din("w_proj", [2 * PLE, D])
    w_in_c = din("w_in_c", [D, 2 * D]); w_grp = din("w_grp", [D, 256]); w_out_c = din("w_out_c", [D, D])
    gmix_d = din("gmix", [128, 16]); gple_d = din("gple", [128, 16])
    hgain_d = din("hgain", [1, 4 * 64])
    lamv_d = din("lamv", [1, 4 * 64])
    subln_d = din("subln", [1, 128]); cscale_d = din("cscale", [1, D])
    cs_p = din("cs_p", [NSLOT, 128]); cs_s = din("cs_s", [SS, 128])
    kaug_p = din("kaug_p", [4, NSLOT], BF16); kaug_s = din("kaug_s", [4, SS], BF16)
    qaug_p = din("qaug_p", [4, 2, 4, SQP], BF16); qaug_s = din("qaug_s", [4, 2, 4, SS], BF16)
    dbias_d = din("dbias", [128, 4 * 128], BF16); bhalo_d = din("bhalo", [128, 2 * 4 * 16], BF16)
    vone_p_d = din("vone_p", [128, NKBP]); vone_s_d = din("vone_s", [128, NKBS])
    ident_d = din("ident", [128, 128], BF16)
    bandM_p_d = din("bandM_p", [128, 3 * 4 * 128], BF16); bandM_s_d = din("bandM_s", [128, 3 * 4 * 128], BF16)
    bandPN_d = din("bandPN", [128, 2 * 4 * 128], BF16); bandH_d = din("bandH", [16, 2 * 4 * 128], BF16)
    icnt_p_d = din("icnt_p", [128, NOWN * 4]); icnt_s_d = din("icnt_s", [128, NKBS * 4])
    yp = nc.dram_tensor("yp", [CH, D], F32, kind="ExternalOutput").ap()
    ys = nc.dram_tensor("ys", [NSEQ, SS, D], F32, kind="ExternalOutput").ap()

    wab_s = dscr("wab_s", [D, 3328], BF16); woab_s = dscr("woab_s", [D, D], BF16)
    wg_s = dscr("wg_s", [2 * D, D], BF16); wpp_s = dscr("wpp_s", [2 * PLE, D], BF16)
    wic_s = dscr("wic_s", [D, 2 * D], BF16); wgrp_s = dscr("wgrp_s", [D, 256], BF16); woc_s = dscr("woc_s", [D, D], BF16)
    SKMAX = cfg.NKBMAX * 128
    kT_s = dscr("kT_s", [10, 64, SKMAX], BF16)
    v_s = dscr("v_s", [SKMAX, VW], BF16)
    qT_s = dscr("qT_s", [16, 64, cfg.SQMAX], BF16)
    gate_s = dscr("gate_s", [cfg.SQMAX, D], F32)
    o_s = dscr("o_s", [cfg.SQMAX, D], F32)
    h1_s = dscr("h1_s", [cfg.SQMAX, D], F32)

    NB_G = 67 * 1024; NB_B = 36 * 1024; NB_W = 62 * 1024
    arenaG = es.enter_context(nc.sbuf_tensor("arenaG", [128, NB_G // 2], BF16))
    arenaB = es.enter_context(nc.sbuf_tensor("arenaB", [128, NB_B // 2], BF16))
    arenaW = es.enter_context(nc.sbuf_tensor("arenaW", [128, NB_W // 2], BF16))
    consts = es.enter_context(nc.sbuf_tensor("consts", [128, 3072], F32))
    cbf = es.enter_context(nc.sbuf_tensor("cbf", [128, 6144], BF16))
    ps = es.enter_context(nc.psum_tensor("ps", [128, 8, 512], F32))

    class Carver:
        def __init__(self, arena, nbytes):
            self.a, self.n, self.off = arena, nbytes, 0

        def reset(self):
            self.off = 0

        def get(self, free_shape, dtype):
            esz = 4 if dtype == F32 else 2
            n = int(np.prod(free_shape)) * esz
            n_al = (n + 31) // 32 * 32
            assert self.off + n_al <= self.n, ("arena overflow", self.off, n_al, self.n)
            v = self.a[:, self.off // 2:(self.off + n) // 2]
            self.off += n_al
            if dtype == F32:
                v = v.bitcast(F32)
            if len(free_shape) == 2:
                v = v.rearrange("p (a b) -> p a b", a=free_shape[0])
            elif len(free_shape) == 3:
                v = v.rearrange("p (a b c) -> p a b c", a=free_shape[0], b=free_shape[1])
            return v

    CG = Carver(arenaG, NB_G); CB = Carver(arenaB, NB_B); CW = Carver(arenaW, NB_W)

    coff = [0]

    def cget(n):
        v = consts[:, coff[0]:coff[0] + n]; coff[0] += n
        assert coff[0] <= 3072
        return v
    gmix = cget(16); gple = cget(16); hgain = cget(256); lamv = cget(256); subln = cget(128)
    cscale = cget(D); vone_p = cget(NKBP); vone_s = cget(NKBS)
    icnt_p = cget(NOWN * 4); icnt_s = cget(NKBS * 4)
    lam_t = cget(4); neglam = cget(1); sg8 = cget(128); junk1 = cget(8)
    statA = cget(64); statB = cget(64); statC = cget(64)
    identf = cget(128)
    boff = [0]

    def bget(n):
        v = cbf[:, boff[0]:boff[0] + n]; boff[0] += n
        assert boff[0] <= 6144
        return v
    ident = bget(128); dbias = bget(512); bhalo = bget(128)
    bandM_p = bget(1536); bandM_s = bget(1536); bandPN = bget(1024); bandH = bget(1024)

    P = Prog(nc)
    dk = [0]

    def DK(name):
        return name

    def cload(dst, src, key, bc=False):
        P.dma(dst, src.partition_broadcast(128) if bc else src, writes=[key], dkey="c_" + key)
    cload(gmix, gmix_d[:, :], "gmix"); cload(gple, gple_d[:, :], "gple")
    cload(hgain, hgain_d[0:1, :], "hgain", True); cload(lamv, lamv_d[0:1, :], "lamv", True)
    cload(subln, subln_d[0:1, :], "subln", True); cload(cscale, cscale_d[0:1, :], "cscale", True)
    cload(vone_p, vone_p_d[:, :], "vone_p"); cload(vone_s, vone_s_d[:, :], "vone_s")
    cload(icnt_p, icnt_p_d[:, :], "icnt_p"); cload(icnt_s, icnt_s_d[:, :], "icnt_s")
    cload(ident, ident_d[:, :], "ident"); cload(dbias, dbias_d[:, :], "dbias"); cload(bhalo, bhalo_d[:, :], "bhalo")
    cload(bandM_p, bandM_p_d[:, :], "bandM_p"); cload(bandM_s, bandM_s_d[:, :], "bandM_s")
    cload(bandPN, bandPN_d[:, :], "bandPN")
    P.dma(bandH[0:16, :], bandH_d[:, :], writes=["bandH"], dkey="c_bandH")
    P.op("dve", lambda e: e.tensor_copy(out=identf[:, :], in_=ident[:, :]), reads=["ident"], writes=["identf"])
    lv = lamv.rearrange("p (a d) -> p a d", a=4)
    P.op("dve", lambda e: e.tensor_tensor(out=statA[:, 0:64], in0=lv[:, 0, :], in1=lv[:, 1, :], op=ALU.mult), reads=["lamv"], writes=["statA"])
    P.op("dve", lambda e: e.tensor_reduce(out=lam_t[:, 0:1], in_=statA[:, 0:64], axis=AX.X, op=ALU.add), reads=["statA"], writes=["lam0"])
    P.op("dve", lambda e: e.tensor_tensor(out=statB[:, 0:64], in0=lv[:, 2, :], in1=lv[:, 3, :], op=ALU.mult), reads=["lamv"], writes=["statB"])
    P.op("dve", lambda e: e.tensor_reduce(out=lam_t[:, 1:2], in_=statB[:, 0:64], axis=AX.X, op=ALU.add), reads=["statB"], writes=["lam1"])
    P.op("act", lambda e: e.activation(out=lam_t[:, 2:4], in_=lam_t[:, 0:2], func=AF.Exp), reads=["lam0", "lam1"], writes=["lam2"])
    P.op("dve", lambda e: e.tensor_tensor(out=neglam[:, 0:1], in0=lam_t[:, 3:4], in1=lam_t[:, 2:3], op=ALU.subtract), reads=["lam2"], writes=["neglam"])
    P.op("dve", lambda e: e.tensor_scalar_add(out=neglam[:, 0:1], in0=neglam[:, 0:1], scalar1=-LAM_INIT), reads=["neglam"], writes=["neglam"])
    P.op("dve", lambda e: e.tensor_scalar_mul(out=sg8[:, :], in0=subln[:, :], scalar1=1.0 - LAM_INIT), reads=["subln"], writes=["sg8"])

    CW.reset()
    wtf = [CW.get([2048], F32) for _ in range(2)]
    wtb = [CW.get([2048], BF16) for _ in range(2)]
    wi = [0]

    def wconv(src, dst, K, N, name):
        for kc in range(K // 128):
            for n0 in range(0, N, 2048):
                nn = min(2048, N - n0)
                i = wi[0] % 2; wi[0] += 1
                P.dma(wtf[i][:, 0:nn], src[kc * 128:(kc + 1) * 128, n0:n0 + nn], writes=[("wtf", i)], dkey="wld%d" % i)
                eng = ("dve", "pool", "act")[wi[0] % 3]
                if eng == "act":
                    P.op("act", lambda e, i=i, nn=nn: e.copy(out=wtb[i][:, 0:nn], in_=wtf[i][:, 0:nn]), reads=[("wtf", i)], writes=[("wtb", i)])
                else:
                    P.op(eng, lambda e, i=i, nn=nn: e.tensor_copy(out=wtb[i][:, 0:nn], in_=wtf[i][:, 0:nn]), reads=[("wtf", i)], writes=[("wtb", i)])
                P.dma(dst[kc * 128:(kc + 1) * 128, n0:n0 + nn], wtb[i][:, 0:nn], reads=[("wtb", i)], writes=[("w", name)], dkey="wst%d" % i)
    wconv(w_in_ab, wab_s, D, 3328, "wab"); wconv(w_out_ab, woab_s, D, D, "woab")
    wconv(w_gate, wg_s, 2 * D, D, "wg"); wconv(w_proj, wpp_s, 2 * PLE, D, "wpp")
    wconv(w_in_c, wic_s, D, 2 * D, "wic"); wconv(w_grp, wgrp_s, D, 256, "wgrp"); wconv(w_out_c, woc_s, D, D, "woc")
    P.barrier(junk1[:, 0:1])

    jobs = []
    jobs.append(dict(name="p", x=xp, nkb=NKBP, nown=NOWN, halo=True, cs=cs_p, kaug=kaug_p, qaug=qaug_p, vone=vone_p,
                     p0=pp0, p1=pp1, icnt=icnt_p, bandM=bandM_p, out=yp, vkey="vone_p", ikey="icnt_p", bkey="bandM_p"))
    for s in range(NSEQ):
        jobs.append(dict(name="s%d" % s, x=xs[s], nkb=NKBS, nown=NKBS, halo=False, cs=cs_s, kaug=kaug_s, qaug=qaug_s, vone=vone_s,
                         p0=psm[0, s], p1=psm[1, s], icnt=icnt_s, bandM=bandM_s, out=ys[s], vkey="vone_s", ikey="icnt_s", bkey="bandM_s"))

    psb = ps.bitcast(BF16) if False else None

    def ps_bf(bank):
        return ps[:, bank, :].bitcast(BF16)

    evac_i = [0]

    def evac_copy(out, in_, reads, writes, scale=None):
        evac_i[0] += 1
        if evac_i[0] % 2 == 0:
            if scale is None:
                P.op("dve", lambda e: e.tensor_copy(out=out, in_=in_), reads=reads, writes=writes)
            else:
                P.op("dve", lambda e: e.tensor_scalar_mul(out=out, in0=in_, scalar1=scale), reads=reads, writes=writes)
        else:
            if scale is None:
                P.op("act", lambda e: e.copy(out=out, in_=in_), reads=reads, writes=writes)
            else:
                P.op("act", lambda e: e.activation(out=out, in_=in_, func=AF.Copy, scale=scale), reads=reads, writes=writes)

    def rms_rstd(ss_ap, n, r, reads, key):
        P.op("act", lambda e: e.activation(out=ss_ap, in_=ss_ap, func=AF.Ln, scale=1.0 / n, bias=EPS), reads=reads, writes=[key])
        P.op("act", lambda e: e.activation(out=ss_ap, in_=ss_ap, func=AF.Exp, scale=-0.5), reads=[key], writes=[key])

    def sigmoid_act(dst, src, reads, key):
        P.op("act", lambda e: e.activation(out=dst, in_=src, func=AF.Exp, scale=-1.0), reads=reads, writes=[key])
        P.op("act", lambda e: e.activation(out=dst, in_=dst, func=AF.Ln, scale=1.0, bias=1.0), reads=[key], writes=[key])
        P.op("act", lambda e: e.activation(out=dst, in_=dst, func=AF.Exp, scale=-1.0), reads=[key], writes=[key])

    def transpose_rows(src_bf, r, n_kc, si, bank0, reads, tagw):
        for kc in range(n_kc):
            pb = ps_bf(bank0 + kc // 2)
            c0 = (kc % 2) * 512 + si * 128
            P.op("pe", lambda e, pb=pb, c0=c0, kc=kc: e.transpose(out=pb[:, c0:c0 + r], in_=src_bf[0:r, kc * 128:(kc + 1) * 128], identity=ident[0:r, 0:r]),
                 reads=reads + ["ident"], writes=[("ps", bank0 + kc // 2)])

    def evac_T(dstT, n_kc, W, bank0, key, gain=None, gl=0):
        for kc in range(n_kc):
            pb = ps_bf(bank0 + kc // 2)
            c0 = (kc % 2) * 512
            sc = None if gain is None else gain[:, gl * 8 + kc:gl * 8 + kc + 1]
            evac_copy(dstT[:, kc, 0:W], pb[:, c0:c0 + W], reads=[("ps", bank0 + kc // 2)] + ([] if gain is None else ["gmix", "gple"]), writes=[key], scale=sc)

    def linear(actT, r, si, wt, n_kc, col0, ncols, bank, reads):
        for kc in range(n_kc):
            P.op("pe", lambda e, kc=kc: e.matmul(ps[0:r, bank, 0:ncols], lhsT=actT[:, kc, si * 128:si * 128 + r], rhs=wt[:, kc, col0:col0 + ncols],
                                                 start=(kc == 0), stop=(kc == n_kc - 1)),
                 reads=reads, writes=[("ps", bank)])

    def load_w(carver, src, K, N, key, wkey):
        t = carver.get([K // 128, N], BF16)
        P.dma(t, src.rearrange("(kc p) n -> p kc n", p=128), reads=[("w", wkey)], writes=[key], dkey="wl_" + key)
        return t

    def macros_of(job):
        out = []
        n = job["nown"]
        for m in range(0, n, 4):
            out.append((list(range(m, min(m + 4, n))), True, 128))
        k = n
        if job["halo"]:
            out.append(([n], True, 16))
            k = n + 1
        while k < job["nkb"]:
            out.append((list(range(k, min(k + 4, job["nkb"]))), False, 128))
            k += 4
        return out

    def qrow0(job, sub):
        return sub * 128 if sub < job["nown"] else job["nown"] * 128

    for job in jobs:
        nkb, nown, halo = job["nkb"], job["nown"], job["halo"]
        x = job["x"]
        SK = nkb * 128
        jn = job["name"]
        CG.reset(); CB.reset(); CW.reset()
        wab = load_w(CG, wab_s, D, 3328, "wab_t", "wab")
        xt = [CW.get([D], F32) for _ in range(2)]
        xnb = [CW.get([D], BF16) for _ in range(2)]
        xnT = CW.get([8, 512], BF16)
        sqt = CW.get([D], F32)
        t1 = CW.get([512], F32)
        t2 = CW.get([512], F32)
        t3 = CW.get([512], F32)
        knb = CW.get([640], BF16)
        qnb = CW.get([1024], BF16)
        kst = [CW.get([10, 128], BF16) for _ in range(2)]
        qst = [CW.get([16, 128], BF16) for _ in range(2)]
        vst = [CW.get([VW], BF16) for _ in range(2)]
        gst = [CW.get([D], F32) for _ in range(2)]
        cst = [CW.get([128], F32) for _ in range(2)]
        hg = hgain.rearrange("p (a d) -> p a d", a=4)

        def norm_heads(pv, r, nh, gi, rope, cs_t, out_bf, pkeys, okey):
            n = nh * 64
            P.op("act", lambda e: e.activation(out=sqt[0:r, 0:n], in_=pv, func=AF.Square), reads=pkeys, writes=["sqt"])
            P.op("dve", lambda e: e.tensor_reduce(out=statA[0:r, 0:nh], in_=sqt[0:r, 0:n].rearrange("p (h d) -> p h d", d=64), axis=AX.X, op=ALU.add),
                 reads=["sqt"], writes=["statA"])
            rms_rstd(statA[0:r, 0:nh], 64.0, r, ["statA"], "statA")
            P.op("dve", lambda e: e.tensor_tensor(out=t1[0:r, 0:n].rearrange("p (h d) -> p h d", d=64), in0=pv.rearrange("p (h d) -> p h d", d=64),
                                                  in1=statA[0:r, 0:nh].unsqueeze(2).to_broadcast([r, nh, 64]), op=ALU.mult),
                 reads=pkeys + ["statA"], writes=["t1"])
            gb = hg[0:r, gi, :].unsqueeze(1).to_broadcast([r, nh, 64])
            if not rope:
                P.op("pool", lambda e: e.tensor_tensor(out=out_bf.rearrange("p (h d) -> p h d", d=64), in0=t1[0:r, 0:n].rearrange("p (h d) -> p h d", d=64), in1=gb, op=ALU.mult),
                     reads=["t1", "hgain"], writes=[okey])
                return
            P.op("pool", lambda e: e.tensor_tensor(out=t2[0:r, 0:n].rearrange("p (h d) -> p h d", d=64), in0=t1[0:r, 0:n].rearrange("p (h d) -> p h d", d=64), in1=gb, op=ALU.mult),
                 reads=["t1", "hgain"], writes=["t2"])
            v2 = t2[0:r, 0:n].rearrange("p (h a b c) -> p h a b c", a=2, b=2, c=16)
            v3 = t3[0:r, 0:n].rearrange("p (h a b c) -> p h a b c", a=2, b=2, c=16)
            cosv = cs_t[0:r, 0:64].unsqueeze(1).to_broadcast([r, nh, 64])
            sinv = cs_t[0:r, 64:128].rearrange("p (a b c) -> p a b c", a=2, b=2)
            for half in range(2):
                P.op("pool", lambda e, half=half: e.tensor_tensor(out=v3[:, :, :, half, :], in0=v2[:, :, :, 1 - half, :],
                                                                  in1=sinv[:, :, half, :].unsqueeze(1).to_broadcast([r, nh, 2, 16]), op=ALU.mult),
                     reads=["t2", "cst"], writes=["t3"])
            P.op("dve", lambda e: e.tensor_tensor(out=t1[0:r, 0:n].rearrange("p (h d) -> p h d", d=64), in0=t2[0:r, 0:n].rearrange("p (h d) -> p h d", d=64), in1=cosv, op=ALU.mult),
                 reads=["t2", "cst"], writes=["t1"])
            P.op("dve", lambda e: e.tensor_tensor(out=out_bf, in0=t1[0:r, 0:n], in1=t3[0:r, 0:n], op=ALU.add), reads=["t1", "t3"], writes=[okey])

        gsub = 0
        r = 128
        for (subs, own, rq) in macros_of(job):
            ns = len(subs)
            W = ns * 128
            for si, sb in enumerate(subs):
                b = gsub % 2; gsub += 1
                P.dma(xt[b][:, :], x[sb * 128:(sb + 1) * 128, :], writes=[("xt", b)], dkey="xt%d" % b)
                P.op("act", lambda e, b=b, si=si: e.activation(out=sqt[:, :], in_=xt[b][:, :], func=AF.Square, accum_out=statB[:, si:si + 1]),
                     reads=[("xt", b)], writes=["sqt", ("statB", si)])
                rms_rstd(statB[:, si:si + 1], float(D), r, [("statB", si)], ("statB", si))
                P.op("dve", lambda e, b=b, si=si: e.tensor_scalar_mul(out=xnb[b][:, :], in0=xt[b][:, :], scalar1=statB[:, si:si + 1]),
                     reads=[("xt", b), ("statB", si)], writes=[("xnb", b)])
                transpose_rows(xnb[b], r, 8, si, 0, [("xnb", b)], None)
            evac_T(xnT, 8, W, 0, "xnT", gain=gmix, gl=0)
            for si, sb in enumerate(subs):
                b = sb % 2
                P.dma(cst[b][:, :], job["cs"][sb * 128:(sb + 1) * 128, :], writes=[("cst", b), "cst"], dkey="cst%d" % b)
                linear(xnT, r, si, wab, 8, 512, 256, 4, ["xnT", "wab_t"])
                norm_heads(ps[0:r, 4, 0:128], r, 2, 1, True, cst[b], knb[0:r, 0:128], [("ps", 4)], "knb")
                P.op("dve", lambda e, b=b: e.tensor_copy(out=vst[b][0:r, 0:130].rearrange("p (h c) -> p h c", c=65)[:, :, 0:64],
                                                         in_=ps[0:r, 4, 128:256].rearrange("p (h c) -> p h c", c=64)),
                     reads=[("ps", 4)], writes=[("vst", b)])
                linear(xnT, r, si, wab, 8, 1792, 512, 5, ["xnT", "wab_t"])
                norm_heads(ps[0:r, 5, 0:512], r, 8, 3, False, None, knb[0:r, 128:640], [("ps", 5)], "knb")
                linear(xnT, r, si, wab, 8, 2304, 512, 4, ["xnT", "wab_t"])
                P.op("act", lambda e, b=b: e.copy(out=vst[b][0:r, 130:VW].rearrange("p (h c) -> p h c", c=129)[:, :, 0:128],
                                                  in_=ps[0:r, 4, 0:512].rearrange("p (h c) -> p h c", c=128)),
                     reads=[("ps", 4)], writes=[("vst", b)])
                vo = job["vone"][0:r, sb:sb + 1]
                P.op("pool", lambda e, b=b, vo=vo: e.tensor_copy(out=vst[b][0:r, 0:130].rearrange("p (h c) -> p h c", c=65)[:, :, 64:65],
                                                                 in_=vo.unsqueeze(1).to_broadcast([r, 2, 1])),
                     reads=[job["vkey"]], writes=[("vst", b)])
                P.op("pool", lambda e, b=b, vo=vo: e.tensor_copy(out=vst[b][0:r, 130:VW].rearrange("p (h c) -> p h c", c=129)[:, :, 128:129],
                                                                 in_=vo.unsqueeze(1).to_broadcast([r, 4, 1])),
                     reads=[job["vkey"]], writes=[("vst", b)])
                P.dma(v_s[sb * 128:(sb + 1) * 128, :], vst[b][:, :], reads=[("vst", b)], writes=[("v", sb)], dkey="vst%d" % b, queue="pool")
                for hh in range(10):
                    bank = 6 if hh < 2 else 7
                    c0 = (hh if hh < 2 else hh - 2) * 128
                    P.op("pe", lambda e, hh=hh, bank=bank, c0=c0: e.transpose(out=ps_bf(bank)[0:64, c0:c0 + r], in_=knb[0:r, hh * 64:(hh + 1) * 64], identity=ident[0:r, 0:r]),
                         reads=["knb", "ident"], writes=[("ps", bank)])
                evac_copy(kst[b][0:64, 0:2, :], ps_bf(6)[0:64, 0:256].rearrange("p (h c) -> p h c", c=128), [("ps", 6)], [("kst", b)])
                evac_copy(kst[b][0:64, 2:10, :], ps_bf(7)[0:64, 0:1024].rearrange("p (h c) -> p h c", c=128), [("ps", 7)], [("kst", b)])
                P.dma(kT_s[:, :, sb * 128:(sb + 1) * 128].rearrange("u d t -> d u t"), kst[b][0:64, :, :], reads=[("kst", b)], writes=[("kT", sb)], dkey="kst%d" % b, queue="pool")
                if own:
                    linear(xnT, rq, si, wab, 8, 0, 512, 5, ["xnT", "wab_t"])
                    norm_heads(ps[0:rq, 5, 0:512], rq, 8, 0, True, cst[b], qnb[0:rq, 0:512], [("ps", 5)], "qnb")
                    linear(xnT, rq, si, wab, 8, 1280, 512, 4, ["xnT", "wab_t"])
                    norm_heads(ps[0:rq, 4, 0:512], rq, 8, 2, False, None, qnb[0:rq, 512:1024], [("ps", 4)], "qnb")
                    linear(xnT, rq, si, wab, 8, 768, 512, 5, ["xnT", "wab_t"])
                    sigmoid_act(gst[b][0:rq, 0:512], ps[0:rq, 5, 0:512], [("ps", 5)], ("gst", b))
                    P.op("dve", lambda e, b=b: e.tensor_tensor(out=gst[b][0:rq, 0:512], in0=gst[b][0:rq, 0:512], in1=ps[0:rq, 5, 0:512], op=ALU.mult), reads=[("ps", 5), ("gst", b)], writes=[("gst", b)])
                    linear(xnT, rq, si, wab, 8, 2816, 512, 4, ["xnT", "wab_t"])
                    sigmoid_act(gst[b][0:rq, 512:1024], ps[0:rq, 4, 0:512], [("ps", 4)], ("gst", b))
                    P.op("dve", lambda e, b=b: e.tensor_tensor(out=gst[b][0:rq, 512:1024], in0=gst[b][0:rq, 512:1024], in1=ps[0:rq, 4, 0:512], op=ALU.mult), reads=[("ps", 4), ("gst", b)], writes=[("gst", b)])
                    q0 = qrow0(job, sb)
                    P.dma(gate_s[q0:q0 + rq, :], gst[b][0:rq, :], reads=[("gst", b)], writes=[("gate", sb)], dkey="gst%d" % b, queue="pool")
                    for hh in range(16):
                        bank = 6 if hh < 8 else 7
                        c0 = (hh % 8) * 128
                        P.op("pe", lambda e, hh=hh, bank=bank, c0=c0: e.transpose(out=ps_bf(bank)[0:64, c0:c0 + rq], in_=qnb[0:rq, hh * 64:(hh + 1) * 64], identity=ident[0:rq, 0:rq]),
                             reads=["qnb", "ident"], writes=[("ps", bank)])
                    for half in range(2):
                        evac_copy(qst[b][0:64, half * 8:half * 8 + 8, 0:rq], ps_bf(6 + half)[0:64, 0:1024].rearrange("p (h c) -> p h c", c=128)[:, :, 0:rq],
                                  [("ps", 6 + half)], [("qst", b)])
                    P.dma(qT_s[:, :, q0:q0 + rq].rearrange("u d t -> d u t"), qst[b][0:64, :, 0:rq], reads=[("qst", b)], writes=[("qT", sb)], dkey="qst%d" % b, queue="pool")
        P.barrier(junk1[:, 0:1])

        CG.reset(); CB.reset(); CW.reset()
        KT = CG.get([2, SK], BF16)
        VtA = CB.get([nkb, 65], BF16)
        CB.reset()
        VtB = CB.get([nkb, 129], BF16)
        QT = [CW.get([2, 2, 512], BF16) for _ in range(2)]
        PT = [CW.get([2, 512], BF16) for _ in range(3)]
        ost = [CW.get([4, 128], F32) for _ in range(2)]
        oT = [CW.get([512], F32) for _ in range(2)]
        o1 = CW.get([4, 128], F32); o2 = CW.get([4, 128], F32); ob = CW.get([4, 128], F32); sq2 = CW.get([4, 128], F32)
        P.op("pool", lambda e: e.memset(KT[64:128, :, :], 0.0), writes=["KT"])
        for qs_ in range(2):
            P.op("pool", lambda e: e.memset(QT[qs_][64:128, :, :, :], 0.0), writes=[("QT", qs_)])
        kT_keys = [("kT", sb) for sb in range(nkb)]
        v_keys = [("v", sb) for sb in range(nkb)]
        qtiles = []
        for m in range(0, nown, 4):
            qtiles.append(dict(i=m // 4, q0=m * 128, ns=min(4, nown - m), r=128, halo=False, keys=[("qT", m + t_) for t_ in range(min(4, nown - m))], subs0=m))
        if halo:
            qtiles.append(dict(i=-1, q0=nown * 128, ns=1, r=16, halo=True, keys=[("qT", nown)], subs0=nown))
        rnd = [0]; gi_ = [0]; qi_ = [0]

        def key_loop(qt, QTt, mslot, kmap, kind, h, obanks, vt, vw, round_started):
            W = (qt["ns"] - 1) * 128 + qt["r"]
            ns, r = qt["ns"], qt["r"]
            KR = 128
            groups = []
            kb = 0
            while kb < nkb:
                grp = [kb] if kb + 1 >= nkb else [kb, kb + 1]
                g = gi_[0]; gi_[0] += 1
                groups.append((grp, g % 2, g % 3))
                kb += len(grp)

            def emit_qk(grp, sslot):
                for j, k in enumerate(grp):
                    bank = sslot * 2 + j
                    kcols = slice(k * 128, (k + 1) * 128)
                    mode = "plain"; var = 0
                    if kind == "b":
                        if qt["halo"]:
                            if k < nown:
                                var = 1
                            elif k == nown:
                                mode = "halo"
                            else:
                                var = 0
                        else:
                            i4 = qt["i"] * 4
                            if k < i4:
                                var = 1
                            elif k < i4 + 4 and k < nown:
                                mode = "diag"
                            else:
                                var = 0
                    rd = ["KT", ("QT", qi_[0] % 2)]
                    if mode == "plain":
                        P.op("pe", lambda e: e.matmul(ps[:, bank, 0:W], lhsT=KT[0:KR, kmap, kcols], rhs=QTt[0:KR, mslot, var, 0:W], start=True, stop=True),
                             reads=rd, writes=[("ps", bank)])
                    elif mode == "diag":
                        dd = k - qt["i"] * 4
                        first = True
                        for s in range(ns):
                            cs_ = slice(s * 128, (s + 1) * 128)
                            if s == dd:
                                P.op("pe", lambda e: e.matmul(ps[:, bank, cs_], lhsT=KT[0:64, kmap, kcols], rhs=QTt[0:64, mslot, 0, cs_],
                                                              start=first, stop=False, skip_group_check=True),
                                     reads=rd, writes=[("ps", bank)])
                                P.op("pe", lambda e: e.matmul(ps[:, bank, cs_], lhsT=ident[:, :], rhs=dbias[:, h * 128:(h + 1) * 128], start=False, stop=True, skip_group_check=True),
                                     reads=["ident", "dbias"], writes=[("ps", bank)])
                            else:
                                v_ = 0 if s < dd else 1
                                P.op("pe", lambda e: e.matmul(ps[:, bank, cs_], lhsT=KT[0:128, kmap, kcols], rhs=QTt[0:128, mslot, v_, cs_],
                                                              start=first, stop=True, skip_group_check=True),
                                     reads=rd, writes=[("ps", bank)])
                            first = False
                    else:
                        P.op("pe", lambda e: e.matmul(ps[:, bank, 0:W], lhsT=KT[0:64, kmap, kcols], rhs=QTt[0:64, mslot, 0, 0:W], start=True, stop=False, skip_group_check=True),
                             reads=rd, writes=[("ps", bank)])
                        for hl in range(2):
                            c0 = (hl * 4 + h) * 16
                            P.op("pe", lambda e: e.matmul(ps[:, bank, 0:W], lhsT=ident[:, :], rhs=bhalo[:, c0:c0 + 16], start=False, stop=(hl == 1), skip_group_check=True),
                                 reads=["ident", "bhalo"], writes=[("ps", bank)])

            if PIPE:
                emit_qk(groups[0][0], groups[0][1])
            for gidx, (grp, sslot, pslot) in enumerate(groups):
                if not PIPE:
                    emit_qk(grp, sslot)
                elif gidx + 1 < len(groups):
                    emit_qk(groups[gidx + 1][0], groups[gidx + 1][1])
                ng = len(grp)
                P.op("act", lambda e: e.activation(out=PT[pslot][:, 0:ng, 0:W], in_=ps[:, sslot * 2:sslot * 2 + ng, 0:W], func=AF.Exp, scale=0.125),
                     reads=[("ps", sslot * 2 + j) for j in range(ng)], writes=[("PT", pslot)])
                for j, k in enumerate(grp):
                    last = (k == nkb - 1)
                    if kind == "a":
                        obank = obanks[0]
                        st = obank not in round_started
                        round_started.add(obank)
                        P.op("pe", lambda e: e.matmul(ps[0:65, obank, 0:W], lhsT=vt[:, k, 0:65], rhs=PT[pslot][:, j, 0:W], start=st, stop=last, skip_group_check=True),
                             reads=[("PT", pslot), "Vt"], writes=[("ps", obank)])
                        continue
                    for s in range(ns):
                        obank = obanks[s // 2]; oc = (s % 2) * 129
                        st = obank not in round_started
                        round_started.add(obank)
                        P.op("pe", lambda e: e.matmul(ps[0:r, obank, oc:oc + vw], lhsT=PT[pslot][:, j, s * 128:s * 128 + r], rhs=vt[:, k, 0:vw], start=st, stop=last, skip_group_check=True),
                             reads=[("PT", pslot), "Vt"], writes=[("ps", obank)])

        for kvh in range(2):
            P.dma(KT[0:64, 0, :], kT_s[kvh, :, 0:SK], reads=kT_keys, writes=["KT"], dkey="KTa")
            for c0 in range(0, nkb, 16):
                c1 = min(nkb, c0 + 16)
                P.dma(VtA[:, c0:c1, :], v_s[c0 * 128:c1 * 128, kvh * 65:(kvh + 1) * 65].rearrange("(kb p) c -> p kb c", p=128), reads=v_keys[c0:c1], writes=["Vt"], dkey="Vta")
            for g4 in range(4):
                hq = kvh * 4 + g4
                for qt in qtiles:
                    qi_[0] += 1
                    qs = qi_[0] % 2
                    W = (qt["ns"] - 1) * 128 + qt["r"]
                    P.dma(QT[qs][0:64, 0, 0, 0:W], qT_s[hq, :, qt["q0"]:qt["q0"] + W], reads=qt["keys"], writes=[("QT", qs)], dkey="QT%d" % qs)
                    rnd[0] += 1
                    obank = 4 + rnd[0] % 2
                    key_loop(qt, QT[qs], 0, 0, "a", 0, [obank], VtA, 65, set())
                    ns, r = qt["ns"], qt["r"]
                    os_ = ost[rnd[0] % 2]
                    oTt = oT[rnd[0] % 2]
                    evac_copy(oTt[0:65, 0:W], ps[0:65, obank, 0:W], [("ps", obank)], [("oT", rnd[0] % 2)])
                    obank = 6 + rnd[0] % 2
                    for s_ in range(ns):
                        P.op("pe", lambda e: e.transpose(out=ps[0:r, obank, s_ * 65:(s_ + 1) * 65], in_=oTt[0:65, s_ * 128:s_ * 128 + r], identity=identf[0:65, 0:65]),
                             reads=[("oT", rnd[0] % 2), "identf"], writes=[("ps", obank)])
                    pso = ps[0:r, obank, 0:ns * 65].rearrange("p (s c) -> p s c", c=65)
                    P.op("dve", lambda e, pso=pso, r=r, ns=ns: e.reciprocal(out=statC[0:r, 0:ns].unsqueeze(2), in_=pso[:, :, 64:65]), reads=[("ps", obank)], writes=["statC"])
                    P.op("dve", lambda e, pso=pso, r=r, ns=ns, os_=os_: e.tensor_tensor(out=os_[0:r, 0:ns, 0:64], in0=pso[:, :, 0:64], in1=statC[0:r, 0:ns].unsqueeze(2).to_broadcast([r, ns, 64]), op=ALU.mult),
                         reads=[("ps", obank), "statC"], writes=[("ost", rnd[0] % 2)])
                    if qt["halo"]:
                        dst = o_s[qt["q0"]:qt["q0"] + r, hq * 64:(hq + 1) * 64]
                        srcv = os_[0:r, 0, 0:64]
                    else:
                        dst = o_s[qt["q0"]:qt["q0"] + ns * 128, hq * 64:(hq + 1) * 64].rearrange("(s p) c -> p s c", p=128)
                        srcv = os_[:, 0:ns, 0:64]
                    P.dma(dst, srcv, reads=[("ost", rnd[0] % 2)], writes=[("o", qt["subs0"])], dkey="ost%d" % (rnd[0] % 2), queue="pool")
        for h in range(4):
            for m in range(2):
                P.dma(KT[0:64, m, :], kT_s[2 + h * 2 + m, :, 0:SK], reads=kT_keys, writes=["KT"], dkey="KTb%d" % m)
                P.dma(KT[64:68, m, :], job["kaug"][:, 0:SK], reads=[], writes=["KT"], dkey="KTx%d" % m)
            for c0 in range(0, nkb, 16):
                c1 = min(nkb, c0 + 16)
                P.dma(VtB[:, c0:c1, :], v_s[c0 * 128:c1 * 128, 130 + h * 129:130 + (h + 1) * 129].rearrange("(kb p) c -> p kb c", p=128), reads=v_keys[c0:c1], writes=["Vt"], dkey="Vtb")
            for qt in qtiles:
                qi_[0] += 1
                qs = qi_[0] % 2
                ns, r = qt["ns"], qt["r"]
                W = (ns - 1) * 128 + r
                for m in range(2):
                    for var in range(2):
                        P.dma(QT[qs][0:64, m, var, 0:W], qT_s[8 + h * 2 + m, :, qt["q0"]:qt["q0"] + W], reads=qt["keys"], writes=[("QT", qs)], dkey="QTb%d%d%d" % (qs, m, var))
                        P.dma(QT[qs][64:68, m, var, 0:W], job["qaug"][h, var, :, qt["q0"]:qt["q0"] + W], reads=[], writes=[("QT", qs)], dkey="QTx%d%d%d" % (qs, m, var))
                for m in range(2):
                    key_loop(qt, QT[qs], m, m, "b", h, [4 + 2 * m, 5 + 2 * m], VtB, 129, set())
                rnd[0] += 1

                def pso(m, half):
                    return ps[0:r, 4 + 2 * m + half, 0:258].rearrange("p (s c) -> p s c", c=129)
                nhalf = (ns + 1) // 2
                for m in range(2):
                    dstt = o1 if m == 0 else o2
                    for half in range(nhalf):
                        nn = min(2, ns - half * 2)
                        pv = pso(m, half)
                        P.op("dve", lambda e, pv=pv, nn=nn, m=m, half=half: e.reciprocal(out=statC[0:r, 8 + m * 4 + half * 2:8 + m * 4 + half * 2 + nn].unsqueeze(2), in_=pv[:, 0:nn, 128:129]),
                             reads=[("ps", 4 + 2 * m + half)], writes=[("statC", m, half)])
                        P.op("dve", lambda e, pv=pv, nn=nn, m=m, half=half, dstt=dstt: e.tensor_tensor(
                            out=dstt[0:r, half * 2:half * 2 + nn, :], in0=pv[:, 0:nn, 0:128],
                            in1=statC[0:r, 8 + m * 4 + half * 2:8 + m * 4 + half * 2 + nn].unsqueeze(2).to_broadcast([r, nn, 128]), op=ALU.mult),
                            reads=[("ps", 4 + 2 * m + half), ("statC", m, half)], writes=[("o12", m)])
                P.op("dve", lambda e, r=r, ns=ns: e.scalar_tensor_tensor(out=ob[0:r, 0:ns, :], in0=o2[0:r, 0:ns, :], scalar=neglam[0:r, 0:1], in1=o1[0:r, 0:ns, :], op0=ALU.mult, op1=ALU.add),
                     reads=[("o12", 0), ("o12", 1), "neglam"], writes=["ob"])
                P.op("act", lambda e, r=r, ns=ns: e.activation(out=sq2[0:r, 0:ns, :], in_=ob[0:r, 0:ns, :], func=AF.Square), reads=["ob"], writes=["sq2"])
                P.op("dve", lambda e, r=r, ns=ns: e.tensor_reduce(out=statA[0:r, 0:ns], in_=sq2[0:r, 0:ns, :], axis=AX.X, op=ALU.add), reads=["sq2"], writes=["statA"])
                rms_rstd(statA[0:r, 0:ns], 128.0, r, ["statA"], "statA")
                P.op("dve", lambda e, r=r, ns=ns: e.tensor_tensor(out=sq2[0:r, 0:ns, :], in0=ob[0:r, 0:ns, :], in1=statA[0:r, 0:ns].unsqueeze(2).to_broadcast([r, ns, 128]), op=ALU.mult),
                     reads=["ob", "statA"], writes=["sq2"])
                os_ = ost[rnd[0] % 2]
                P.op("pool", lambda e, r=r, ns=ns, os_=os_: e.tensor_tensor(out=os_[0:r, 0:ns, :], in0=sq2[0:r, 0:ns, :], in1=sg8[0:r, :].unsqueeze(1).to_broadcast([r, ns, 128]), op=ALU.mult),
                     reads=["sq2", "sg8"], writes=[("ost", rnd[0] % 2)])
                if qt["halo"]:
                    dst = o_s[qt["q0"]:qt["q0"] + r, 512 + h * 128:512 + (h + 1) * 128]
                    srcv = os_[0:r, 0, :]
                else:
                    dst = o_s[qt["q0"]:qt["q0"] + ns * 128, 512 + h * 128:512 + (h + 1) * 128].rearrange("(s p) c -> p s c", p=128)
                    srcv = os_[:, 0:ns, :]
                P.dma(dst, srcv, reads=[("ost", rnd[0] % 2)], writes=[("o", qt["subs0"])], dkey="ost%d" % (rnd[0] % 2), queue="pool")
        P.barrier(junk1[:, 0:1])

        def ple_block(hts, r, ns, W, gT, pT, wgt, wpt, layer, p_ap, prow0, ptile, pbf, sgt, tmp, hnb, gkey):
            for si in range(ns):
                ht = hts[si]
                P.op("act", lambda e, ht=ht, si=si: e.activation(out=tmp[0:r, :], in_=ht[0:r, :], func=AF.Square, accum_out=statB[0:r, si:si + 1]),
                     reads=[("ht", si)], writes=["tmp", ("statB", si)])
                rms_rstd(statB[0:r, si:si + 1], float(D), r, [("statB", si)], ("statB", si))
                hb = hnb[si % 2]
                P.op("dve", lambda e, ht=ht, si=si, hb=hb: e.tensor_scalar_mul(out=hb[0:r, :], in0=ht[0:r, :], scalar1=statB[0:r, si:si + 1]),
                     reads=[("ht", si), ("statB", si)], writes=[("hnb", si % 2)])
                transpose_rows(hb, r, 8, si, 0, [("hnb", si % 2)], None)
                P.dma(ptile[si % 2][0:r, :], p_ap[prow0 + si * 128:prow0 + si * 128 + r, :], writes=[("ptile", si % 2)], dkey="pt%d" % (si % 2))
                P.op("pool", lambda e, si=si: e.tensor_copy(out=pbf[si % 2][0:r, :], in_=ptile[si % 2][0:r, :]), reads=[("ptile", si % 2)], writes=[("pbf", si % 2)])
                for kc in range(2):
                    P.op("pe", lambda e, si=si, kc=kc: e.transpose(out=ps_bf(6)[:, kc * 512 + si * 128:kc * 512 + si * 128 + r], in_=pbf[si % 2][0:r, kc * 128:(kc + 1) * 128], identity=ident[0:r, 0:r]),
                         reads=[("pbf", si % 2), "ident"], writes=[("ps", 6)])
            evac_T(gT, 8, W, 0, gkey, gain=gple, gl=layer)
            for kc in range(2):
                evac_copy(pT[:, kc, 0:W], ps_bf(6)[:, kc * 512:kc * 512 + W], [("ps", 6)], ["pT"])
            for si in range(ns):
                ht = hts[si]
                for c in range(2):
                    linear(gT, r, si, wgt, 8, c * 512, 512, 4 + c, [gkey, "wgt"])
                    sigmoid_act(sgt[0:r, c * 512:(c + 1) * 512], ps[0:r, 4 + c, :], [("ps", 4 + c)], ("sgt", c))
                for c in range(2):
                    linear(pT, r, si, wpt, 2, c * 512, 512, 6 + c, ["pT", "wpt"])
                    P.op("dve", lambda e, c=c: e.tensor_tensor(out=tmp[0:r, c * 512:(c + 1) * 512], in0=ps[0:r, 6 + c, :], in1=sgt[0:r, c * 512:(c + 1) * 512], op=ALU.mult),
                         reads=[("ps", 6 + c), ("sgt", c)], writes=["tmp"])
                    P.op("pool", lambda e, c=c, ht=ht: e.tensor_tensor(out=ht[0:r, c * 512:(c + 1) * 512], in0=ht[0:r, c * 512:(c + 1) * 512], in1=tmp[0:r, c * 512:(c + 1) * 512], op=ALU.add),
                         reads=["tmp", ("ht", si)], writes=[("ht", si)])

        CG.reset(); CB.reset(); CW.reset()
        woab = load_w(CB, woab_s, D, D, "woab_t", "woab")
        wgt0 = load_w(CB, wg_s[0:D, :], D, D, "wgt", "wg")
        wpt0 = load_w(CB, wpp_s[0:PLE, :], PLE, D, "wpt", "wpp")
        hts = [CW.get([D], F32) for _ in range(4)]
        ot = [CW.get([D], F32) for _ in range(2)]
        gt_ = [CW.get([D], F32) for _ in range(2)]
        ybf = [CW.get([D], BF16) for _ in range(2)]
        yT = CW.get([8, 512], BF16)
        gT = yT
        pT = CW.get([2, 512], BF16)
        ptile = [CW.get([PLE], F32) for _ in range(2)]
        pbf = [CW.get([PLE], BF16) for _ in range(2)]
        sgt = CW.get([D], F32)
        tmp = CW.get([D], F32)
        hnb = [CW.get([D], BF16) for _ in range(2)]
        for qt in qtiles:
            ns, r = qt["ns"], qt["r"]
            W = (ns - 1) * 128 + r
            for si in range(ns):
                b = si % 2
                row = qt["q0"] + si * 128
                P.dma(ot[b][0:r, :], o_s[row:row + r, :], reads=[("o", qt["subs0"])], writes=[("ot", b)], dkey="ot%d" % b)
                P.dma(gt_[b][0:r, :], gate_s[row:row + r, :], reads=[("gate", qt["subs0"] + si)], writes=[("gt", b)], dkey="gt%d" % b)
                P.op("dve", lambda e, b=b: e.tensor_tensor(out=ybf[b][0:r, :], in0=ot[b][0:r, :], in1=gt_[b][0:r, :], op=ALU.mult), reads=[("ot", b), ("gt", b)], writes=[("ybf", b)])
                transpose_rows(ybf[b], r, 8, si, 0, [("ybf", b)], None)
            evac_T(yT, 8, W, 0, "yT")
            for si in range(ns):
                b = si % 2
                xrow = (qt["subs0"] + si) * 128
                P.dma(hts[si][0:r, :], x[xrow:xrow + r, :], writes=[("ht", si)], dkey="xq%d" % si)
                for c in range(2):
                    linear(yT, r, si, woab, 8, c * 512, 512, 4 + c, ["yT", "woab_t"])
                    P.op("dve", lambda e, c=c, si=si: e.tensor_tensor(out=hts[si][0:r, c * 512:(c + 1) * 512], in0=ps[0:r, 4 + c, :], in1=hts[si][0:r, c * 512:(c + 1) * 512], op=ALU.add),
                         reads=[("ps", 4 + c), ("ht", si)], writes=[("ht", si)])
            ple_block(hts, r, ns, W, gT, pT, wgt0, wpt0, 0, job["p0"], qt["q0"], ptile, pbf, sgt, tmp, hnb, "yT")
            for si in range(ns):
                row = qt["q0"] + si * 128
                P.dma(h1_s[row:row + r, :], hts[si][0:r, :], reads=[("ht", si)], writes=[("h1", qt["subs0"] + si)], dkey="h1st%d" % si, queue="pool")
        P.barrier(junk1[:, 0:1])

        CG.reset(); CB.reset(); CW.reset()
        wic = load_w(CG, wic_s, D, 2 * D, "wic_t", "wic")
        wgr = load_w(CG, wgrp_s, D, 256, "wgr_t", "wgrp")
        woc = load_w(CG, woc_s, D, D, "woc_t", "woc")
        wgt1 = load_w(CB, wg_s[D:2 * D, :], D, D, "wgt", "wg")
        wpt1 = load_w(CB, wpp_s[PLE:2 * PLE, :], PLE, D, "wpt", "wpp")
        NU = 5
        ubf = [CG.get([D], BF16) for _ in range(NU)]
        uh = CG.get([D], BF16)
        hT = CW.get([8, 256], BF16)
        sgb = [CB.get([D], BF16) for _ in range(4)]
        h1t = [CB.get([D], F32) for _ in range(2)]
        hnb1 = [CW.get([D], BF16) for _ in range(2)]
        pooled = [CW.get([D], BF16) for _ in range(2)]
        pooledT = CW.get([8, 256], BF16)
        y2 = [CW.get([D], BF16) for _ in range(2)]
        y2T = CW.get([8, 256], BF16)
        h2 = [CW.get([D], F32) for _ in range(2)]
        tmpc = CW.get([D], F32)
        sgt1 = CW.get([D], F32)
        tmp1 = CW.get([D], F32)
        gT1 = CW.get([8, 256], BF16)
        pT1 = CW.get([2, 256], BF16)
        ptile1 = [CW.get([PLE], F32) for _ in range(2)]
        pbf1 = [CW.get([PLE], BF16) for _ in range(2)]
        hnb2 = [CW.get([D], BF16) for _ in range(2)]
        sgf = CW.get([D], F32)

        def evac_T2(dstT, n_kc, W, key, gain=None, gl=0):
            evac_T(dstT, n_kc, W, 0, key, gain, gl)

        def stageA(subs_, r, want_g, dst_u):
            ns = len(subs_)
            W = (ns - 1) * 128 + r
            for si, sb in enumerate(subs_):
                b = si % 2
                row = qrow0(job, sb)
                P.dma(h1t[b][0:r, :], h1_s[row:row + r, :], reads=[("h1", sb)], writes=[("h1t", b)], dkey="h1t%d" % b)
                P.op("act", lambda e, b=b, si=si: e.activation(out=sgf[0:r, :], in_=h1t[b][0:r, :], func=AF.Square, accum_out=statB[0:r, 8 + si:9 + si]),
                     reads=[("h1t", b)], writes=[("sgf", 0), ("sgf", 1), ("statB", 8 + si)])
                rms_rstd(statB[0:r, 8 + si:9 + si], float(D), r, [("statB", 8 + si)], ("statB", 8 + si))
                P.op("dve", lambda e, b=b, si=si: e.tensor_scalar_mul(out=hnb1[b][0:r, :], in0=h1t[b][0:r, :], scalar1=statB[0:r, 8 + si:9 + si]),
                     reads=[("h1t", b), ("statB", 8 + si)], writes=[("hnb1", b)])
                transpose_rows(hnb1[b], r, 8, si, 0, [("hnb1", b)], None)
            evac_T2(hT, 8, W, "hT", gain=gmix, gl=1)
            for si, sb in enumerate(subs_):
                ut, ukey = dst_u(sb)
                for c in range(2):
                    linear(hT, r, si, wic, 8, c * 512, 512, 4 + c, ["hT", "wic_t"])
                    evac_copy(ut[0:r, c * 512:(c + 1) * 512], ps[0:r, 4 + c, :], [("ps", 4 + c)], [ukey])
                if want_g:
                    for c in range(2):
                        linear(hT, r, si, wic, 8, D + c * 512, 512, 6 + c, ["hT", "wic_t"])
                        sigmoid_act(sgf[0:r, c * 512:(c + 1) * 512], ps[0:r, 6 + c, :], [("ps", 6 + c)], ("sgf", c))
                        P.op("dve", lambda e, c=c, sb=sb: e.tensor_tensor(out=sgb[sb % 4][0:r, c * 512:(c + 1) * 512], in0=sgf[0:r, c * 512:(c + 1) * 512], in1=ps[0:r, 6 + c, :], op=ALU.mult),
                             reads=[("ps", 6 + c), ("sgf", c)], writes=[("sgb", sb % 4)])

        l1macros = [list(range(m, min(m + 2, nown))) for m in range(0, nown, 2)]
        if halo:
            stageA([nown], 16, False, lambda sb: (uh, "uh"))
        stageA(l1macros[0], 128, True, lambda sb: (ubf[sb % NU], ("ubf", sb % NU)))
        bM = job["bandM"].rearrange("p (v g t) -> p v g t", v=3, g=4)
        bPN = bandPN.rearrange("p (v g t) -> p v g t", v=2, g=4)
        bH = bandH.rearrange("p (v g t) -> p v g t", v=2, g=4)
        ic = job["icnt"].rearrange("p (s g) -> p s g", g=4)
        for mi, subs_ in enumerate(l1macros):
            if mi + 1 < len(l1macros):
                stageA(l1macros[mi + 1], 128, True, lambda sb: (ubf[sb % NU], ("ubf", sb % NU)))
            ns = len(subs_)
            W = ns * 128
            r = 128
            for si, sb in enumerate(subs_):
                b = si % 2
                var = 0 if sb == 0 else (2 if sb == nown - 1 else 1)
                for half in range(2):
                    bank = 4 + half
                    for gg in range(2):
                        g = half * 2 + gg
                        cols = slice(g * 256, (g + 1) * 256)
                        mm = []
                        mm.append((bM[:, var, g, :], ubf[sb % NU][:, cols], [job["bkey"], ("ubf", sb % NU)]))
                        if sb > 0:
                            mm.append((bPN[:, 0, g, :], ubf[(sb - 1) % NU][:, cols], ["bandPN", ("ubf", (sb - 1) % NU)]))
                        elif halo:
                            mm.append((bH[0:16, 0, g, :], uh[0:16, cols], ["bandH", "uh"]))
                        if sb < nown - 1:
                            mm.append((bPN[:, 1, g, :], ubf[(sb + 1) % NU][:, cols], ["bandPN", ("ubf", (sb + 1) % NU)]))
                        elif halo:
                            mm.append((bH[0:16, 1, g, :], uh[0:16, cols], ["bandH", "uh"]))
                        for k_, (lt, rh, rd) in enumerate(mm):
                            P.op("pe", lambda e, lt=lt, rh=rh, bank=bank, gg=gg, k_=k_, nmm=len(mm): e.matmul(ps[:, bank, gg * 256:(gg + 1) * 256], lhsT=lt, rhs=rh,
                                                                                                             start=(k_ == 0 and gg == 0), stop=(k_ == nmm - 1), skip_group_check=True),
                                 reads=rd, writes=[("ps", bank)])
                    P.op("dve", lambda e, bank=bank, half=half, b=b, sb=sb: e.tensor_tensor(
                        out=pooled[b][:, half * 512:(half + 1) * 512].rearrange("p (g c) -> p g c", g=2), in0=ps[:, bank, :].rearrange("p (g c) -> p g c", g=2),
                        in1=ic[:, sb, half * 2:half * 2 + 2].unsqueeze(2).to_broadcast([128, 2, 256]), op=ALU.mult),
                        reads=[("ps", bank), job["ikey"]], writes=[("pooled", b)])
                transpose_rows(pooled[b], r, 8, si, 0, [("pooled", b)], None)
            evac_T2(pooledT, 8, W, "pooledT")
            for si, sb in enumerate(subs_):
                b = si % 2
                for half in range(2):
                    bank = 4 + half
                    for gg in range(2):
                        g = half * 2 + gg
                        for kc in range(2):
                            P.op("pe", lambda e, bank=bank, gg=gg, g=g, kc=kc, si=si: e.matmul(ps[:, bank, gg * 256:(gg + 1) * 256], lhsT=pooledT[:, g * 2 + kc, si * 128:(si + 1) * 128],
                                                                                               rhs=wgr[:, g * 2 + kc, :], start=(kc == 0 and gg == 0), stop=(kc == 1), skip_group_check=True),
                                 reads=["pooledT", "wgr_t"], writes=[("ps", bank)])
                    P.op("dve", lambda e, bank=bank, half=half: e.tensor_tensor(out=tmpc[:, half * 512:(half + 1) * 512], in0=ps[:, bank, :], in1=cscale[:, half * 512:(half + 1) * 512], op=ALU.mult),
                         reads=[("ps", bank), "cscale"], writes=[("tmpc", half)])
                    P.op("pool", lambda e, half=half, b=b, sb=sb: e.tensor_tensor(out=y2[b][:, half * 512:(half + 1) * 512], in0=tmpc[:, half * 512:(half + 1) * 512],
                                                                                  in1=sgb[sb % 4][:, half * 512:(half + 1) * 512], op=ALU.mult),
                         reads=[("tmpc", half), ("sgb", sb % 4)], writes=[("y2", b)])
                transpose_rows(y2[b], r, 8, si, 0, [("y2", b)], None)
            evac_T2(y2T, 8, W, "y2T")
            for si, sb in enumerate(subs_):
                b = si % 2
                row = sb * 128
                P.dma(h2[b][:, :], h1_s[row:row + 128, :], reads=[("h1", sb)], writes=[("ht", si)], dkey="h2l%d" % b)
                for c in range(2):
                    linear(y2T, r, si, woc, 8, c * 512, 512, 4 + c, ["y2T", "woc_t"])
                    P.op("dve", lambda e, c=c, b=b: e.tensor_tensor(out=h2[b][:, c * 512:(c + 1) * 512], in0=ps[:, 4 + c, :], in1=h2[b][:, c * 512:(c + 1) * 512], op=ALU.add),
                         reads=[("ps", 4 + c), ("ht", si)], writes=[("ht", si)])
            ple_block([h2[si % 2] for si in range(ns)], r, ns, W, gT1, pT1, wgt1, wpt1, 1, job["p1"], subs_[0] * 128, ptile1, pbf1, sgt1, tmp1, hnb2, "gT1")
            for si, sb in enumerate(subs_):
                P.dma(job["out"][sb * 128:(sb + 1) * 128, :], h2[si % 2][:, :], reads=[("ht", si)], writes=[("out", jn, sb)], dkey="out%d" % (si % 2), queue="pool")
        P.barrier(junk1[:, 0:1])

    P.emit()
    es.close()
    return nc, len(P.ops)


def rope_cs(pos):
    pos = np.asarray(pos, dtype=np.int64)
    row = (pos // GRID_W).astype(np.float32)
    col = (pos % GRID_W).astype(np.float32)
    half = HD // 2
    inv = (np.float32(THETA) ** (-np.arange(0, half, 2, dtype=np.float32) / np.float32(half))).astype(np.float32)
    ar = row[:, None] * inv
    ac = col[:, None] * inv
    ang = np.concatenate([ar, ar, ac, ac], axis=-1).astype(np.float32)
    cos = np.cos(ang).astype(np.float32)
    sin = np.sin(ang).astype(np.float32)
    sign = np.concatenate([-np.ones(16), np.ones(16), -np.ones(16), np.ones(16)]).astype(np.float32)
    return np.concatenate([cos, sin * sign[None, :]], axis=-1).astype(np.float32)


def kaug_rows(pos, sigma):
    pos = np.asarray(pos, dtype=np.int64)
    hi = (pos // 128) * 128
    lo = pos % 128
    s = np.asarray(sigma, dtype=np.float32)
    return np.stack([s, s, s * hi, s * lo], 0).astype(bf16)


def qaug_rows(pos, sigma):
    pos = np.asarray(pos, dtype=np.int64)
    hi = ((pos // 128) * 128).astype(np.float32)
    lo = (pos % 128).astype(np.float32)
    s = np.asarray(sigma, dtype=np.float32)
    out = []
    for h in range(4):
        s8 = np.float32(SLOPES[h] * 8.0)
        out.append(np.stack([s * s8 * hi, s * s8 * lo, -s * s8 * np.ones_like(hi), -s * s8 * np.ones_like(hi)], 0))
    return np.stack(out, 0).astype(bf16)


def band_tables(S, first, last):
    M = np.zeros((4, 128, 128), np.float32)
    for g, w in enumerate(POOL_W):
        for t in range(128):
            lo = t - w // 2
            hi = t + w // 2
            if first:
                lo = max(lo, 0)
            if last:
                hi = min(hi, 128)
            cnt = hi - lo
            for tp in range(max(lo, 0), min(hi, 128)):
                M[g, tp, t] = 1.0
            M[g, t, t] += -cnt
    return M


def band_pn():
    Pm = np.zeros((4, 128, 128), np.float32)
    Nm = np.zeros((4, 128, 128), np.float32)
    for g, w in enumerate(POOL_W):
        for t in range(128):
            for tp in range(128):
                if tp - 128 >= t - w // 2:
                    Pm[g, tp, t] = 1.0
                if tp + 128 < t + w // 2:
                    Nm[g, tp, t] = 1.0
    return Pm, Nm


def icnt_table(nsub, first_edge, last_edge):
    S = nsub * 128
    out = np.zeros((128, nsub, 4), np.float32)
    for g, w in enumerate(POOL_W):
        t = np.arange(S)
        lo = t - w // 2
        hi = t + w // 2
        if first_edge:
            lo = np.maximum(lo, 0)
        if last_edge:
            hi = np.minimum(hi, S)
        cnt = (hi - lo).astype(np.float32)
        out[:, :, g] = (1.0 / cnt).reshape(nsub, 128).T
    return out


def pack_pmajor(a):
    n = a.shape[0]
    return np.ascontiguousarray(np.transpose(a, (1, 0, 2)).reshape(128, -1))


def host_prepare(cfg, inp):
    CH, SP, NCH, SS, NSEQ, NOWN = cfg.CH, cfg.SP, cfg.NCH, cfg.SS, cfg.NSEQ, cfg.NOWN
    f32 = lambda a: np.ascontiguousarray(np.asarray(a, dtype=np.float32))
    x_prompt = f32(inp["x_prompt"]); x_sample = f32(inp["x_sample"])
    p_prompt = f32(inp["p_prompt"]); p_sample = f32(inp["p_sample"])
    shared = {}
    shared["w_in_ab"] = f32(inp["w_in_ab"][0]); shared["w_out_ab"] = f32(inp["w_out_ab"][0])
    shared["w_gate"] = f32(inp["w_ple_gate"]).reshape(2 * D, D); shared["w_proj"] = f32(inp["w_ple_proj"]).reshape(2 * PLE, D)
    shared["w_in_c"] = f32(inp["w_in_c"][0]); shared["w_grp"] = f32(inp["w_grp_c"][0]).reshape(D, 256); shared["w_out_c"] = f32(inp["w_out_c"][0])
    shared["gmix"] = np.ascontiguousarray(f32(inp["norm_mix"]).reshape(2, 8, 128).transpose(2, 0, 1).reshape(128, 16))
    shared["gple"] = np.ascontiguousarray(f32(inp["norm_ple"]).reshape(2, 8, 128).transpose(2, 0, 1).reshape(128, 16))
    shared["hgain"] = np.concatenate([f32(inp[k][0]) for k in ("qn_a", "kn_a", "qn_b", "kn_b")])[None, :]
    shared["lamv"] = np.concatenate([f32(inp[k][0]) for k in ("lam_q1", "lam_k1", "lam_q2", "lam_k2")])[None, :]
    shared["subln"] = f32(inp["subln_b"][0])[None, :]; shared["cscale"] = f32(inp["scale_c"][0])[None, :]
    spos = np.arange(SS)
    shared["cs_s"] = rope_cs(spos)
    shared["kaug_s"] = kaug_rows(spos, np.ones(SS))
    shared["qaug_s"] = np.ascontiguousarray(np.stack([qaug_rows(spos, np.ones(SS)), qaug_rows(spos, -np.ones(SS))], 1))
    kk = np.arange(128)
    db = np.stack([-(SLOPES[h] * 8.0) * np.abs(kk[None, :] - kk[:, None]) for h in range(4)], 1)
    shared["dbias"] = np.ascontiguousarray(db.reshape(128, 512)).astype(bf16)
    shared["vone_s"] = np.ones((128, cfg.NKBS), np.float32)
    shared["ident"] = np.eye(128).astype(bf16)
    Pm, Nm = band_pn()
    shared["bandPN"] = pack_pmajor(np.concatenate([Pm, Nm], 0)).astype(bf16)
    bs = np.concatenate([band_tables(SS, True, False), band_tables(SS, False, False), band_tables(SS, False, True)], 0)
    shared["bandM_s"] = pack_pmajor(bs).astype(bf16)
    shared["icnt_s"] = np.ascontiguousarray(icnt_table(cfg.NKBS, True, True).reshape(128, -1))
    in_maps = []
    for c in range(NCORES):
        b, j = c // NCH, c % NCH
        c0 = j * CH
        m = dict(shared)
        kpos = -np.ones(cfg.NSLOT, np.int64)
        kpos[0:CH] = c0 + np.arange(CH)
        hb = NOWN * 128
        hpos = -np.ones(16, np.int64)
        if j > 0:
            hpos[0:8] = c0 - 8 + np.arange(8)
        if j < NCH - 1:
            hpos[8:16] = c0 + CH + np.arange(8)
        kpos[hb:hb + 16] = hpos
        used = np.zeros(SP, bool)
        used[kpos[kpos >= 0]] = True
        far = np.nonzero(~used)[0]
        fb = (NOWN + 1) * 128
        assert len(far) <= cfg.NFAR * 128
        kpos[fb:fb + len(far)] = far
        real = kpos >= 0
        kp0 = np.where(real, kpos, 0)
        xp = np.zeros((cfg.NSLOT, D), np.float32)
        xp[real] = x_prompt[b, kpos[real]]
        m["xp"] = xp
        pp0 = np.zeros((cfg.SQP, PLE), np.float32)
        pp0[0:CH] = p_prompt[0, b, c0:c0 + CH]
        hr = hpos >= 0
        pp0[CH:CH + 16][hr] = p_prompt[0, b, hpos[hr]]
        m["pp0"] = pp0
        m["pp1"] = np.ascontiguousarray(p_prompt[1, b, c0:c0 + CH])
        m["xs"] = np.ascontiguousarray(x_sample[c * NSEQ:(c + 1) * NSEQ])
        m["psm"] = np.ascontiguousarray(p_sample[:, c * NSEQ:(c + 1) * NSEQ])
        m["cs_p"] = rope_cs(kp0)
        sigma = np.where(kp0 < c0, -1.0, 1.0)
        sigma[0:CH] = 1.0
        sigma[~real] = 1.0
        kpa = np.where(real, kpos, 2 * SP)
        m["kaug_p"] = kaug_rows(kpa, sigma)
        hq = hpos.copy()
        hq[0:8] = np.where(hpos[0:8] >= 0, hpos[0:8], 0)
        hq[8:16] = np.where(hpos[8:16] >= 0, hpos[8:16], SP)
        qpos = np.concatenate([c0 + np.arange(CH), hq])
        s0 = np.ones(cfg.SQP); s1 = -np.ones(cfg.SQP)
        s1[CH:CH + 8] = 1.0
        s1[CH + 8:CH + 16] = -1.0
        m["qaug_p"] = np.ascontiguousarray(np.stack([qaug_rows(qpos, s0), qaug_rows(qpos, s1)], 1))
        bh = np.zeros((128, 2, 4, 16), np.float32)
        for s in range(16):
            if hpos[s] < 0:
                continue
            for q in range(16):
                if hpos[q] < 0:
                    continue
                dist = abs(int(hpos[q]) - int(hpos[s]))
                for h in range(4):
                    s8 = SLOPES[h] * 8.0
                    bh[s, 0, h, q] = -s8 * ((dist // 128) * 128)
                    bh[s, 1, h, q] = -s8 * (dist % 128)
        m["bhalo"] = np.ascontiguousarray(bh.reshape(128, 128)).astype(bf16)
        m["vone_p"] = np.ascontiguousarray(real.astype(np.float32).reshape(cfg.NKBP, 128).T)
        first_edge = (j == 0); last_edge = (j == NCH - 1)
        bp = np.concatenate([band_tables(CH, first_edge, False), band_tables(CH, False, False), band_tables(CH, False, last_edge)], 0)
        m["bandM_p"] = pack_pmajor(bp).astype(bf16)
        HPm = np.zeros((4, 16, 128), np.float32); HNm = np.zeros((4, 16, 128), np.float32)
        if j > 0:
            HPm[:, 0:8, :] = Pm[:, 120:128, :]
        if j < NCH - 1:
            HNm[:, 8:16, :] = Nm[:, 0:8, :]
        m["bandH"] = np.ascontiguousarray(np.transpose(np.concatenate([HPm, HNm], 0), (1, 0, 2)).reshape(16, -1)).astype(bf16)
        m["icnt_p"] = np.ascontiguousarray(icnt_table(NOWN, first_edge, last_edge).reshape(128, -1))
        in_maps.append(m)
    return in_maps


_CACHE = {}


def run(cfg, inp, runner=None):
    key = (cfg.SP, cfg.NCH, cfg.SS, cfg.NSEQ)
    if key not in _CACHE:
        _CACHE[key] = build_program(cfg)
    nc, nops = _CACHE[key]
    in_maps = host_prepare(cfg, inp)
    if runner is None:
        res = run_bass_kernel_spmd(nc, in_maps, core_ids=list(range(NCORES))).results
    else:
        res = runner(nc, in_maps)
    B = NCORES // cfg.NCH
    yp = np.zeros((B, cfg.SP, D), np.float32)
    ys = np.zeros((NCORES * cfg.NSEQ, cfg.SS, D), np.float32)
    for c in range(len(res)):
        b, j = c // cfg.NCH, c % cfg.NCH
        yp[b, j * cfg.CH:(j + 1) * cfg.CH] = np.asarray(res[c]["yp"], dtype=np.float32)
        ys[c * cfg.NSEQ:(c + 1) * cfg.NSEQ] = np.asarray(res[c]["ys"], dtype=np.float32)
    return yp, ys


def kernel(**inputs):
    cfg = Cfg()
    return run(cfg, inputs)
```
